# Optimizing a Trainium2 kernel written in Bass

```python
import jax, jax.numpy as jnp
from jax import lax
import numpy as np

D_MODEL = 2048
BATCH = 4
SEQ = 4096
DEPTH = 2

HEAD_DIM = 128
SB_HEADS = 6
SB_WIDTH = SB_HEADS * HEAD_DIM
SC_GROUPS = 4
SC_WIDTH = SC_GROUPS * HEAD_DIM
GDN_HEADS = 6
GDN_WIDTH = GDN_HEADS * HEAD_DIM
D_MIX = SB_WIDTH + SC_WIDTH + GDN_WIDTH
SC_CONV = 3
GDN_CONV = 4
GDN_CHUNK = 64
SB_BLOCK = 128
D_FF = 4 * D_MODEL
RMS_EPS = 1e-6
L2_EPS = 1e-6
IN_DIM = 3 * SB_WIDTH + 3 * SC_WIDTH + 3 * GDN_WIDTH + GDN_WIDTH + 2 * GDN_HEADS

kernel_name = "hybrid_sb_conv_gdn_block"


def rmsnorm(x, w):
    xf = x.astype(jnp.float32)
    r = lax.rsqrt(jnp.mean(xf * xf, axis=-1, keepdims=True) + RMS_EPS)
    return (xf * r * w.astype(jnp.float32)).astype(x.dtype)


def l2norm(x):
    xf = x.astype(jnp.float32)
    return xf * lax.rsqrt(jnp.sum(xf * xf, axis=-1, keepdims=True) + L2_EPS)


def causal_depthwise_conv(u, w):
    K = w.shape[0]
    T = u.shape[1]
    up = jnp.pad(u, ((0, 0), (K - 1, 0), (0, 0)))
    return sum(w[i] * up[:, i:i + T] for i in range(K))


def stick_breaking_attention(q, k, v):
    B_, T, H, Dh = q.shape
    nb = T // SB_BLOCK
    scale = Dh ** -0.5
    qb = q.reshape(B_, nb, SB_BLOCK, H, Dh).transpose(1, 0, 3, 2, 4)
    k_pos = jnp.arange(T)

    def one_block(args):
        q_blk, blk = args
        q_pos = blk * SB_BLOCK + jnp.arange(SB_BLOCK)
        z = jnp.einsum('bhqd,bkhd->bhqk', q_blk, k).astype(jnp.float32) * scale
        causal = k_pos[None, :] < q_pos[:, None]
        log_fail = jnp.where(causal, -jax.nn.softplus(z), 0.0)
        rem = lax.cumsum(log_fail, axis=3, reverse=True) - log_fail
        w = jnp.where(causal, jnp.exp(jax.nn.log_sigmoid(z) + rem), 0.0)
        return jnp.einsum('bhqk,bkhd->bqhd', w.astype(v.dtype), v)

    out = lax.map(one_block, (qb, jnp.arange(nb)))
    return out.transpose(1, 0, 2, 3, 4).reshape(B_, T, H * Dh)


def gated_delta_rule(q, k, v, g, beta):
    B_, T, H, Dk = q.shape
    Dv = v.shape[-1]
    C = GDN_CHUNK
    N = T // C
    f32 = jnp.float32

    def chunks(a):
        a = a.astype(f32).reshape((B_, N, C, H) + a.shape[3:])
        return jnp.moveaxis(a, 3, 1)

    q = chunks(q) * (Dk ** -0.5)
    k = chunks(k)
    v = chunks(v)
    g = jnp.cumsum(chunks(g), axis=-1)
    beta = chunks(beta)
    tri = jnp.tril(jnp.ones((C, C), bool))
    strict = jnp.tril(jnp.ones((C, C), bool), -1)
    diff = g[..., :, None] - g[..., None, :]
    decay = jnp.where(tri, jnp.exp(jnp.where(tri, diff, 0.0)), 0.0)
    k_beta = k * beta[..., None]
    v_beta = v * beta[..., None]
    m = jnp.where(strict, jnp.einsum('bhnid,bhnjd->bhnij', k_beta, k) * decay, 0.0)
    eye = jnp.eye(C, dtype=f32)
    t_mat = lax.linalg.triangular_solve(eye + m, jnp.broadcast_to(eye, m.shape),
                                        left_side=True, lower=True, unit_diagonal=True)
    u = t_mat @ v_beta
    w = t_mat @ (k_beta * jnp.exp(g)[..., None])
    attn_intra = jnp.where(tri, jnp.einsum('bhnid,bhnjd->bhnij', q, k) * decay, 0.0)
    g_last = g[..., -1]
    q_dec = q * jnp.exp(g)[..., None]
    k_dec = k * jnp.exp(g_last[..., None] - g)[..., None]

    def step(state, xs):
        q_c, k_c, u_c, w_c, a_c, gl_c = xs
        v_new = u_c - w_c @ state
        o = q_c @ state + a_c @ v_new
        state = state * jnp.exp(gl_c)[..., None, None] + jnp.swapaxes(k_c, -1, -2) @ v_new
        return state, o

    xs = tuple(jnp.moveaxis(a, 2, 0) for a in (q_dec, k_dec, u, w, attn_intra, g_last))
    state0 = jnp.zeros((B_, H, Dk, Dv), f32)
    _, o = lax.scan(step, state0, xs)
    return o.transpose(1, 0, 3, 2, 4).reshape(B_, T, H, Dv)


def gdn_mixer(qkv, z, a, b, conv_w, a_log, dt_bias, norm_w):
    B_, T, _ = qkv.shape
    qkv = jax.nn.silu(causal_depthwise_conv(qkv, conv_w))
    q, k, v = jnp.split(qkv, 3, axis=-1)
    q = l2norm(q.reshape(B_, T, GDN_HEADS, HEAD_DIM))
    k = l2norm(k.reshape(B_, T, GDN_HEADS, HEAD_DIM))
    v = v.reshape(B_, T, GDN_HEADS, HEAD_DIM)
    g = -jnp.exp(a_log.astype(jnp.float32)) * jax.nn.softplus(a.astype(jnp.float32) + dt_bias.astype(jnp.float32))
    beta = jax.nn.sigmoid(b.astype(jnp.float32))
    o = gated_delta_rule(q, k, v, g, beta)
    o = rmsnorm(o, norm_w) * jax.nn.silu(z.astype(jnp.float32).reshape(B_, T, GDN_HEADS, HEAD_DIM))
    return o.reshape(B_, T, GDN_WIDTH).astype(qkv.dtype)


def hybrid_layer(x, norm1_w, w_in, sc_conv_w, gdn_conv_w, gdn_a_log, gdn_dt_bias, gdn_norm_w,
                 w_out, norm2_w, w_up, w_down):
    B_, T, _ = x.shape
    h = rmsnorm(x, norm1_w)
    proj = h @ w_in
    sizes = [SB_WIDTH, SB_WIDTH, SB_WIDTH, SC_WIDTH, SC_WIDTH, SC_WIDTH,
             3 * GDN_WIDTH, GDN_WIDTH, GDN_HEADS, GDN_HEADS]
    cuts = [int(c) for c in np.cumsum(sizes)[:-1]]
    sb_q, sb_k, sb_v, sc_b, sc_c, sc_h, gdn_qkv, gdn_z, gdn_a, gdn_b = jnp.split(proj, cuts, axis=-1)
    heads = lambda t: t.reshape(B_, T, SB_HEADS, HEAD_DIM)
    y_sb = stick_breaking_attention(heads(sb_q), heads(sb_k), heads(sb_v))
    y_sc = sc_b * causal_depthwise_conv(sc_c * sc_h, sc_conv_w)
    y_gdn = gdn_mixer(gdn_qkv, gdn_z, gdn_a, gdn_b, gdn_conv_w, gdn_a_log, gdn_dt_bias, gdn_norm_w)
    y = jnp.concatenate([y_sb.astype(x.dtype), y_sc, y_gdn], axis=-1)
    x = x + y @ w_out
    h2 = rmsnorm(x, norm2_w)
    return x + jnp.square(jax.nn.relu(h2 @ w_up)) @ w_down


def setup_inputs(seed: int = 0) -> dict:
    key = jax.random.key(seed)
    ks = jax.random.split(key, 14)
    f32 = jnp.float32
    x = jax.random.normal(ks[0], (BATCH, SEQ, D_MODEL), f32)
    norm1_w = 1.0 + 0.02 * jax.random.normal(ks[1], (DEPTH, D_MODEL), f32)
    w_in = jax.random.normal(ks[2], (DEPTH, D_MODEL, IN_DIM), f32) * D_MODEL ** -0.5
    sc_conv_w = jax.random.normal(ks[3], (DEPTH, SC_CONV, SC_WIDTH), f32) * SC_CONV ** -0.5
    gdn_conv_w = jax.random.normal(ks[4], (DEPTH, GDN_CONV, 3 * GDN_WIDTH), f32) * GDN_CONV ** -0.5
    gdn_a_log = jnp.log(jax.random.uniform(ks[5], (DEPTH, GDN_HEADS), f32, 1.0, 16.0))
    dt = jnp.exp(jax.random.uniform(ks[6], (DEPTH, GDN_HEADS), f32, np.log(1e-3), np.log(1e-1)))
    gdn_dt_bias = dt + jnp.log(-jnp.expm1(-dt))
    gdn_norm_w = 1.0 + 0.02 * jax.random.normal(ks[7], (DEPTH, HEAD_DIM), f32)
    w_out = jax.random.normal(ks[8], (DEPTH, D_MIX, D_MODEL), f32) * D_MIX ** -0.5
    norm2_w = 1.0 + 0.02 * jax.random.normal(ks[9], (DEPTH, D_MODEL), f32)
    w_up = jax.random.normal(ks[10], (DEPTH, D_MODEL, D_FF), f32) * D_MODEL ** -0.5
    w_down = jax.random.normal(ks[11], (DEPTH, D_FF, D_MODEL), f32) * D_FF ** -0.5
    final_norm_w = 1.0 + 0.02 * jax.random.normal(ks[12], (D_MODEL,), f32)
    return {"x": x, "norm1_w": norm1_w, "w_in": w_in, "sc_conv_w": sc_conv_w,
            "gdn_conv_w": gdn_conv_w, "gdn_a_log": gdn_a_log, "gdn_dt_bias": gdn_dt_bias,
            "gdn_norm_w": gdn_norm_w, "w_out": w_out, "norm2_w": norm2_w, "w_up": w_up,
            "w_down": w_down, "final_norm_w": final_norm_w}


def reference(x, norm1_w, w_in, sc_conv_w, gdn_conv_w, gdn_a_log, gdn_dt_bias, gdn_norm_w,
              w_out, norm2_w, w_up, w_down, final_norm_w):
    for l in range(DEPTH):
        x = hybrid_layer(x, norm1_w[l], w_in[l], sc_conv_w[l], gdn_conv_w[l], gdn_a_log[l],
                         gdn_dt_bias[l], gdn_norm_w[l], w_out[l], norm2_w[l], w_up[l], w_down[l])
    return rmsnorm(x, final_norm_w)
```

```python
import contextlib
import numpy as np
import concourse.bass as bass
import concourse.mybir as mybir
from concourse.bass_utils import run_bass_kernel_spmd

F32 = mybir.dt.float32
BF16 = mybir.dt.bfloat16
AF = mybir.ActivationFunctionType
ALU = mybir.AluOpType

D = 2048
KT = 16
DFF = 8192
IN_DIM = 6924
RMS_EPS = 1e-6
L2_EPS = 1e-6
import os
GV = os.environ.get("GV", "v3")
GSTOP = int(os.environ.get("GSTOP", "9"))


class Buf:
    __slots__ = ("name", "w", "r", "psum")

    def __init__(self, name="", psum=False):
        self.name = name
        self.w = None
        self.r = []
        self.psum = psum


class Phase(contextlib.ExitStack):
    def __init__(self, P):
        super().__init__()
        self.P = P

    def __exit__(self, *a):
        if a[0] is None:
            self.P.barrier()
        return super().__exit__(*a)


class Prog:
    ENG = ("pe", "act", "dve", "pool", "sp")

    def __init__(self, nc, stack):
        self.nc = nc
        self.lists = {e: [] for e in self.ENG}
        self.sem = {e: stack.enter_context(nc.semaphore("s_" + e)) for e in self.ENG if e != "sp"}
        self.cnt = {e: 0 for e in self.ENG}
        self.known = {e: {} for e in self.ENG}
        self.dsem = {}
        self.dcnt = {}
        self.drr = {}
        for q, n in (("sp", 16), ("pool", 8), ("act", 4)):
            self.dsem[q] = [stack.enter_context(nc.semaphore(f"d_{q}{i}")) for i in range(n)]
            self.dcnt[q] = [0] * n
            self.drr[q] = 0
        self.out_events = []

    def _wait(self, eng, ev):
        sem, val = ev[0], ev[1]
        k = self.known[eng]
        if k.get(id(sem), 0) >= val:
            return
        k[id(sem)] = val
        self.lists[eng].append(("wait", sem, val))

    def _deps(self, eng, reads, writes):
        for b in reads:
            if b.w is not None and not (eng == "pe" and b.w[2] == "pe"):
                self._wait(eng, b.w)
            if b.psum:
                for ev in b.r:
                    if ev[2] != eng:
                        self._wait(eng, ev)
        for b in writes:
            if b.w is not None and not (eng == "pe" and b.w[2] == "pe"):
                self._wait(eng, b.w)
            for ev in b.r:
                if not (eng == "pe" and ev[2] == "pe"):
                    self._wait(eng, ev)

    def _mark(self, ev, reads, writes):
        for b in reads:
            b.r = [e for e in b.r if e[0] is not ev[0]] + [ev]
        for b in writes:
            b.w = ev
            b.r = []

    def op(self, eng, fn, reads=(), writes=(), inc=True):
        self._deps(eng, reads, writes)
        if inc:
            self.cnt[eng] += 1
            ev = (self.sem[eng], self.cnt[eng], eng)
            self.lists[eng].append(("op", fn, self.sem[eng]))
        else:
            ev = (self.sem[eng], self.cnt[eng] + 1, eng)
            self.lists[eng].append(("op", fn, None))
        self._mark(ev, reads, writes)
        return ev

    def dma(self, q, fn, reads=(), writes=(), is_output=False):
        i = self.drr[q]
        self.drr[q] = (i + 1) % len(self.dsem[q])
        sem = self.dsem[q][i]
        if self.dcnt[q][i] > 0:
            self._wait(q, (sem, self.dcnt[q][i]))
        self._deps(q, reads, writes)
        self.dcnt[q][i] += 16
        ev = (sem, self.dcnt[q][i], "dma")
        self.lists[q].append(("dma", fn, sem))
        self._mark(ev, reads, writes)
        if is_output:
            self.out_events.append(ev)
        return ev

    def barrier(self):
        evs = [(self.sem[e], self.cnt[e], e) for e in self.sem if self.cnt[e] > 0]
        for q in self.dsem:
            for sem, c in zip(self.dsem[q], self.dcnt[q]):
                if c > 0:
                    evs.append((sem, c, "dma"))
        for eng in self.ENG:
            for ev in evs:
                self._wait(eng, ev)

    def finish(self):
        for ev in self.out_events:
            self._wait("sp", ev)

    def emit(self, block):
        def run(e, lst):
            for it in lst:
                if it[0] == "wait":
                    e.wait_ge(it[1], it[2])
                elif it[0] == "op":
                    ins = it[1](e)
                    if it[2] is not None:
                        ins.then_inc(it[2], 1)
                else:
                    it[1](e).then_inc(it[2], 16)
        L = self.lists

        @block.sync
        def _(e):
            run(e, L["sp"])

        @block.tensor
        def _(e):
            run(e, L["pe"])

        @block.scalar
        def _(e):
            run(e, L["act"])

        @block.vector
        def _(e):
            run(e, L["dve"])

        @block.gpsimd
        def _(e):
            run(e, L["pool"])


def build(T=4096, depth=2, flags=("sb", "sc", "gdn")):
    nc = bass.Bass("TRN2", target_bir_lowering=False)
    depth_ = depth
    depth = max(depth, 1)
    NTT = T // 512
    NTB = T // 128
    G = min(T, 2048)
    NG = T // G
    GT = G // 512

    def din(name, shape):
        return nc.dram_tensor(name, shape, F32, kind="ExternalInput").ap()

    x_in = din("x", [T, D])
    norm1_w = din("norm1_w", [depth, D])
    w_in = din("w_in", [depth, D, IN_DIM])
    sc_conv_w = din("sc_conv_w", [depth, 3, 512])
    gdn_conv_w = din("gdn_conv_w", [depth, 4, 2304])
    gdn_a_log = din("gdn_a_log", [depth, 6])
    gdn_dt_bias = din("gdn_dt_bias", [depth, 6])
    gdn_norm_w = din("gdn_norm_w", [depth, 128])
    w_out = din("w_out", [depth, D, D])
    norm2_w = din("norm2_w", [depth, D])
    w_up = din("w_up", [depth, D, DFF])
    w_down = din("w_down", [depth, DFF, D])
    final_norm_w = din("final_norm_w", [D])
    out = nc.dram_tensor("out", [T, D], F32, kind="ExternalOutput").ap()

    def scratch(name, shape, dt):
        return nc.dram_tensor(name, shape, dt, kind="Internal").ap()

    xT = scratch("xT", [D, T], F32)
    qkT = scratch("qkT", [1536, T], BF16)
    vtok = scratch("vtok", [T, 768], BF16)
    pfT = scratch("pfT", [4608, T], F32)
    abtok = scratch("abtok", [T, 12], F32)
    yT = scratch("yT", [D, T], BF16)
    winF = scratch("winF", [depth, 12, 128, KT, 512], BF16)
    winV = scratch("winV", [depth, 128, KT, 768], BF16)
    winAB = scratch("winAB", [depth, 128, KT, 12], BF16)
    woutF = scratch("woutF", [depth, 4, 128, KT, 512], BF16)
    wupF = scratch("wupF", [depth, 16, 128, KT, 512], BF16)
    wdnF = scratch("wdnF", [depth, 4, 4, 128, KT, 512], BF16)

    with contextlib.ExitStack() as gst:
        P = Prog(nc, gst)

        uniq = [0]

        def sb(name, shape, dt, st=gst):
            uniq[0] += 1
            return st.enter_context(nc.sbuf_tensor(f"{name}_{uniq[0]}", shape, dt))

        ps = [gst.enter_context(nc.psum_tensor(f"ps{i}", [128, 512], F32)) for i in range(8)]
        psb = [Buf(f"ps{i}", psum=True) for i in range(8)]

        ident = sb("ident", [128, 128], F32)
        ones_bf = sb("ones_bf", [128, 128], BF16)
        Bc = Buf("consts")

        P.op("pool", lambda e: e.memset(ident[:], 0.0), writes=[Bc])
        P.op("pool", lambda e: e.memset(ones_bf[:], 1.0), writes=[Bc])
        P.op("pool", lambda e: e.affine_select(out=ident[:], in_=ident[:], pattern=[[-1, 128]], compare_op=ALU.not_equal,
                                                fill=1.0, base=0, channel_multiplier=1), reads=[Bc], writes=[Bc])

        nw_all = sb("nw_all", [128, 2 * depth + 1, KT], F32)
        Bnw = Buf("nw")
        for i in range(2 * depth + 1):
            if i == 2 * depth:
                src = final_norm_w
            elif i % 2 == 0:
                src = norm1_w[i // 2]
            else:
                src = norm2_w[i // 2]
            P.dma("sp", lambda e, i=i, src=src: e.dma_start(out=nw_all[:, i, :], in_=src.rearrange("(kt p) -> p kt", p=128),
                                                             allow_slow_non_contiguous=True), writes=[Bnw])

        WB = {}

        def conv_dma(key, out_ap, in_ap):
            b = WB.setdefault(key, Buf(str(key)))
            P.dma("pool", lambda e: e.dma_start(out=out_ap, in_=in_ap), writes=[b])

        FCOLS = [0, 512, 1024] + [2304 + 512 * i for i in range(9)]
        for l in range(depth_):
            wl = w_in[l].rearrange("(kt p) c -> p kt c", p=128)
            for j, c0 in enumerate(FCOLS):
                for hh in range(2):
                    conv_dma(("win", l, j), winF[l, j, :, hh * 8:(hh + 1) * 8, :], wl[:, hh * 8:(hh + 1) * 8, c0:c0 + 512])
            for hh in range(2):
                conv_dma(("winV", l), winV[l, :, hh * 8:(hh + 1) * 8, :], wl[:, hh * 8:(hh + 1) * 8, 1536:2304])
            conv_dma(("winAB", l), winAB[l], wl[:, :, 6912:6924])
            wl = w_out[l].rearrange("(kt p) c -> p kt c", p=128)
            for j in range(4):
                for hh in range(2):
                    conv_dma(("wout", l, j), woutF[l, j, :, hh * 8:(hh + 1) * 8, :], wl[:, hh * 8:(hh + 1) * 8, j * 512:(j + 1) * 512])
            wl = w_up[l].rearrange("(kt p) c -> p kt c", p=128)
            for j in range(16):
                for hh in range(2):
                    conv_dma(("wup", l, j), wupF[l, j, :, hh * 8:(hh + 1) * 8, :], wl[:, hh * 8:(hh + 1) * 8, j * 512:(j + 1) * 512])
            wl = w_down[l].rearrange("(ft p) c -> p ft c", p=128)
            for j in range(4):
                for fq in range(4):
                    conv_dma(("wdn", l, j, fq), wdnF[l, j, fq], wl[:, fq * 16:(fq + 1) * 16, j * 512:(j + 1) * 512])

        BxT = [Buf(f"xT{tt}") for tt in range(NTT)]
        Bqk = Buf("qkT")
        Bv = Buf("vtok")
        Bpf = Buf("pfT")
        Bab = Buf("abtok")
        ByT = Buf("yT")

        xT_v = xT.rearrange("(kt p) t -> p kt t", p=128)
        yT_v = yT.rearrange("(kt p) t -> p kt t", p=128)

        class Rot:
            def __init__(self, tiles):
                self.t = tiles
                self.b = [Buf() for _ in tiles]
                self.i = 0

            def next(self):
                i = self.i
                self.i = (i + 1) % len(self.t)
                return self.t[i], self.b[i]

        class Pref:
            def __init__(self, rot, loaders, depth=2):
                self.rot, self.loaders, self.q, self.i = rot, loaders, [], 0
                assert len(rot.t) > depth
                for _ in range(depth):
                    self._issue()

            def _issue(self):
                if self.i < len(self.loaders):
                    wt, Bwt = self.rot.next()
                    self.loaders[self.i](wt, Bwt)
                    self.q.append((wt, Bwt))
                    self.i += 1

            def get(self):
                cur = self.q.pop(0)
                self._issue()
                return cur

        psrot = Rot(ps[0:4])
        psrot.b = psb[0:4]
        evac_flip = [0]

        def evac(out_ap, in_ap, reads, writes):
            evac_flip[0] ^= 1
            if evac_flip[0]:
                P.op("act", lambda e: e.activation(out_ap, in_ap, AF.Copy), reads=reads, writes=writes)
            else:
                P.op("dve", lambda e: e.tensor_copy(out_ap, in_ap), reads=reads, writes=writes)

        def mm(out_ap, lhsT, rhs, start, stop, reads, writes, inc):
            P.op("pe", lambda e: e.matmul(out_ap, lhsT, rhs, start=start, stop=stop), reads=reads, writes=writes, inc=inc)

        def rmsnorm_tile(st_, xs, Bxs, sq_rot, rr, Brr, widx, tt, hT, BhT, hcol0, ps_ss, Bps):
            P.dma("sp", lambda e: e.dma_start(out=xs[:], in_=xT_v[:, :, tt * 512:(tt + 1) * 512]), reads=[BxT[tt]], writes=[Bxs])
            for kt in range(KT):
                sq, Bsq = sq_rot.next()
                P.op("act", lambda e, sq=sq, kt=kt: e.activation(sq[:], xs[:, kt, :], AF.Square), reads=[Bxs], writes=[Bsq])
                mm(ps_ss[:], ones_bf[:], sq[:], kt == 0, kt == KT - 1, [Bsq, Bc], [Bps], True)
            P.op("act", lambda e: e.activation(rr[:], ps_ss[:], AF.Sqrt, bias=RMS_EPS, scale=1.0 / D), reads=[Bps], writes=[Brr])
            P.op("dve", lambda e: e.reciprocal(rr[:], rr[:]), reads=[Brr], writes=[Brr])
            for kt in range(KT):
                P.op("dve", lambda e, kt=kt: e.scalar_tensor_tensor(hT[:, kt, hcol0:hcol0 + 512], xs[:, kt, :], nw_all[:, widx, kt:kt + 1],
                                                                     rr[:], ALU.mult, ALU.mult),
                     reads=[Bxs, Brr, Bnw], writes=[BhT[kt]])

        with Phase(P) as st:
            xin = Rot([sb(f"xin{i}", [128, D], F32, st) for i in range(2)])
            xst = Rot([sb(f"xst{i}", [128, KT, 128], F32, st) for i in range(2)])
            for tb in range(NTB):
                xi, Bxi = xin.next()
                P.dma("sp", lambda e, xi=xi, tb=tb: e.dma_start(out=xi[:], in_=x_in[tb * 128:(tb + 1) * 128, :]), writes=[Bxi])
                xs_, Bxs_ = xst.next()
                for g in range(4):
                    pt, Bpt = psrot.next()
                    for j in range(4):
                        P.op("pe", lambda e, pt=pt, xi=xi, g=g, j=j: e.transpose(pt[:, j * 128:(j + 1) * 128], xi[:, (4 * g + j) * 128:(4 * g + j + 1) * 128], ident[:]),
                             reads=[Bxi, Bc], writes=[Bpt], inc=(j == 3))
                    evac(xs_[:, 4 * g:4 * g + 4, :], pt[:].rearrange("p (j t) -> p j t", j=4), [Bpt], [Bxs_])
                P.dma("sp", lambda e, xs_=xs_, tb=tb: e.dma_start(out=xT_v[:, :, tb * 128:(tb + 1) * 128], in_=xs_[:]),
                      reads=[Bxs_], writes=[BxT[tb // 4]])

        def _layer(l):
            with Phase(P) as st:
                hT = sb("hT_a", [128, KT, G], BF16, st)
                BhT = [Buf(f"hT{k}") for k in range(KT)]
                xs = sb("xs_a", [128, KT, 512], F32, st)
                Bxs = Buf("xs")
                sq_rot = Rot([sb(f"sq_a{i}", [128, 512], BF16, st) for i in range(3)])
                rr = sb("rr_a", [128, 512], F32, st)
                Brr = Buf("rr")
                wts = Rot([sb(f"wt_a{i}", [128, KT, 512], BF16, st) for i in range(3)])
                wab = sb("wab_a", [128, KT, 12], BF16, st)
                Bwab = Buf("wab")
                stg32 = Rot([sb(f"stg32_a{i}", [128, 512], F32, st) for i in range(3)])
                stg16 = Rot([sb(f"stg16_a{i}", [128, 512], BF16, st) for i in range(3)])
                ldA = []
                for g in range(NG):
                    for j in range(12):
                        ldA.append(lambda wt, Bwt, l=l, j=j: P.dma("sp", lambda e: e.dma_start(out=wt[:], in_=winF[l, j]), reads=[WB[("win", l, j)]], writes=[Bwt]))
                    for (c0, n) in ((0, 512), (512, 256)):
                        ldA.append(lambda wt, Bwt, l=l, c0=c0, n=n: P.dma("sp", lambda e: e.dma_start(out=wt[:, :, 0:n], in_=winV[l, :, :, c0:c0 + n]),
                                                                         reads=[WB[("winV", l)]], writes=[Bwt]))
                prefA = Pref(wts, ldA)
                for g in range(NG):
                    for t4 in range(GT):
                        rmsnorm_tile(st, xs, Bxs, sq_rot, rr, Brr, 2 * l, g * GT + t4, hT, BhT, t4 * 512, ps[7], psb[7])
                    for j in range(12):
                        wt, Bwt = prefA.get()
                        for ci in range(4):
                            for t4 in range(GT):
                                pt, Bpt = psrot.next()
                                for kt in range(KT):
                                    mm(pt[:], wt[:, kt, ci * 128:(ci + 1) * 128], hT[:, kt, t4 * 512:(t4 + 1) * 512], kt == 0, kt == KT - 1,
                                       [Bwt, BhT[kt]], [Bpt], kt == KT - 1)
                                tok0 = g * G + t4 * 512
                                if j < 3:
                                    sg, Bsg = stg16.next()
                                    evac(sg[:], pt[:], [Bpt], [Bsg])
                                    r0 = j * 512 + ci * 128
                                    P.dma("sp", lambda e, sg=sg, r0=r0, tok0=tok0: e.dma_start(out=qkT[r0:r0 + 128, tok0:tok0 + 512], in_=sg[:]),
                                          reads=[Bsg], writes=[Bqk])
                                else:
                                    sg, Bsg = stg32.next()
                                    evac(sg[:], pt[:], [Bpt], [Bsg])
                                    r0 = (j - 3) * 512 + ci * 128
                                    P.dma("sp", lambda e, sg=sg, r0=r0, tok0=tok0: e.dma_start(out=pfT[r0:r0 + 128, tok0:tok0 + 512], in_=sg[:]),
                                          reads=[Bsg], writes=[Bpf])
                    for (c0, n) in ((0, 512), (512, 256)):
                        wt, Bwt = prefA.get()
                        for tb in range(G // 128):
                            pt, Bpt = psrot.next()
                            for kt in range(KT):
                                mm(pt[:, 0:n], hT[:, kt, tb * 128:(tb + 1) * 128], wt[:, kt, 0:n], kt == 0, kt == KT - 1, [Bwt, BhT[kt]], [Bpt], kt == KT - 1)
                            sg, Bsg = stg16.next()
                            evac(sg[:, 0:n], pt[:, 0:n], [Bpt], [Bsg])
                            tok0 = g * G + tb * 128
                            P.dma("sp", lambda e, sg=sg, c0=c0, n=n, tok0=tok0: e.dma_start(out=vtok[tok0:tok0 + 128, c0:c0 + n], in_=sg[:, 0:n]),
                                  reads=[Bsg], writes=[Bv])
                    P.dma("sp", lambda e, l=l: e.dma_start(out=wab[:], in_=winAB[l]), reads=[WB[("winAB", l)]], writes=[Bwab])
                    for tb in range(G // 128):
                        pt, Bpt = psrot.next()
                        for kt in range(KT):
                            mm(pt[:, 0:12], hT[:, kt, tb * 128:(tb + 1) * 128], wab[:, kt, :], kt == 0, kt == KT - 1, [Bwab, BhT[kt]], [Bpt], kt == KT - 1)
                        sg, Bsg = stg32.next()
                        evac(sg[:, 0:12], pt[:, 0:12], [Bpt], [Bsg])
                        tok0 = g * G + tb * 128
                        P.dma("sp", lambda e, sg=sg, tok0=tok0: e.dma_start(out=abtok[tok0:tok0 + 128, :], in_=sg[:, 0:12]),
                              reads=[Bsg], writes=[Bab])

            if "sb" in flags:
                phase_sb(nc, P, l, T, qkT, vtok, yT, Bqk, Bv, ByT, ps, psb, sb)
            else:
                zero_rows(nc, P, yT, ByT, 0, 768, T, sb)
            if "sc" in flags:
                phase_sc(nc, P, l, T, pfT, yT, sc_conv_w, Bpf, ByT, sb)
            else:
                zero_rows(nc, P, yT, ByT, 768, 512, T, sb)
            if "gdn" in flags:
                phase_gdn(nc, P, l, T, pfT, abtok, yT, gdn_conv_w, gdn_a_log, gdn_dt_bias, gdn_norm_w, Bpf, Bab, ByT, ps, psb, sb, ident, ones_bf, Bc)
            else:
                zero_rows(nc, P, yT, ByT, 1280, 768, T, sb)

            with Phase(P) as st:
                yS = sb("yS_e", [128, KT, G], BF16, st)
                ByS = Buf("yS")
                wts = Rot([sb(f"wt_e{i}", [128, KT, 512], BF16, st) for i in range(3)])
                xold = Rot([sb(f"xold_e{i}", [128, 512], F32, st) for i in range(3)])
                stg = Rot([sb(f"stg_e{i}", [128, 512], F32, st) for i in range(3)])
                ldE = []
                for g in range(NG):
                    for j in range(4):
                        ldE.append(lambda wt, Bwt, l=l, j=j: P.dma("sp", lambda e: e.dma_start(out=wt[:], in_=woutF[l, j]), reads=[WB[("wout", l, j)]], writes=[Bwt]))
                prefE = Pref(wts, ldE)
                for g in range(NG):
                    P.dma("sp", lambda e, g=g: e.dma_start(out=yS[:], in_=yT_v[:, :, g * G:(g + 1) * G]), reads=[ByT], writes=[ByS])
                    for j in range(4):
                        wt, Bwt = prefE.get()
                        for ci in range(4):
                            dm = 4 * j + ci
                            for t4 in range(GT):
                                tt = g * GT + t4
                                xo, Bxo = xold.next()
                                P.dma("sp", lambda e, xo=xo, dm=dm, tt=tt: e.dma_start(out=xo[:], in_=xT[dm * 128:(dm + 1) * 128, tt * 512:(tt + 1) * 512]),
                                      reads=[BxT[tt]], writes=[Bxo])
                                pt, Bpt = psrot.next()
                                for kt in range(KT):
                                    mm(pt[:], wt[:, kt, ci * 128:(ci + 1) * 128], yS[:, kt, t4 * 512:(t4 + 1) * 512], kt == 0, kt == KT - 1,
                                       [Bwt, ByS], [Bpt], kt == KT - 1)
                                sg, Bsg = stg.next()
                                P.op("dve", lambda e, sg=sg, pt=pt, xo=xo: e.tensor_tensor(sg[:], pt[:], xo[:], ALU.add), reads=[Bpt, Bxo], writes=[Bsg])
                                P.dma("sp", lambda e, sg=sg, dm=dm, tt=tt: e.dma_start(out=xT[dm * 128:(dm + 1) * 128, tt * 512:(tt + 1) * 512], in_=sg[:]),
                                      reads=[Bsg], writes=[BxT[tt]])

            with Phase(P) as st:
                hT = sb("hT_f", [128, KT, 512], BF16, st)
                BhT = [Buf(f"hT{k}") for k in range(KT)]
                aT = sb("aT_f", [128, 64, 512], BF16, st)
                BaT = Buf("aT")
                xs = sb("xs_f", [128, KT, 512], F32, st)
                Bxs = Buf("xs")
                sq_rot = Rot([sb(f"sq_f{i}", [128, 512], BF16, st) for i in range(3)])
                rr = sb("rr_f", [128, 512], F32, st)
                Brr = Buf("rr")
                wts = Rot([sb(f"wt_f{i}", [128, KT, 512], BF16, st) for i in range(3)])
                rl = Rot([sb(f"rl_f{i}", [128, 512], F32, st) for i in range(3)])
                xold = Rot([sb(f"xold_f{i}", [128, 512], F32, st) for i in range(4)])
                stg = Rot([sb(f"stg_f{i}", [128, 512], F32, st) for i in range(4)])
                ldF = []
                for tt in range(NTT):
                    for j in range(16):
                        ldF.append(lambda wt, Bwt, l=l, j=j: P.dma("sp", lambda e: e.dma_start(out=wt[:], in_=wupF[l, j]), reads=[WB[("wup", l, j)]], writes=[Bwt]))
                    for j in range(4):
                        for fq in range(4):
                            ldF.append(lambda wt, Bwt, l=l, j=j, fq=fq: P.dma("sp", lambda e: e.dma_start(out=wt[:], in_=wdnF[l, j, fq]), reads=[WB[("wdn", l, j, fq)]], writes=[Bwt]))
                prefF = Pref(wts, ldF)
                rmsnorm_tile(st, xs, Bxs, sq_rot, rr, Brr, 2 * l + 1, 0, hT, BhT, 0, ps[7], psb[7])
                for tt in range(NTT):
                    for j in range(16):
                        wt, Bwt = prefF.get()
                        for ci in range(4):
                            pt, Bpt = psrot.next()
                            for kt in range(KT):
                                mm(pt[:], wt[:, kt, ci * 128:(ci + 1) * 128], hT[:, kt, :], kt == 0, kt == KT - 1, [Bwt, BhT[kt]], [Bpt], kt == KT - 1)
                            r_, Br_ = rl.next()
                            P.op("act", lambda e, r_=r_, pt=pt: e.activation(r_[:], pt[:], AF.Relu), reads=[Bpt], writes=[Br_])
                            P.op("dve", lambda e, r_=r_, f=4 * j + ci: e.tensor_tensor(aT[:, f, :], r_[:], r_[:], ALU.mult), reads=[Br_], writes=[BaT])
                    if tt + 1 < NTT:
                        rmsnorm_tile(st, xs, Bxs, sq_rot, rr, Brr, 2 * l + 1, tt + 1, hT, BhT, 0, ps[7], psb[7])
                    for j in range(4):
                        for fq in range(4):
                            wt, Bwt = prefF.get()
                            for ft in range(16):
                                f = fq * 16 + ft
                                for ci in range(4):
                                    mm(ps[ci][:], wt[:, ft, ci * 128:(ci + 1) * 128], aT[:, f, :], f == 0, f == 63, [Bwt, BaT], [psb[ci]], f == 63 or ft == 15)
                        for ci in range(4):
                            dm = 4 * j + ci
                            xo, Bxo = xold.next()
                            P.dma("sp", lambda e, xo=xo, dm=dm, tt=tt: e.dma_start(out=xo[:], in_=xT[dm * 128:(dm + 1) * 128, tt * 512:(tt + 1) * 512]),
                                  reads=[BxT[tt]], writes=[Bxo])
                            sg, Bsg = stg.next()
                            P.op("dve", lambda e, sg=sg, ci=ci, xo=xo: e.tensor_tensor(sg[:], ps[ci][:], xo[:], ALU.add), reads=[psb[ci], Bxo], writes=[Bsg])
                            P.dma("sp", lambda e, sg=sg, dm=dm, tt=tt: e.dma_start(out=xT[dm * 128:(dm + 1) * 128, tt * 512:(tt + 1) * 512], in_=sg[:]),
                                  reads=[Bsg], writes=[BxT[tt]])

        for l in range(depth_):
            _layer(l)

        with Phase(P) as st:
            hF = sb("hF_o", [128, KT, 512], F32, st)
            BhF = [Buf(f"hF{k}") for k in range(KT)]
            xs = sb("xs_o", [128, KT, 512], F32, st)
            Bxs = Buf("xs")
            sq_rot = Rot([sb(f"sq_o{i}", [128, 512], BF16, st) for i in range(3)])
            rr = sb("rr_o", [128, 512], F32, st)
            Brr = Buf("rr")
            ost = Rot([sb(f"ost{i}", [128, D], F32, st) for i in range(2)])
            for tt in range(NTT):
                rmsnorm_tile(st, xs, Bxs, sq_rot, rr, Brr, 2 * depth, tt, hF, BhF, 0, ps[7], psb[7])
                for s4 in range(4):
                    os_, Bos = ost.next()
                    for g in range(4):
                        pt, Bpt = psrot.next()
                        for j in range(4):
                            P.op("pe", lambda e, pt=pt, g=g, j=j, s4=s4: e.transpose(pt[:, j * 128:(j + 1) * 128], hF[:, 4 * g + j, s4 * 128:(s4 + 1) * 128], ident[:]),
                                 reads=[BhF[4 * g + j], Bc], writes=[Bpt], inc=(j == 3))
                        evac(os_[:, g * 512:(g + 1) * 512], pt[:], [Bpt], [Bos])
                    tok0 = tt * 512 + s4 * 128
                    P.dma("sp", lambda e, os_=os_, tok0=tok0: e.dma_start(out=out[tok0:tok0 + 128, :], in_=os_[:]), reads=[Bos], is_output=True)

        P.finish()
        with nc.Block() as block:
            P.emit(block)
    return nc


def zero_rows(nc, P, yT, ByT, r0, nrows, T, sb):
    with Phase(P) as st:
        z = sb(f"zr{r0}", [128, T], BF16, st)
        Bz = Buf("z")
        P.op("pool", lambda e: e.memset(z[:], 0.0), writes=[Bz])
        for r in range(r0, r0 + nrows, 128):
            P.dma("sp", lambda e, r=r: e.dma_start(out=yT[r:r + 128, :], in_=z[:]), reads=[Bz], writes=[ByT])
        for ev in list(ByT.r) + ([ByT.w] if ByT.w else []):
            pass


def phase_sc(nc, P, l, T, pfT, yT, sc_conv_w, Bpf, ByT, sb):
    with Phase(P) as st:
        cw = sb("cw_sc", [128, 4, 3], F32, st)
        Bcw = Buf("cw")
        for ci in range(4):
            for k in range(3):
                P.dma("sp", lambda e, ci=ci, k=k: e.dma_start(out=cw[:, ci, k:k + 1], in_=sc_conv_w[l, k, ci * 128:(ci + 1) * 128].rearrange("(p o) -> p o", o=1)), writes=[Bcw])
        bt = sb("b_sc", [128, T], F32, st)
        ct = sb("c_sc", [128, T], F32, st)
        ht = sb("h_sc", [128, T], F32, st)
        acc = sb("acc_sc", [128, T], F32, st)
        yo = sb("yo_sc", [128, T], BF16, st)
        Bb, Bc_, Bh, Ba, By = Buf(), Buf(), Buf(), Buf(), Buf()
        for ci in range(4):
            P.dma("sp", lambda e, ci=ci: e.dma_start(out=bt[:], in_=pfT[ci * 128:(ci + 1) * 128, :]), reads=[Bpf], writes=[Bb])
            P.dma("sp", lambda e, ci=ci: e.dma_start(out=ct[:], in_=pfT[512 + ci * 128:512 + (ci + 1) * 128, :]), reads=[Bpf], writes=[Bc_])
            P.dma("sp", lambda e, ci=ci: e.dma_start(out=ht[:], in_=pfT[1024 + ci * 128:1024 + (ci + 1) * 128, :]), reads=[Bpf], writes=[Bh])
            P.op("dve", lambda e: e.tensor_tensor(ct[:], ct[:], ht[:], ALU.mult), reads=[Bc_, Bh], writes=[Bc_])
            P.op("dve", lambda e, ci=ci: e.tensor_scalar(acc[:], ct[:], cw[:, ci, 2:3], None, ALU.mult), reads=[Bc_, Bcw], writes=[Ba])
            P.op("dve", lambda e, ci=ci: e.scalar_tensor_tensor(acc[:, 1:T], ct[:, 0:T - 1], cw[:, ci, 1:2], acc[:, 1:T], ALU.mult, ALU.add), reads=[Bc_, Bcw, Ba], writes=[Ba])
            P.op("dve", lambda e, ci=ci: e.scalar_tensor_tensor(acc[:, 2:T], ct[:, 0:T - 2], cw[:, ci, 0:1], acc[:, 2:T], ALU.mult, ALU.add), reads=[Bc_, Bcw, Ba], writes=[Ba])
            P.op("dve", lambda e: e.tensor_tensor(yo[:], acc[:], bt[:], ALU.mult), reads=[Ba, Bb], writes=[By])
            P.dma("sp", lambda e, ci=ci: e.dma_start(out=yT[768 + ci * 128:768 + (ci + 1) * 128, :], in_=yo[:]), reads=[By], writes=[ByT])


def phase_sb(nc, P, l, T, qkT, vtok, yT, Bqk, Bv, ByT, ps, psb, sb):
    NQT = T // 512
    NTB = T // 128
    scale = 128 ** -0.5
    with Phase(P) as st:
        Bk = Buf("sbconst")
        tmp = sb("sbtmp", [128, 4, 512], F32, st)
        uinc = sb("uinc", [128, 128], BF16, st)
        remm = sb("remm", [128, 128], BF16, st)
        masks = sb("masks", [128, 4, 512], BF16, st)
        P.op("pool", lambda e: e.memset(tmp[:, 0, 0:128], -1.0), writes=[Bk])
        P.op("pool", lambda e: e.affine_select(out=tmp[:, 0, 0:128], in_=tmp[:, 0, 0:128], pattern=[[-1, 128]], compare_op=ALU.is_ge, fill=0.0,
                                                base=0, channel_multiplier=1), reads=[Bk], writes=[Bk])
        P.op("pool", lambda e: e.tensor_copy(uinc[:], tmp[:, 0, 0:128]), reads=[Bk], writes=[Bk])
        P.op("pool", lambda e: e.memset(tmp[:, 0, 0:128], -1.0), reads=[Bk], writes=[Bk])
        P.op("pool", lambda e: e.affine_select(out=tmp[:, 0, 0:128], in_=tmp[:, 0, 0:128], pattern=[[1, 128]], compare_op=ALU.is_gt, fill=0.0,
                                                base=0, channel_multiplier=-1), reads=[Bk], writes=[Bk])
        P.op("pool", lambda e: e.tensor_copy(remm[:], tmp[:, 0, 0:128]), reads=[Bk], writes=[Bk])
        P.op("pool", lambda e: e.memset(tmp[:], 1.0), reads=[Bk], writes=[Bk])
        P.op("pool", lambda e: e.affine_select(out=tmp[:], in_=tmp[:], pattern=[[-128, 4], [1, 512]], compare_op=ALU.is_gt, fill=0.0,
                                                base=0, channel_multiplier=-1), reads=[Bk], writes=[Bk])
        P.op("pool", lambda e: e.tensor_copy(masks[:], tmp[:]), reads=[Bk], writes=[Bk])

        QT = [sb(f"QT{i}", [128, T], BF16, st) for i in range(2)]
        KTt = [sb(f"KTt{i}", [128, T], BF16, st) for i in range(2)]
        VV = [sb(f"VV{i}", [128, NTB, 128], BF16, st) for i in range(2)]
        BQ = [Buf(), Buf()]
        BK_ = [Buf(), Buf()]
        BV_ = [Buf(), Buf()]
        Et = [sb(f"E{i}", [128, 512], F32, st) for i in range(3)]
        SPt = [sb(f"SP{i}", [128, 512], BF16, st) for i in range(3)]
        Xt = [sb(f"X{i}", [128, 512], F32, st) for i in range(2)]
        Wt = [sb(f"W{i}", [128, 512], BF16, st) for i in range(3)]
        Ot = [sb(f"Osb{i}", [128, 512], BF16, st) for i in range(2)]
        BE = [Buf() for _ in range(3)]
        BSP = [Buf() for _ in range(3)]
        BX = [Buf() for _ in range(2)]
        BW = [Buf() for _ in range(3)]
        BOt = [Buf() for _ in range(2)]

        pairs = []
        g = 0
        for h in range(6):
            for qt in range(NQT):
                kbs = list(range(4 * qt + 3, -1, -1))
                for n, kb in enumerate(kbs):
                    pairs.append((h, qt, kb, n == 0, n == len(kbs) - 1, g))
                g += 1
        loaded = set()

        def load_head(h):
            if h in loaded or h >= 6:
                return
            loaded.add(h)
            b = h % 2
            P.dma("sp", lambda e: e.dma_start(out=QT[b][:], in_=qkT[h * 128:(h + 1) * 128, :]), reads=[Bqk], writes=[BQ[b]])
            P.dma("sp", lambda e: e.dma_start(out=KTt[b][:], in_=qkT[768 + h * 128:768 + (h + 1) * 128, :]), reads=[Bqk], writes=[BK_[b]])
            P.dma("sp", lambda e: e.dma_start(out=VV[b][:], in_=vtok[:, h * 128:(h + 1) * 128].rearrange("(blk p) d -> p blk d", p=128)),
                  reads=[Bv], writes=[BV_[b]])

        def stage1z(i):
            h, qt, kb, first, last, g = pairs[i]
            load_head(h)
            b = h % 2
            z, Bz = ps[i % 2], psb[i % 2]
            q0 = qt * 512
            P.op("pe", lambda e: e.matmul(z[:], KTt[b][:, kb * 128:(kb + 1) * 128], QT[b][:, q0:q0 + 512], start=True, stop=True),
                 reads=[BK_[b], BQ[b]], writes=[Bz])

        def stage1a(i):
            h, qt, kb, first, last, g = pairs[i]
            z, Bz = ps[i % 2], psb[i % 2]
            E, SPb = Et[i % 3], SPt[i % 3]
            P.op("act", lambda e: e.activation(E[:], z[:], AF.Exp, scale=scale), reads=[Bz], writes=[BE[i % 3]])
            P.op("act", lambda e: e.activation(SPb[:], E[:], AF.Ln, bias=1.0), reads=[BE[i % 3]], writes=[BSP[i % 3]])
            r = kb - 4 * qt
            if r >= 0:
                P.op("dve", lambda e: e.tensor_tensor(SPb[:], SPb[:], masks[:, r, :], ALU.mult), reads=[BSP[i % 3], Bk], writes=[BSP[i % 3]])
                P.op("pool", lambda e: e.tensor_tensor(E[:], E[:], masks[:, r, :], ALU.mult), reads=[BE[i % 3], Bk], writes=[BE[i % 3]])

        def stage2a(i):
            h, qt, kb, first, last, g = pairs[i]
            C, BC = ps[2 + g % 2], psb[2 + g % 2]
            SPb = SPt[i % 3]
            P.op("pe", lambda e: e.matmul(C[:], uinc[:], SPb[:], start=first, stop=True, skip_group_check=True), reads=[BSP[i % 3], Bk], writes=[BC])

        def stage2b(i):
            h, qt, kb, first, last, g = pairs[i]
            C, BC = ps[2 + g % 2], psb[2 + g % 2]
            E, SPb = Et[i % 3], SPt[i % 3]
            X, W = Xt[i % 2], Wt[i % 3]
            P.op("act", lambda e: e.activation(X[:], C[:], AF.Exp), reads=[BC], writes=[BX[i % 2]])
            P.op("pe", lambda e: e.matmul(C[:], remm[:], SPb[:], start=False, stop=True, skip_group_check=True), reads=[BSP[i % 3], Bk], writes=[BC])
            P.op("dve", lambda e: e.tensor_tensor(W[:], X[:], E[:], ALU.mult), reads=[BX[i % 2], BE[i % 3]], writes=[BW[i % 3]])

        def stage3(i):
            h, qt, kb, first, last, g = pairs[i]
            b = h % 2
            O, BO = ps[4 + g % 2], psb[4 + g % 2]
            W = Wt[i % 3]
            P.op("pe", lambda e: e.matmul(O[:], VV[b][:, kb, :], W[:], start=first, stop=last), reads=[BW[i % 3], BV_[b]], writes=[BO])
            if last:
                o, Bo = Ot[g % 2], BOt[g % 2]
                P.op("act", lambda e: e.activation(o[:], O[:], AF.Copy), reads=[BO], writes=[Bo])
                q0 = qt * 512
                P.dma("sp", lambda e: e.dma_start(out=yT[h * 128:(h + 1) * 128, q0:q0 + 512], in_=o[:]), reads=[Bo], writes=[ByT])
                if qt == NQT - 1:
                    load_head(h + 2) if (h + 2) % 2 == h % 2 else None

        load_head(0)
        load_head(1)
        n = len(pairs)
        stage1z(0)
        for s_ in range(n + 2):
            if s_ < n:
                stage1a(s_)
            if 0 <= s_ - 1 < n:
                stage2a(s_ - 1)
            if s_ + 1 < n:
                stage1z(s_ + 1)
            if 0 <= s_ - 1 < n:
                stage2b(s_ - 1)
            if 0 <= s_ - 2 < n:
                stage3(s_ - 2)


def phase_gdn(nc, P, l, T, pfT, abtok, yT, gdn_conv_w, gdn_a_log, gdn_dt_bias, gdn_norm_w, Bpf, Bab, ByT, ps, psb, sb, ident, ones_bf, Bc):
    NB = T // 128
    NTT = T // 512
    with Phase(P) as st:
        def op(eng, fn, reads=(), writes=()):
            P.op(eng, fn, reads=list(reads), writes=list(writes))

        def t(name, shape, dt):
            return sb("g_" + name, shape, dt, st), Buf(name)
        Bk = Buf("gconst")
        tri, _ = t("tri", [128, 128], F32)
        trib, _ = t("trib", [128, 128], BF16)
        negL, _ = t("negL", [128, 128], F32)
        blk1, _ = t("blk1", [128, 128], F32)
        half0, _ = t("half0", [128, 128], F32)
        half1, _ = t("half1", [128, 128], F32)
        onesf, _ = t("onesf", [128, 128], F32)
        identb, _ = t("identb", [128, 128], BF16)
        op("pool", lambda e: e.memset(tri[:], 1.0), [], [Bk])
        op("pool", lambda e: e.affine_select(out=tri[:], in_=tri[:], pattern=[[1, 128]], compare_op=ALU.is_ge, fill=0.0, base=0, channel_multiplier=-1), [Bk], [Bk])
        op("pool", lambda e: e.memset(tri[0:64, 64:128], 0.0), [Bk], [Bk])
        op("pool", lambda e: e.tensor_copy(trib[:], tri[:]), [Bk], [Bk])
        op("pool", lambda e: e.memset(negL[:], -1.0), [Bk], [Bk])
        op("pool", lambda e: e.affine_select(out=negL[:], in_=negL[:], pattern=[[-1, 128]], compare_op=ALU.is_gt, fill=0.0, base=0, channel_multiplier=1), [Bk], [Bk])
        op("pool", lambda e: e.memset(negL[64:128, 0:64], 0.0), [Bk], [Bk])
        op("pool", lambda e: e.memset(blk1[:], 0.0), [Bk], [Bk])
        op("pool", lambda e: e.memset(blk1[0:64, 0:64], 1.0), [Bk], [Bk])
        op("pool", lambda e: e.memset(blk1[64:128, 64:128], 1.0), [Bk], [Bk])
        op("pool", lambda e: e.memset(half0[:], 0.0), [Bk], [Bk])
        op("pool", lambda e: e.memset(half0[0:64, :], 1.0), [Bk], [Bk])
        op("pool", lambda e: e.memset(half1[:], 0.0), [Bk], [Bk])
        op("pool", lambda e: e.memset(half1[64:128, :], 1.0), [Bk], [Bk])
        op("pool", lambda e: e.memset(onesf[:], 1.0), [Bk], [Bk])
        op("pool", lambda e: e.tensor_copy(identb[:], ident[:]), [Bk, Bc], [Bk])
        cwg, Bcw = t("cwg", [128, 3, 6, 4], F32)
        for x in range(3):
            for h in range(6):
                for i in range(4):
                    c0 = x * 768 + h * 128
                    P.dma("sp", lambda e, x=x, h=h, i=i, c0=c0: e.dma_start(out=cwg[:, x, h, i:i + 1], in_=gdn_conv_w[l, i, c0:c0 + 128].rearrange("(p o) -> p o", o=1)), writes=[Bcw])
        gnw, Bgnw = t("gnw", [128, 1], F32)
        P.dma("sp", lambda e: e.dma_start(out=gnw[:], in_=gdn_norm_w[l].rearrange("(p o) -> p o", o=1)), writes=[Bgnw])
        alb, Balb = t("alb", [128, 6], F32)
        dtb, Bdtb = t("dtb", [128, 6], F32)
        P.dma("sp", lambda e: e.dma_start(out=alb[:], in_=gdn_a_log[l:l + 1, :].broadcast_to([128, 6])), writes=[Balb])
        P.dma("sp", lambda e: e.dma_start(out=dtb[:], in_=gdn_dt_bias[l:l + 1, :].broadcast_to([128, 6])), writes=[Bdtb])
        nea, Bnea = t("nea", [128, 6], F32)
        op("act", lambda e: e.activation(nea[:], alb[:], AF.Exp), [Balb], [Bnea])
        op("dve", lambda e: e.tensor_scalar(nea[:], nea[:], -1.0, None, ALU.mult), [Bnea], [Bnea])
        ab, Bab_s = t("ab", [128, NB, 12], F32)
        P.dma("sp", lambda e: e.dma_start(out=ab[:], in_=abtok.rearrange("(blk p) c -> p blk c", p=128)), reads=[Bab], writes=[Bab_s])
        g_, Bg = t("g", [128, NB, 6], F32)
        beta, Bbeta = t("beta", [128, NB, 6], F32)
        gcs, Bgcs = t("gcs", [128, NB, 6], F32)
        kbs, Bkbs = t("kbs", [128, NB, 6], F32)
        kds, Bkds = t("kds", [128, NB, 6], F32)
        eb0, Beb0 = t("eb0", [128, NB, 6], F32)
        eb1, Beb1 = t("eb1", [128, NB, 6], F32)
        tA, BtA = t("tA", [128, 6], F32)
        tB, BtB = t("tB", [128, 6], F32)
        p7, B7 = ps[7], psb[7]
        for b in range(NB):
            op("dve", lambda e, b=b: e.tensor_tensor(tA[:], ab[:, b, 0:6], dtb[:], ALU.add), [Bab_s, Bdtb], [BtA])
            op("act", lambda e: e.activation(tA[:], tA[:], AF.Exp), [BtA], [BtA])
            op("act", lambda e: e.activation(tA[:], tA[:], AF.Ln, bias=1.0), [BtA], [BtA])
            op("dve", lambda e, b=b: e.tensor_tensor(g_[:, b, :], tA[:], nea[:], ALU.mult), [BtA, Bnea], [Bg])
            op("act", lambda e, b=b: e.activation(tB[:], ab[:, b, 6:12], AF.Exp, scale=-1.0), [Bab_s], [BtB])
            op("dve", lambda e: e.tensor_scalar(tB[:], tB[:], 1.0, None, ALU.add), [BtB], [BtB])
            op("dve", lambda e, b=b: e.reciprocal(beta[:, b, :], tB[:]), [BtB], [Bbeta])
            P.op("pe", lambda e, b=b: e.matmul(p7[:, 0:6], tri[:], g_[:, b, :], start=True, stop=True), reads=[Bg, Bk], writes=[B7])
            P.op("pe", lambda e, b=b: e.matmul(p7[:, 8:14], blk1[:], g_[:, b, :], start=True, stop=True), reads=[Bg, Bk], writes=[B7])
            P.op("pe", lambda e, b=b: e.matmul(p7[:, 16:22], half0[:], g_[:, b, :], start=True, stop=True), reads=[Bg, Bk], writes=[B7])
            P.op("pe", lambda e, b=b: e.matmul(p7[:, 24:30], half1[:], g_[:, b, :], start=True, stop=True), reads=[Bg, Bk], writes=[B7])
            op("dve", lambda e, b=b: e.tensor_copy(gcs[:, b, :], p7[:, 0:6]), [B7], [Bgcs])
            op("act", lambda e, b=b: e.activation(tA[:], p7[:, 0:6], AF.Exp), [B7], [BtA])
            op("dve", lambda e, b=b: e.tensor_tensor(kbs[:, b, :], tA[:], beta[:, b, :], ALU.mult), [BtA, Bbeta], [Bkbs])
            op("dve", lambda e, b=b: e.tensor_tensor(tB[:], p7[:, 8:14], gcs[:, b, :], ALU.subtract), [B7, Bgcs], [BtB])
            op("act", lambda e, b=b: e.activation(kds[:, b, :], tB[:], AF.Exp), [BtB], [Bkds])
            op("act", lambda e, b=b: e.activation(eb0[:, b, :], p7[:, 16:22], AF.Exp), [B7], [Beb0])
            op("act", lambda e, b=b: e.activation(eb1[:, b, :], p7[:, 24:30], AF.Exp), [B7], [Beb1])
        xin, Bxin = t("xin", [128, T + 3], F32)
        acc, Bacc = t("acc", [128, T], F32)
        sqb, Bsqb = t("sqb", [128, 512], BF16)
        rr, Brr = t("rr", [128, 512], F32)
        zt, Bzt = t("zt", [128, 512], F32)
        yg, Byg = t("yg", [128, 512], F32)
        yo, Byo = t("yo", [128, 512], BF16)
        op("pool", lambda e: e.memset(xin[:, 0:3], 0.0), [], [Bxin])
        qscale = 128 ** -0.5

        class Ctx:
            pass
        ctxs = []
        for ci in range(2):
            C = Ctx()
            for nm, shp, dt in (("vc", [128, T], F32), ("qn", [128, T], BF16), ("kn", [128, T], BF16), ("oT", [128, T], F32),
                                ("S32", [128, 128], F32), ("Sb", [128, 128], BF16), ("gb", [128, 128], F32), ("d1", [128, 128], F32),
                                ("dl", [128, 128], F32), ("du", [128, 128], F32), ("er", [128, 128], F32), ("qd", [128, 128], BF16),
                                ("t1", [128, 128], F32), ("AT", [128, 128], BF16), ("kbg", [128, 128], BF16), ("kdec", [128, 128], BF16),
                                ("vb", [128, 128], BF16), ("ktok", [128, 128], F32), ("vtk", [128, 128], F32), ("PTm", [128, 2, 128], BF16),
                                ("wTm", [128, 2, 128], BF16), ("usb", [128, 128], F32), ("vnew", [128, 128], BF16)):
                tt_, bb_ = t(f"{nm}c{ci}", shp, dt)
                setattr(C, nm, tt_)
                setattr(C, "B" + nm, bb_)
            C.Nn = [t(f"N{i}c{ci}", [128, 128], BF16) for i in range(2)]
            C.NT = [t(f"NT{i}c{ci}", [128, 128], BF16) for i in range(2)]
            C.PT = [t(f"PT{i}c{ci}", [128, 128], BF16) for i in range(2)]
            C.pA, C.pB, C.pC, C.pD = ps[4 * ci:4 * ci + 4]
            C.BA, C.BB, C.BC, C.BD = psb[4 * ci:4 * ci + 4]
            C.pDb = C.pD[:].bitcast(BF16)
            op("pool", lambda e, C=C: e.memset(C.PTm[:], 0.0), [], [C.BPTm])
            op("pool", lambda e, C=C: e.memset(C.wTm[:], 0.0), [], [C.BwTm])
            ctxs.append(C)

        def conv_norm(h, C):
            for x in range(3):
                r0 = 1536 + x * 768 + h * 128
                P.dma("sp", lambda e, r0=r0: e.dma_start(out=xin[:, 3:T + 3], in_=pfT[r0:r0 + 128, :]), reads=[Bpf], writes=[Bxin])
                op("dve", lambda e, x=x: e.tensor_scalar(acc[:], xin[:, 3:T + 3], cwg[:, x, h, 3:4], None, ALU.mult), [Bxin, Bcw], [Bacc])
                for i in range(3):
                    op("dve", lambda e, x=x, i=i: e.scalar_tensor_tensor(acc[:], xin[:, i:i + T], cwg[:, x, h, i:i + 1], acc[:], ALU.mult, ALU.add), [Bxin, Bcw, Bacc], [Bacc])
                if x == 2:
                    op("act", lambda e: e.activation(C.vc[:], acc[:], AF.Silu), [Bacc], [C.Bvc])
                else:
                    op("act", lambda e: e.activation(acc[:], acc[:], AF.Silu), [Bacc], [Bacc])
                    dst, Bdst = (C.qn, C.Bqn) if x == 0 else (C.kn, C.Bkn)
                    sc_ = qscale if x == 0 else 1.0
                    for tt in range(NTT):
                        sl = slice(tt * 512, (tt + 1) * 512)
                        op("act", lambda e, sl=sl: e.activation(sqb[:], acc[:, sl], AF.Square), [Bacc], [Bsqb])
                        P.op("pe", lambda e: e.matmul(p7[:], ones_bf[:], sqb[:], start=True, stop=True), reads=[Bsqb, Bc], writes=[B7])
                        op("act", lambda e: e.activation(rr[:], p7[:], AF.Sqrt, bias=L2_EPS), [B7], [Brr])
                        op("dve", lambda e: e.reciprocal(rr[:], rr[:]), [Brr], [Brr])
                        op("dve", lambda e, sl=sl, dst=dst, sc_=sc_: e.scalar_tensor_tensor(dst[:, sl], acc[:, sl], sc_, rr[:], ALU.mult, ALU.mult), [Bacc, Brr], [Bdst])

        def block_ops(h, b, C, L):
            def rop(eng, fn, reads, writes):
                L.append(lambda: P.op(eng, fn, reads=list(reads), writes=list(writes)))
            pA, pB, pC, pD, pDb = C.pA, C.pB, C.pC, C.pD, C.pDb
            BA, BB, BC, BD = C.BA, C.BB, C.BC, C.BD
            c0 = b * 128
            ksl = C.kn[:, c0:c0 + 128]
            qsl = C.qn[:, c0:c0 + 128]
            rop("pe", lambda e: e.transpose(pDb[:, 0:128], ksl, identb[:]), [C.Bkn, Bk], [BD])
            rop("act", lambda e: e.activation(C.ktok[:], pDb[:, 0:128], AF.Copy), [BD], [C.Bktok])
            rop("dve", lambda e: e.tensor_scalar(C.kbg[:], C.ktok[:], kbs[:, b, h:h + 1], None, ALU.mult), [C.Bktok, Bkbs], [C.Bkbg])
            rop("dve", lambda e: e.tensor_scalar(C.kdec[:], C.ktok[:], kds[:, b, h:h + 1], None, ALU.mult), [C.Bktok, Bkds], [C.Bkdec])
            rop("pe", lambda e: e.transpose(pC[:, 0:128], C.vc[:, c0:c0 + 128], ident[:]), [C.Bvc, Bc], [BC])
            rop("act", lambda e: e.activation(C.vtk[:], pC[:, 0:128], AF.Copy), [BC], [C.Bvtk])
            rop("dve", lambda e: e.tensor_scalar(C.vb[:], C.vtk[:], beta[:, b, h:h + 1], None, ALU.mult), [C.Bvtk, Bbeta], [C.Bvb])
            rop("pe", lambda e: e.matmul(pA[:, 0:128], ksl, ksl, start=True, stop=True), [C.Bkn], [BA])
            rop("pe", lambda e: e.matmul(pB[:, 0:128], ksl, qsl, start=True, stop=True), [C.Bkn, C.Bqn], [BB])
            rop("dve", lambda e: e.tensor_scalar(C.gb[:], onesf[:], g_[:, b, h:h + 1], None, ALU.mult), [Bg, Bk], [C.Bgb])
            rop("pe", lambda e: e.matmul(pC[:, 0:128], C.gb[:], tri[:], start=True, stop=True), [C.Bgb, Bk], [BC])
            rop("dve", lambda e: e.tensor_scalar(C.d1[:], pC[:, 0:128], -1.0, gcs[:, b, h:h + 1], ALU.mult, ALU.add), [BC, Bgcs], [C.Bd1])
            rop("act", lambda e: e.activation(C.er[:], pC[:, 0:128], AF.Exp), [BC], [C.Ber])
            rop("dve", lambda e: e.tensor_scalar(C.dl[:], C.d1[:], 0.0, None, ALU.min), [C.Bd1], [C.Bdl])
            rop("dve", lambda e: e.tensor_scalar(C.du[:], C.d1[:], -1.0, 0.0, ALU.mult, ALU.min), [C.Bd1], [C.Bdu])
            rop("act", lambda e: e.activation(C.dl[:], C.dl[:], AF.Exp), [C.Bdl], [C.Bdl])
            rop("act", lambda e: e.activation(C.du[:], C.du[:], AF.Exp), [C.Bdu], [C.Bdu])
            rop("dve", lambda e: e.tensor_tensor(C.qd[:], qsl, C.er[:], ALU.mult), [C.Bqn, C.Ber], [C.Bqd])
            N, BN = C.Nn[0]
            NTt, BNT = C.NT[0]
            PTt, BPT = C.PT[0]
            rop("dve", lambda e: e.scalar_tensor_tensor(C.t1[:], pA[:, 0:128], beta[:, b, h:h + 1], C.dl[:], ALU.mult, ALU.mult), [BA, Bbeta, C.Bdl], [C.Bt1])
            rop("dve", lambda e, N=N: e.tensor_tensor(N[:], C.t1[:], negL[:], ALU.mult), [C.Bt1, Bk], [BN])
            rop("pe", lambda e, N=N: e.transpose(pDb[:, 0:128], N[:], identb[:]), [BN, Bk], [BD])
            rop("act", lambda e, NTt=NTt: e.activation(NTt[:], pDb[:, 0:128], AF.Copy), [BD], [BNT])
            rop("dve", lambda e, PTt=PTt: e.tensor_tensor(PTt[:], pDb[:, 0:128], identb[:], ALU.add), [BD, Bk], [BPT])
            rop("dve", lambda e: e.tensor_tensor(C.t1[:], pB[:, 0:128], C.du[:], ALU.mult), [BB, C.Bdu], [C.Bt1])
            rop("dve", lambda e: e.tensor_tensor(C.AT[:], C.t1[:], tri[:], ALU.mult), [C.Bt1, Bk], [C.BAT])
            cur = 0
            for step in range(5):
                N, BN = C.Nn[cur]
                NTt, BNT = C.NT[cur]
                PTt, BPT = C.PT[cur]
                N2, BN2 = C.Nn[1 - cur]
                NT2, BNT2 = C.NT[1 - cur]
                PT2, BPT2 = C.PT[1 - cur]
                rop("pe", lambda e, NTt=NTt, N=N: e.matmul(pA[:, 0:128], NTt[:], N[:], start=True, stop=True), [BNT, BN], [BA])
                rop("act", lambda e, N2=N2: e.activation(N2[:], pA[:, 0:128], AF.Copy), [BA], [BN2])
                if step < 4:
                    rop("pe", lambda e, NTt=NTt, N=N: e.matmul(pB[:, 0:128], N[:], NTt[:], start=True, stop=True), [BNT, BN], [BB])
                    rop("act", lambda e, NT2=NT2: e.activation(NT2[:], pB[:, 0:128], AF.Copy), [BB], [BNT2])
                rop("pe", lambda e, N2=N2, PTt=PTt: e.matmul(pC[:, 0:128], N2[:], PTt[:], start=True, stop=True), [BN2, BPT], [BC])
                rop("dve", lambda e, PT2=PT2, PTt=PTt: e.tensor_tensor(PT2[:], pC[:, 0:128], PTt[:], ALU.add), [BC, BPT], [BPT2])
                cur = 1 - cur
            PTt, BPT = C.PT[cur]
            rop("act", lambda e, PTt=PTt: e.activation(C.PTm[:, 0, 0:64], PTt[:, 0:64], AF.Copy), [BPT], [C.BPTm])
            rop("act", lambda e, PTt=PTt: e.activation(C.PTm[:, 1, 64:128], PTt[:, 64:128], AF.Copy), [BPT], [C.BPTm])
            rop("pe", lambda e, PTt=PTt: e.matmul(pA[:, 0:128], C.kbg[:], PTt[:], start=True, stop=True), [C.Bkbg, BPT], [BA])
            rop("dve", lambda e: e.tensor_copy(C.wTm[:, 0, 0:64], pA[:, 0:64]), [BA], [C.BwTm])
            rop("dve", lambda e: e.tensor_copy(C.wTm[:, 1, 64:128], pA[:, 64:128]), [BA], [C.BwTm])
            for c in range(2):
                cs = slice(c * 64, (c + 1) * 64)
                rop("pe", lambda e, c=c: e.matmul(pC[:, 0:128], C.PTm[:, c, :], C.vb[:], start=True, stop=True), [C.BPTm, C.Bvb], [BC])
                rop("act", lambda e: e.activation(C.usb[:], pC[:, 0:128], AF.Copy), [BC], [C.Busb])
                rop("pe", lambda e, c=c: e.matmul(pA[:, 0:128], C.wTm[:, c, :], C.Sb[:], start=True, stop=True), [C.BwTm, C.BSb], [BA])
                rop("dve", lambda e: e.tensor_tensor(C.vnew[:], C.usb[:], pA[:, 0:128], ALU.subtract), [C.Busb, BA], [C.Bvnew])
                rop("pe", lambda e, cs=cs: e.matmul(pD[:, cs], C.Sb[:], C.qd[:, cs], start=True, stop=False), [C.BSb, C.Bqd], [BD])
                rop("pe", lambda e, cs=cs: e.matmul(pD[:, cs], C.vnew[:], C.AT[:, cs], start=False, stop=True), [C.Bvnew, C.BAT], [BD])
                rop("pe", lambda e: e.matmul(pB[:, 0:128], C.kdec[:], C.vnew[:], start=True, stop=True), [C.Bkdec, C.Bvnew], [BB])
                ebc = eb0 if c == 0 else eb1
                Bebc = Beb0 if c == 0 else Beb1
                rop("dve", lambda e, ebc=ebc: e.scalar_tensor_tensor(C.S32[:], C.S32[:], ebc[:, b, h:h + 1], pB[:, 0:128], ALU.mult, ALU.add), [C.BS32, Bebc, BB], [C.BS32])
                rop("act", lambda e: e.activation(C.Sb[:], C.S32[:], AF.Copy), [C.BS32], [C.BSb])
            rop("act", lambda e: e.activation(C.oT[:, c0:c0 + 128], pD[:, 0:128], AF.Copy), [BD], [C.BoT])

        def onorm(h, C):
            for tt in range(NTT):
                sl = slice(tt * 512, (tt + 1) * 512)
                op("act", lambda e, sl=sl: e.activation(sqb[:], C.oT[:, sl], AF.Square), [C.BoT], [Bsqb])
                P.op("pe", lambda e: e.matmul(p7[:], ones_bf[:], sqb[:], start=True, stop=True), reads=[Bsqb, Bc], writes=[B7])
                op("act", lambda e: e.activation(rr[:], p7[:], AF.Sqrt, bias=RMS_EPS, scale=1.0 / 128), [B7], [Brr])
                op("dve", lambda e: e.reciprocal(rr[:], rr[:]), [Brr], [Brr])
                op("dve", lambda e, sl=sl: e.scalar_tensor_tensor(yg[:], C.oT[:, sl], gnw[:, 0:1], rr[:], ALU.mult, ALU.mult), [C.BoT, Brr, Bgnw], [Byg])
                r0 = 1536 + 2304 + h * 128
                P.dma("sp", lambda e, r0=r0, sl=sl: e.dma_start(out=zt[:], in_=pfT[r0:r0 + 128, sl]), reads=[Bpf], writes=[Bzt])
                op("act", lambda e: e.activation(zt[:], zt[:], AF.Silu), [Bzt], [Bzt])
                op("dve", lambda e: e.tensor_tensor(yo[:], yg[:], zt[:], ALU.mult), [Byg, Bzt], [Byo])
                P.dma("sp", lambda e, sl=sl: e.dma_start(out=yT[1280 + h * 128:1280 + (h + 1) * 128, sl], in_=yo[:]), reads=[Byo], writes=[ByT])

        for hp in range(3):
            heads = (2 * hp, 2 * hp + 1)
            lists = []
            for ci, h in enumerate(heads):
                C = ctxs[ci]
                conv_norm(h, C)
                L = []
                L.append(lambda C=C: P.op("pool", lambda e: e.memset(C.S32[:], 0.0), reads=[C.BS32], writes=[C.BS32]))
                L.append(lambda C=C: P.op("pool", lambda e: e.memset(C.Sb[:], 0.0), reads=[C.BSb], writes=[C.BSb]))
                for b in range(NB):
                    block_ops(h, b, C, L)
                lists.append(L)
            for i in range(max(len(L) for L in lists)):
                for L in lists:
                    if i < len(L):
                        L[i]()
            for ci, h in enumerate(heads):
                onorm(h, ctxs[ci])


_CACHE = {}


def kernel(**inputs):
    B, T, _ = inputs["x"].shape
    key = (T,)
    if key not in _CACHE:
        _CACHE[key] = build(T=T, depth=2)
    nc = _CACHE[key]
    in_maps = []
    for c in range(B):
        m = {k: np.ascontiguousarray(np.asarray(v, dtype=np.float32)) for k, v in inputs.items() if k != "x"}
        m["x"] = np.ascontiguousarray(np.asarray(inputs["x"][c], dtype=np.float32))
        in_maps.append(m)
    res = run_bass_kernel_spmd(nc, in_maps, core_ids=list(range(B)))
    return np.stack([np.asarray(r["out"], dtype=np.float32) for r in res.results], axis=0)
```

```python
import contextlib
import numpy as np
import concourse.bass as bass
import concourse.mybir as mybir
from concourse.bass_utils import run_bass_kernel_spmd

F32 = mybir.dt.float32
BF16 = mybir.dt.bfloat16
AF = mybir.ActivationFunctionType
ALU = mybir.AluOpType

D = 2048
KT = 16
DFF = 8192
IN_DIM = 6924
RMS_EPS = 1e-6
L2_EPS = 1e-6
import os
GV = os.environ.get("GV", "v3")
GSTOP = int(os.environ.get("GSTOP", "9"))


class Buf:
    __slots__ = ("name", "w", "r", "psum")

    def __init__(self, name="", psum=False):
        self.name = name
        self.w = None
        self.r = []
        self.psum = psum


class Phase(contextlib.ExitStack):
    def __init__(self, P):
        super().__init__()
        self.P = P

    def __exit__(self, *a):
        if a[0] is None:
            self.P.barrier()
        return super().__exit__(*a)


class Prog:
    ENG = ("pe", "act", "dve", "pool", "sp")

    def __init__(self, nc, stack):
        self.nc = nc
        self.lists = {e: [] for e in self.ENG}
        self.sem = {e: stack.enter_context(nc.semaphore("s_" + e)) for e in self.ENG if e != "sp"}
        self.cnt = {e: 0 for e in self.ENG}
        self.known = {e: {} for e in self.ENG}
        self.dsem = {}
        self.dcnt = {}
        self.drr = {}
        for q, n in (("sp", 16), ("pool", 8), ("act", 4)):
            self.dsem[q] = [stack.enter_context(nc.semaphore(f"d_{q}{i}")) for i in range(n)]
            self.dcnt[q] = [0] * n
            self.drr[q] = 0
        self.out_events = []

    def _wait(self, eng, ev):
        sem, val = ev[0], ev[1]
        k = self.known[eng]
        if k.get(id(sem), 0) >= val:
            return
        k[id(sem)] = val
        self.lists[eng].append(("wait", sem, val))

    def _deps(self, eng, reads, writes):
        for b in reads:
            if b.w is not None and not (eng == "pe" and b.w[2] == "pe"):
                self._wait(eng, b.w)
            if b.psum:
                for ev in b.r:
                    if ev[2] != eng:
                        self._wait(eng, ev)
        for b in writes:
            if b.w is not None and not (eng == "pe" and b.w[2] == "pe"):
                self._wait(eng, b.w)
            for ev in b.r:
                if not (eng == "pe" and ev[2] == "pe"):
                    self._wait(eng, ev)

    def _mark(self, ev, reads, writes):
        for b in reads:
            b.r = [e for e in b.r if e[0] is not ev[0]] + [ev]
        for b in writes:
            b.w = ev
            b.r = []

    def op(self, eng, fn, reads=(), writes=(), inc=True):
        self._deps(eng, reads, writes)
        if inc:
            self.cnt[eng] += 1
            ev = (self.sem[eng], self.cnt[eng], eng)
            self.lists[eng].append(("op", fn, self.sem[eng]))
        else:
            ev = (self.sem[eng], self.cnt[eng] + 1, eng)
            self.lists[eng].append(("op", fn, None))
        self._mark(ev, reads, writes)
        return ev

    def dma(self, q, fn, reads=(), writes=(), is_output=False):
        i = self.drr[q]
        self.drr[q] = (i + 1) % len(self.dsem[q])
        sem = self.dsem[q][i]
        if self.dcnt[q][i] > 0:
            self._wait(q, (sem, self.dcnt[q][i]))
        self._deps(q, reads, writes)
        self.dcnt[q][i] += 16
        ev = (sem, self.dcnt[q][i], "dma")
        self.lists[q].append(("dma", fn, sem))
        self._mark(ev, reads, writes)
        if is_output:
            self.out_events.append(ev)
        return ev

    def barrier(self):
        evs = [(self.sem[e], self.cnt[e], e) for e in self.sem if self.cnt[e] > 0]
        for q in self.dsem:
            for sem, c in zip(self.dsem[q], self.dcnt[q]):
                if c > 0:
                    evs.append((sem, c, "dma"))
        for eng in self.ENG:
            for ev in evs:
                self._wait(eng, ev)

    def finish(self):
        for ev in self.out_events:
            self._wait("sp", ev)

    def emit(self, block):
        def run(e, lst):
            for it in lst:
                if it[0] == "wait":
                    e.wait_ge(it[1], it[2])
                elif it[0] == "op":
                    ins = it[1](e)
                    if it[2] is not None:
                        ins.then_inc(it[2], 1)
                else:
                    it[1](e).then_inc(it[2], 16)
        L = self.lists

        @block.sync
        def _(e):
            run(e, L["sp"])

        @block.tensor
        def _(e):
            run(e, L["pe"])

        @block.scalar
        def _(e):
            run(e, L["act"])

        @block.vector
        def _(e):
            run(e, L["dve"])

        @block.gpsimd
        def _(e):
            run(e, L["pool"])


def build(T=4096, depth=2, flags=("sb", "sc", "gdn")):
    nc = bass.Bass("TRN2", target_bir_lowering=False)
    depth_ = depth
    depth = max(depth, 1)
    NTT = T // 512
    NTB = T // 128
    G = min(T, 2048)
    NG = T // G
    GT = G // 512

    def din(name, shape):
        return nc.dram_tensor(name, shape, F32, kind="ExternalInput").ap()

    x_in = din("x", [T, D])
    norm1_w = din("norm1_w", [depth, D])
    w_in = din("w_in", [depth, D, IN_DIM])
    sc_conv_w = din("sc_conv_w", [depth, 3, 512])
    gdn_conv_w = din("gdn_conv_w", [depth, 4, 2304])
    gdn_a_log = din("gdn_a_log", [depth, 6])
    gdn_dt_bias = din("gdn_dt_bias", [depth, 6])
    gdn_norm_w = din("gdn_norm_w", [depth, 128])
    w_out = din("w_out", [depth, D, D])
    norm2_w = din("norm2_w", [depth, D])
    w_up = din("w_up", [depth, D, DFF])
    w_down = din("w_down", [depth, DFF, D])
    final_norm_w = din("final_norm_w", [D])
    out = nc.dram_tensor("out", [T, D], F32, kind="ExternalOutput").ap()

    def scratch(name, shape, dt):
        return nc.dram_tensor(name, shape, dt, kind="Internal").ap()

    xT = scratch("xT", [D, T], F32)
    qkT = scratch("qkT", [1536, T], BF16)
    vtok = scratch("vtok", [T, 768], BF16)
    pfT = scratch("pfT", [4608, T], F32)
    abtok = scratch("abtok", [T, 12], F32)
    yT = scratch("yT", [D, T], BF16)
    winF = scratch("winF", [depth, 12, 128, KT, 512], BF16)
    winV = scratch("winV", [depth, 128, KT, 768], BF16)
    winAB = scratch("winAB", [depth, 128, KT, 12], BF16)
    woutF = scratch("woutF", [depth, 4, 128, KT, 512], BF16)
    wupF = scratch("wupF", [depth, 16, 128, KT, 512], BF16)
    wdnF = scratch("wdnF", [depth, 4, 4, 128, KT, 512], BF16)

    with contextlib.ExitStack() as gst:
        P = Prog(nc, gst)

        uniq = [0]

        def sb(name, shape, dt, st=gst):
            uniq[0] += 1
            return st.enter_context(nc.sbuf_tensor(f"{name}_{uniq[0]}", shape, dt))

        ps = [gst.enter_context(nc.psum_tensor(f"ps{i}", [128, 512], F32)) for i in range(8)]
        psb = [Buf(f"ps{i}", psum=True) for i in range(8)]

        ident = sb("ident", [128, 128], F32)
        ones_bf = sb("ones_bf", [128, 128], BF16)
        Bc = Buf("consts")

        P.op("pool", lambda e: e.memset(ident[:], 0.0), writes=[Bc])
        P.op("pool", lambda e: e.memset(ones_bf[:], 1.0), writes=[Bc])
        P.op("pool", lambda e: e.affine_select(out=ident[:], in_=ident[:], pattern=[[-1, 128]], compare_op=ALU.not_equal,
                                                fill=1.0, base=0, channel_multiplier=1), reads=[Bc], writes=[Bc])

        nw_all = sb("nw_all", [128, 2 * depth + 1, KT], F32)
        Bnw = Buf("nw")
        for i in range(2 * depth + 1):
            if i == 2 * depth:
                src = final_norm_w
            elif i % 2 == 0:
                src = norm1_w[i // 2]
            else:
                src = norm2_w[i // 2]
            P.dma("sp", lambda e, i=i, src=src: e.dma_start(out=nw_all[:, i, :], in_=src.rearrange("(kt p) -> p kt", p=128),
                                                             allow_slow_non_contiguous=True), writes=[Bnw])

        WB = {}

        def conv_dma(key, out_ap, in_ap):
            b = WB.setdefault(key, Buf(str(key)))
            P.dma("pool", lambda e: e.dma_start(out=out_ap, in_=in_ap), writes=[b])

        FCOLS = [0, 512, 1024] + [2304 + 512 * i for i in range(9)]
        for l in range(depth_):
            wl = w_in[l].rearrange("(kt p) c -> p kt c", p=128)
            for j, c0 in enumerate(FCOLS):
                for hh in range(2):
                    conv_dma(("win", l, j), winF[l, j, :, hh * 8:(hh + 1) * 8, :], wl[:, hh * 8:(hh + 1) * 8, c0:c0 + 512])
            for hh in range(2):
                conv_dma(("winV", l), winV[l, :, hh * 8:(hh + 1) * 8, :], wl[:, hh * 8:(hh + 1) * 8, 1536:2304])
            conv_dma(("winAB", l), winAB[l], wl[:, :, 6912:6924])
            wl = w_out[l].rearrange("(kt p) c -> p kt c", p=128)
            for j in range(4):
                for hh in range(2):
                    conv_dma(("wout", l, j), woutF[l, j, :, hh * 8:(hh + 1) * 8, :], wl[:, hh * 8:(hh + 1) * 8, j * 512:(j + 1) * 512])
            wl = w_up[l].rearrange("(kt p) c -> p kt c", p=128)
            for j in range(16):
                for hh in range(2):
                    conv_dma(("wup", l, j), wupF[l, j, :, hh * 8:(hh + 1) * 8, :], wl[:, hh * 8:(hh + 1) * 8, j * 512:(j + 1) * 512])
            wl = w_down[l].rearrange("(ft p) c -> p ft c", p=128)
            for j in range(4):
                for fq in range(4):
                    conv_dma(("wdn", l, j, fq), wdnF[l, j, fq], wl[:, fq * 16:(fq + 1) * 16, j * 512:(j + 1) * 512])

        BxT = [Buf(f"xT{tt}") for tt in range(NTT)]
        Bqk = Buf("qkT")
        Bv = Buf("vtok")
        Bpf = Buf("pfT")
        Bab = Buf("abtok")
        ByT = Buf("yT")

        xT_v = xT.rearrange("(kt p) t -> p kt t", p=128)
        yT_v = yT.rearrange("(kt p) t -> p kt t", p=128)

        class Rot:
            def __init__(self, tiles):
                self.t = tiles
                self.b = [Buf() for _ in tiles]
                self.i = 0

            def next(self):
                i = self.i
                self.i = (i + 1) % len(self.t)
                return self.t[i], self.b[i]

        class Pref:
            def __init__(self, rot, loaders, depth=2):
                self.rot, self.loaders, self.q, self.i = rot, loaders, [], 0
                assert len(rot.t) > depth
                for _ in range(depth):
                    self._issue()

            def _issue(self):
                if self.i < len(self.loaders):
                    wt, Bwt = self.rot.next()
                    self.loaders[self.i](wt, Bwt)
                    self.q.append((wt, Bwt))
                    self.i += 1

            def get(self):
                cur = self.q.pop(0)
                self._issue()
                return cur

        psrot = Rot(ps[0:4])
        psrot.b = psb[0:4]
        evac_flip = [0]

        def evac(out_ap, in_ap, reads, writes):
            evac_flip[0] ^= 1
            if evac_flip[0]:
                P.op("act", lambda e: e.activation(out_ap, in_ap, AF.Copy), reads=reads, writes=writes)
            else:
                P.op("dve", lambda e: e.tensor_copy(out_ap, in_ap), reads=reads, writes=writes)

        def mm(out_ap, lhsT, rhs, start, stop, reads, writes, inc):
            P.op("pe", lambda e: e.matmul(out_ap, lhsT, rhs, start=start, stop=stop), reads=reads, writes=writes, inc=inc)

        def rmsnorm_tile(st_, xs, Bxs, sq_rot, rr, Brr, widx, tt, hT, BhT, hcol0, ps_ss, Bps):
            P.dma("sp", lambda e: e.dma_start(out=xs[:], in_=xT_v[:, :, tt * 512:(tt + 1) * 512]), reads=[BxT[tt]], writes=[Bxs])
            for kt in range(KT):
                sq, Bsq = sq_rot.next()
                P.op("act", lambda e, sq=sq, kt=kt: e.activation(sq[:], xs[:, kt, :], AF.Square), reads=[Bxs], writes=[Bsq])
                mm(ps_ss[:], ones_bf[:], sq[:], kt == 0, kt == KT - 1, [Bsq, Bc], [Bps], True)
            P.op("act", lambda e: e.activation(rr[:], ps_ss[:], AF.Sqrt, bias=RMS_EPS, scale=1.0 / D), reads=[Bps], writes=[Brr])
            P.op("dve", lambda e: e.reciprocal(rr[:], rr[:]), reads=[Brr], writes=[Brr])
            for kt in range(KT):
                P.op("dve", lambda e, kt=kt: e.scalar_tensor_tensor(hT[:, kt, hcol0:hcol0 + 512], xs[:, kt, :], nw_all[:, widx, kt:kt + 1],
                                                                     rr[:], ALU.mult, ALU.mult),
                     reads=[Bxs, Brr, Bnw], writes=[BhT[kt]])

        with Phase(P) as st:
            xin = Rot([sb(f"xin{i}", [128, D], F32, st) for i in range(2)])
            xst = Rot([sb(f"xst{i}", [128, KT, 128], F32, st) for i in range(2)])
            for tb in range(NTB):
                xi, Bxi = xin.next()
                P.dma("sp", lambda e, xi=xi, tb=tb: e.dma_start(out=xi[:], in_=x_in[tb * 128:(tb + 1) * 128, :]), writes=[Bxi])
                xs_, Bxs_ = xst.next()
                for g in range(4):
                    pt, Bpt = psrot.next()
                    for j in range(4):
                        P.op("pe", lambda e, pt=pt, xi=xi, g=g, j=j: e.transpose(pt[:, j * 128:(j + 1) * 128], xi[:, (4 * g + j) * 128:(4 * g + j + 1) * 128], ident[:]),
                             reads=[Bxi, Bc], writes=[Bpt], inc=(j == 3))
                    evac(xs_[:, 4 * g:4 * g + 4, :], pt[:].rearrange("p (j t) -> p j t", j=4), [Bpt], [Bxs_])
                P.dma("sp", lambda e, xs_=xs_, tb=tb: e.dma_start(out=xT_v[:, :, tb * 128:(tb + 1) * 128], in_=xs_[:]),
                      reads=[Bxs_], writes=[BxT[tb // 4]])

        def _layer(l):
            with Phase(P) as st:
                hT = sb("hT_a", [128, KT, G], BF16, st)
                BhT = [Buf(f"hT{k}") for k in range(KT)]
                xs = sb("xs_a", [128, KT, 512], F32, st)
                Bxs = Buf("xs")
                sq_rot = Rot([sb(f"sq_a{i}", [128, 512], BF16, st) for i in range(3)])
                rr = sb("rr_a", [128, 512], F32, st)
                Brr = Buf("rr")
                wts = Rot([sb(f"wt_a{i}", [128, KT, 512], BF16, st) for i in range(3)])
                wab = sb("wab_a", [128, KT, 12], BF16, st)
                Bwab = Buf("wab")
                stg32 = Rot([sb(f"stg32_a{i}", [128, 512], F32, st) for i in range(3)])
                stg16 = Rot([sb(f"stg16_a{i}", [128, 512], BF16, st) for i in range(3)])
                ldA = []
                for g in range(NG):
                    for j in range(12):
                        ldA.append(lambda wt, Bwt, l=l, j=j: P.dma("sp", lambda e: e.dma_start(out=wt[:], in_=winF[l, j]), reads=[WB[("win", l, j)]], writes=[Bwt]))
                    for (c0, n) in ((0, 512), (512, 256)):
                        ldA.append(lambda wt, Bwt, l=l, c0=c0, n=n: P.dma("sp", lambda e: e.dma_start(out=wt[:, :, 0:n], in_=winV[l, :, :, c0:c0 + n]),
                                                                         reads=[WB[("winV", l)]], writes=[Bwt]))
                prefA = Pref(wts, ldA)
                for g in range(NG):
                    for t4 in range(GT):
                        rmsnorm_tile(st, xs, Bxs, sq_rot, rr, Brr, 2 * l, g * GT + t4, hT, BhT, t4 * 512, ps[7], psb[7])
                    for j in range(12):
                        wt, Bwt = prefA.get()
                        for ci in range(4):
                            for t4 in range(GT):
                                pt, Bpt = psrot.next()
                                for kt in range(KT):
                                    mm(pt[:], wt[:, kt, ci * 128:(ci + 1) * 128], hT[:, kt, t4 * 512:(t4 + 1) * 512], kt == 0, kt == KT - 1,
                                       [Bwt, BhT[kt]], [Bpt], kt == KT - 1)
                                tok0 = g * G + t4 * 512
                                if j < 3:
                                    sg, Bsg = stg16.next()
                                    evac(sg[:], pt[:], [Bpt], [Bsg])
                                    r0 = j * 512 + ci * 128
                                    P.dma("sp", lambda e, sg=sg, r0=r0, tok0=tok0: e.dma_start(out=qkT[r0:r0 + 128, tok0:tok0 + 512], in_=sg[:]),
                                          reads=[Bsg], writes=[Bqk])
                                else:
                                    sg, Bsg = stg32.next()
                                    evac(sg[:], pt[:], [Bpt], [Bsg])
                                    r0 = (j - 3) * 512 + ci * 128
                                    P.dma("sp", lambda e, sg=sg, r0=r0, tok0=tok0: e.dma_start(out=pfT[r0:r0 + 128, tok0:tok0 + 512], in_=sg[:]),
                                          reads=[Bsg], writes=[Bpf])
                    for (c0, n) in ((0, 512), (512, 256)):
                        wt, Bwt = prefA.get()
                        for tb in range(G // 128):
                            pt, Bpt = psrot.next()
                            for kt in range(KT):
                                mm(pt[:, 0:n], hT[:, kt, tb * 128:(tb + 1) * 128], wt[:, kt, 0:n], kt == 0, kt == KT - 1, [Bwt, BhT[kt]], [Bpt], kt == KT - 1)
                            sg, Bsg = stg16.next()
                            evac(sg[:, 0:n], pt[:, 0:n], [Bpt], [Bsg])
                            tok0 = g * G + tb * 128
                            P.dma("sp", lambda e, sg=sg, c0=c0, n=n, tok0=tok0: e.dma_start(out=vtok[tok0:tok0 + 128, c0:c0 + n], in_=sg[:, 0:n]),
                                  reads=[Bsg], writes=[Bv])
                    P.dma("sp", lambda e, l=l: e.dma_start(out=wab[:], in_=winAB[l]), reads=[WB[("winAB", l)]], writes=[Bwab])
                    for tb in range(G // 128):
                        pt, Bpt = psrot.next()
                        for kt in range(KT):
                            mm(pt[:, 0:12], hT[:, kt, tb * 128:(tb + 1) * 128], wab[:, kt, :], kt == 0, kt == KT - 1, [Bwab, BhT[kt]], [Bpt], kt == KT - 1)
                        sg, Bsg = stg32.next()
                        evac(sg[:, 0:12], pt[:, 0:12], [Bpt], [Bsg])
                        tok0 = g * G + tb * 128
                        P.dma("sp", lambda e, sg=sg, tok0=tok0: e.dma_start(out=abtok[tok0:tok0 + 128, :], in_=sg[:, 0:12]),
                              reads=[Bsg], writes=[Bab])

            if "sb" in flags:
                phase_sb(nc, P, l, T, qkT, vtok, yT, Bqk, Bv, ByT, ps, psb, sb)
            else:
                zero_rows(nc, P, yT, ByT, 0, 768, T, sb)
            if "sc" in flags:
                phase_sc(nc, P, l, T, pfT, yT, sc_conv_w, Bpf, ByT, sb)
            else:
                zero_rows(nc, P, yT, ByT, 768, 512, T, sb)
            if "gdn" in flags:
                phase_gdn(nc, P, l, T, pfT, abtok, yT, gdn_conv_w, gdn_a_log, gdn_dt_bias, gdn_norm_w, Bpf, Bab, ByT, ps, psb, sb, ident, ones_bf, Bc)
            else:
                zero_rows(nc, P, yT, ByT, 1280, 768, T, sb)

            with Phase(P) as st:
                yS = sb("yS_e", [128, KT, G], BF16, st)
                ByS = Buf("yS")
                wts = Rot([sb(f"wt_e{i}", [128, KT, 512], BF16, st) for i in range(3)])
                xold = Rot([sb(f"xold_e{i}", [128, 512], F32, st) for i in range(3)])
                stg = Rot([sb(f"stg_e{i}", [128, 512], F32, st) for i in range(3)])
                ldE = []
                for g in range(NG):
                    for j in range(4):
                        ldE.append(lambda wt, Bwt, l=l, j=j: P.dma("sp", lambda e: e.dma_start(out=wt[:], in_=woutF[l, j]), reads=[WB[("wout", l, j)]], writes=[Bwt]))
                prefE = Pref(wts, ldE)
                for g in range(NG):
                    P.dma("sp", lambda e, g=g: e.dma_start(out=yS[:], in_=yT_v[:, :, g * G:(g + 1) * G]), reads=[ByT], writes=[ByS])
                    for j in range(4):
                        wt, Bwt = prefE.get()
                        for ci in range(4):
                            dm = 4 * j + ci
                            for t4 in range(GT):
                                tt = g * GT + t4
                                xo, Bxo = xold.next()
                                P.dma("sp", lambda e, xo=xo, dm=dm, tt=tt: e.dma_start(out=xo[:], in_=xT[dm * 128:(dm + 1) * 128, tt * 512:(tt + 1) * 512]),
                                      reads=[BxT[tt]], writes=[Bxo])
                                pt, Bpt = psrot.next()
                                for kt in range(KT):
                                    mm(pt[:], wt[:, kt, ci * 128:(ci + 1) * 128], yS[:, kt, t4 * 512:(t4 + 1) * 512], kt == 0, kt == KT - 1,
                                       [Bwt, ByS], [Bpt], kt == KT - 1)
                                sg, Bsg = stg.next()
                                P.op("dve", lambda e, sg=sg, pt=pt, xo=xo: e.tensor_tensor(sg[:], pt[:], xo[:], ALU.add), reads=[Bpt, Bxo], writes=[Bsg])
                                P.dma("sp", lambda e, sg=sg, dm=dm, tt=tt: e.dma_start(out=xT[dm * 128:(dm + 1) * 128, tt * 512:(tt + 1) * 512], in_=sg[:]),
                                      reads=[Bsg], writes=[BxT[tt]])

            with Phase(P) as st:
                hT = sb("hT_f", [128, KT, 512], BF16, st)
                BhT = [Buf(f"hT{k}") for k in range(KT)]
                aT = sb("aT_f", [128, 64, 512], BF16, st)
                BaT = Buf("aT")
                xs = sb("xs_f", [128, KT, 512], F32, st)
                Bxs = Buf("xs")
                sq_rot = Rot([sb(f"sq_f{i}", [128, 512], BF16, st) for i in range(3)])
                rr = sb("rr_f", [128, 512], F32, st)
                Brr = Buf("rr")
                wts = Rot([sb(f"wt_f{i}", [128, KT, 512], BF16, st) for i in range(3)])
                rl = Rot([sb(f"rl_f{i}", [128, 512], F32, st) for i in range(3)])
                xold = Rot([sb(f"xold_f{i}", [128, 512], F32, st) for i in range(4)])
                stg = Rot([sb(f"stg_f{i}", [128, 512], F32, st) for i in range(4)])
                ldF = []
                for tt in range(NTT):
                    for j in range(16):
                        ldF.append(lambda wt, Bwt, l=l, j=j: P.dma("sp", lambda e: e.dma_start(out=wt[:], in_=wupF[l, j]), reads=[WB[("wup", l, j)]], writes=[Bwt]))
                    for j in range(4):
                        for fq in range(4):
                            ldF.append(lambda wt, Bwt, l=l, j=j, fq=fq: P.dma("sp", lambda e: e.dma_start(out=wt[:], in_=wdnF[l, j, fq]), reads=[WB[("wdn", l, j, fq)]], writes=[Bwt]))
                prefF = Pref(wts, ldF)
                rmsnorm_tile(st, xs, Bxs, sq_rot, rr, Brr, 2 * l + 1, 0, hT, BhT, 0, ps[7], psb[7])
                for tt in range(NTT):
                    for j in range(16):
                        wt, Bwt = prefF.get()
                        for ci in range(4):
                            pt, Bpt = psrot.next()
                            for kt in range(KT):
                                mm(pt[:], wt[:, kt, ci * 128:(ci + 1) * 128], hT[:, kt, :], kt == 0, kt == KT - 1, [Bwt, BhT[kt]], [Bpt], kt == KT - 1)
                            r_, Br_ = rl.next()
                            P.op("act", lambda e, r_=r_, pt=pt: e.activation(r_[:], pt[:], AF.Relu), reads=[Bpt], writes=[Br_])
                            P.op("dve", lambda e, r_=r_, f=4 * j + ci: e.tensor_tensor(aT[:, f, :], r_[:], r_[:], ALU.mult), reads=[Br_], writes=[BaT])
                    if tt + 1 < NTT:
                        rmsnorm_tile(st, xs, Bxs, sq_rot, rr, Brr, 2 * l + 1, tt + 1, hT, BhT, 0, ps[7], psb[7])
                    for j in range(4):
                        for fq in range(4):
                            wt, Bwt = prefF.get()
                            for ft in range(16):
                                f = fq * 16 + ft
                                for ci in range(4):
                                    mm(ps[ci][:], wt[:, ft, ci * 128:(ci + 1) * 128], aT[:, f, :], f == 0, f == 63, [Bwt, BaT], [psb[ci]], f == 63 or ft == 15)
                        for ci in range(4):
                            dm = 4 * j + ci
                            xo, Bxo = xold.next()
                            P.dma("sp", lambda e, xo=xo, dm=dm, tt=tt: e.dma_start(out=xo[:], in_=xT[dm * 128:(dm + 1) * 128, tt * 512:(tt + 1) * 512]),
                                  reads=[BxT[tt]], writes=[Bxo])
                            sg, Bsg = stg.next()
                            P.op("dve", lambda e, sg=sg, ci=ci, xo=xo: e.tensor_tensor(sg[:], ps[ci][:], xo[:], ALU.add), reads=[psb[ci], Bxo], writes=[Bsg])
                            P.dma("sp", lambda e, sg=sg, dm=dm, tt=tt: e.dma_start(out=xT[dm * 128:(dm + 1) * 128, tt * 512:(tt + 1) * 512], in_=sg[:]),
                                  reads=[Bsg], writes=[BxT[tt]])

        for l in range(depth_):
            _layer(l)

        with Phase(P) as st:
            hF = sb("hF_o", [128, KT, 512], F32, st)
            BhF = [Buf(f"hF{k}") for k in range(KT)]
            xs = sb("xs_o", [128, KT, 512], F32, st)
            Bxs = Buf("xs")
            sq_rot = Rot([sb(f"sq_o{i}", [128, 512], BF16, st) for i in range(3)])
            rr = sb("rr_o", [128, 512], F32, st)
            Brr = Buf("rr")
            ost = Rot([sb(f"ost{i}", [128, D], F32, st) for i in range(2)])
            for tt in range(NTT):
                rmsnorm_tile(st, xs, Bxs, sq_rot, rr, Brr, 2 * depth, tt, hF, BhF, 0, ps[7], psb[7])
                for s4 in range(4):
                    os_, Bos = ost.next()
                    for g in range(4):
                        pt, Bpt = psrot.next()
                        for j in range(4):
                            P.op("pe", lambda e, pt=pt, g=g, j=j, s4=s4: e.transpose(pt[:, j * 128:(j + 1) * 128], hF[:, 4 * g + j, s4 * 128:(s4 + 1) * 128], ident[:]),
                                 reads=[BhF[4 * g + j], Bc], writes=[Bpt], inc=(j == 3))
                        evac(os_[:, g * 512:(g + 1) * 512], pt[:], [Bpt], [Bos])
                    tok0 = tt * 512 + s4 * 128
                    P.dma("sp", lambda e, os_=os_, tok0=tok0: e.dma_start(out=out[tok0:tok0 + 128, :], in_=os_[:]), reads=[Bos], is_output=True)

        P.finish()
        with nc.Block() as block:
            P.emit(block)
    return nc


def zero_rows(nc, P, yT, ByT, r0, nrows, T, sb):
    with Phase(P) as st:
        z = sb(f"zr{r0}", [128, T], BF16, st)
        Bz = Buf("z")
        P.op("pool", lambda e: e.memset(z[:], 0.0), writes=[Bz])
        for r in range(r0, r0 + nrows, 128):
            P.dma("sp", lambda e, r=r: e.dma_start(out=yT[r:r + 128, :], in_=z[:]), reads=[Bz], writes=[ByT])
        for ev in list(ByT.r) + ([ByT.w] if ByT.w else []):
            pass


def phase_sc(nc, P, l, T, pfT, yT, sc_conv_w, Bpf, ByT, sb):
    with Phase(P) as st:
        cw = sb("cw_sc", [128, 4, 3], F32, st)
        Bcw = Buf("cw")
        for ci in range(4):
            for k in range(3):
                P.dma("sp", lambda e, ci=ci, k=k: e.dma_start(out=cw[:, ci, k:k + 1], in_=sc_conv_w[l, k, ci * 128:(ci + 1) * 128].rearrange("(p o) -> p o", o=1)), writes=[Bcw])
        bt = sb("b_sc", [128, T], F32, st)
        ct = sb("c_sc", [128, T], F32, st)
        ht = sb("h_sc", [128, T], F32, st)
        acc = sb("acc_sc", [128, T], F32, st)
        yo = sb("yo_sc", [128, T], BF16, st)
        Bb, Bc_, Bh, Ba, By = Buf(), Buf(), Buf(), Buf(), Buf()
        for ci in range(4):
            P.dma("sp", lambda e, ci=ci: e.dma_start(out=bt[:], in_=pfT[ci * 128:(ci + 1) * 128, :]), reads=[Bpf], writes=[Bb])
            P.dma("sp", lambda e, ci=ci: e.dma_start(out=ct[:], in_=pfT[512 + ci * 128:512 + (ci + 1) * 128, :]), reads=[Bpf], writes=[Bc_])
            P.dma("sp", lambda e, ci=ci: e.dma_start(out=ht[:], in_=pfT[1024 + ci * 128:1024 + (ci + 1) * 128, :]), reads=[Bpf], writes=[Bh])
            P.op("dve", lambda e: e.tensor_tensor(ct[:], ct[:], ht[:], ALU.mult), reads=[Bc_, Bh], writes=[Bc_])
            P.op("dve", lambda e, ci=ci: e.tensor_scalar(acc[:], ct[:], cw[:, ci, 2:3], None, ALU.mult), reads=[Bc_, Bcw], writes=[Ba])
            P.op("dve", lambda e, ci=ci: e.scalar_tensor_tensor(acc[:, 1:T], ct[:, 0:T - 1], cw[:, ci, 1:2], acc[:, 1:T], ALU.mult, ALU.add), reads=[Bc_, Bcw, Ba], writes=[Ba])
            P.op("dve", lambda e, ci=ci: e.scalar_tensor_tensor(acc[:, 2:T], ct[:, 0:T - 2], cw[:, ci, 0:1], acc[:, 2:T], ALU.mult, ALU.add), reads=[Bc_, Bcw, Ba], writes=[Ba])
            P.op("dve", lambda e: e.tensor_tensor(yo[:], acc[:], bt[:], ALU.mult), reads=[Ba, Bb], writes=[By])
            P.dma("sp", lambda e, ci=ci: e.dma_start(out=yT[768 + ci * 128:768 + (ci + 1) * 128, :], in_=yo[:]), reads=[By], writes=[ByT])


def phase_sb(nc, P, l, T, qkT, vtok, yT, Bqk, Bv, ByT, ps, psb, sb):
    NQT = T // 512
    NTB = T // 128
    scale = 128 ** -0.5
    with Phase(P) as st:
        Bk = Buf("sbconst")
        tmp = sb("sbtmp", [128, 4, 512], F32, st)
        uinc = sb("uinc", [128, 128], BF16, st)
        remm = sb("remm", [128, 128], BF16, st)
        masks = sb("masks", [128, 4, 512], BF16, st)
        P.op("pool", lambda e: e.memset(tmp[:, 0, 0:128], -1.0), writes=[Bk])
        P.op("pool", lambda e: e.affine_select(out=tmp[:, 0, 0:128], in_=tmp[:, 0, 0:128], pattern=[[-1, 128]], compare_op=ALU.is_ge, fill=0.0,
                                                base=0, channel_multiplier=1), reads=[Bk], writes=[Bk])
        P.op("pool", lambda e: e.tensor_copy(uinc[:], tmp[:, 0, 0:128]), reads=[Bk], writes=[Bk])
        P.op("pool", lambda e: e.memset(tmp[:, 0, 0:128], -1.0), reads=[Bk], writes=[Bk])
        P.op("pool", lambda e: e.affine_select(out=tmp[:, 0, 0:128], in_=tmp[:, 0, 0:128], pattern=[[1, 128]], compare_op=ALU.is_gt, fill=0.0,
                                                base=0, channel_multiplier=-1), reads=[Bk], writes=[Bk])
        P.op("pool", lambda e: e.tensor_copy(remm[:], tmp[:, 0, 0:128]), reads=[Bk], writes=[Bk])
        P.op("pool", lambda e: e.memset(tmp[:], 1.0), reads=[Bk], writes=[Bk])
        P.op("pool", lambda e: e.affine_select(out=tmp[:], in_=tmp[:], pattern=[[-128, 4], [1, 512]], compare_op=ALU.is_gt, fill=0.0,
                                                base=0, channel_multiplier=-1), reads=[Bk], writes=[Bk])
        P.op("pool", lambda e: e.tensor_copy(masks[:], tmp[:]), reads=[Bk], writes=[Bk])

        QT = [sb(f"QT{i}", [128, T], BF16, st) for i in range(2)]
        KTt = [sb(f"KTt{i}", [128, T], BF16, st) for i in range(2)]
        VV = [sb(f"VV{i}", [128, NTB, 128], BF16, st) for i in range(2)]
        BQ = [Buf(), Buf()]
        BK_ = [Buf(), Buf()]
        BV_ = [Buf(), Buf()]
        Et = [sb(f"E{i}", [128, 512], F32, st) for i in range(3)]
        SPt = [sb(f"SP{i}", [128, 512], BF16, st) for i in range(3)]
        Xt = [sb(f"X{i}", [128, 512], F32, st) for i in range(2)]
        Wt = [sb(f"W{i}", [128, 512], BF16, st) for i in range(3)]
        Ot = [sb(f"Osb{i}", [128, 512], BF16, st) for i in range(2)]
        BE = [Buf() for _ in range(3)]
        BSP = [Buf() for _ in range(3)]
        BX = [Buf() for _ in range(2)]
        BW = [Buf() for _ in range(3)]
        BOt = [Buf() for _ in range(2)]

        pairs = []
        g = 0
        for h in range(6):
            for qt in range(NQT):
                kbs = list(range(4 * qt + 3, -1, -1))
                for n, kb in enumerate(kbs):
                    pairs.append((h, qt, kb, n == 0, n == len(kbs) - 1, g))
                g += 1
        loaded = set()

        def load_head(h):
            if h in loaded or h >= 6:
                return
            loaded.add(h)
            b = h % 2
            P.dma("sp", lambda e: e.dma_start(out=QT[b][:], in_=qkT[h * 128:(h + 1) * 128, :]), reads=[Bqk], writes=[BQ[b]])
            P.dma("sp", lambda e: e.dma_start(out=KTt[b][:], in_=qkT[768 + h * 128:768 + (h + 1) * 128, :]), reads=[Bqk], writes=[BK_[b]])
            P.dma("sp", lambda e: e.dma_start(out=VV[b][:], in_=vtok[:, h * 128:(h + 1) * 128].rearrange("(blk p) d -> p blk d", p=128)),
                  reads=[Bv], writes=[BV_[b]])

        def stage1z(i):
            h, qt, kb, first, last, g = pairs[i]
            load_head(h)
            b = h % 2
            z, Bz = ps[i % 2], psb[i % 2]
            q0 = qt * 512
            P.op("pe", lambda e: e.matmul(z[:], KTt[b][:, kb * 128:(kb + 1) * 128], QT[b][:, q0:q0 + 512], start=True, stop=True),
                 reads=[BK_[b], BQ[b]], writes=[Bz])

        def stage1a(i):
            h, qt, kb, first, last, g = pairs[i]
            z, Bz = ps[i % 2], psb[i % 2]
            E, SPb = Et[i % 3], SPt[i % 3]
            P.op("act", lambda e: e.activation(E[:], z[:], AF.Exp, scale=scale), reads=[Bz], writes=[BE[i % 3]])
            P.op("act", lambda e: e.activation(SPb[:], E[:], AF.Ln, bias=1.0), reads=[BE[i % 3]], writes=[BSP[i % 3]])
            r = kb - 4 * qt
            if r >= 0:
                P.op("dve", lambda e: e.tensor_tensor(SPb[:], SPb[:], masks[:, r, :], ALU.mult), reads=[BSP[i % 3], Bk], writes=[BSP[i % 3]])
                P.op("pool", lambda e: e.tensor_tensor(E[:], E[:], masks[:, r, :], ALU.mult), reads=[BE[i % 3], Bk], writes=[BE[i % 3]])

        def stage2a(i):
            h, qt, kb, first, last, g = pairs[i]
            C, BC = ps[2 + g % 2], psb[2 + g % 2]
            SPb = SPt[i % 3]
            P.op("pe", lambda e: e.matmul(C[:], uinc[:], SPb[:], start=first, stop=True, skip_group_check=True), reads=[BSP[i % 3], Bk], writes=[BC])

        def stage2b(i):
            h, qt, kb, first, last, g = pairs[i]
            C, BC = ps[2 + g % 2], psb[2 + g % 2]
            E, SPb = Et[i % 3], SPt[i % 3]
            X, W = Xt[i % 2], Wt[i % 3]
            P.op("act", lambda e: e.activation(X[:], C[:], AF.Exp), reads=[BC], writes=[BX[i % 2]])
            P.op("pe", lambda e: e.matmul(C[:], remm[:], SPb[:], start=False, stop=True, skip_group_check=True), reads=[BSP[i % 3], Bk], writes=[BC])
            P.op("dve", lambda e: e.tensor_tensor(W[:], X[:], E[:], ALU.mult), reads=[BX[i % 2], BE[i % 3]], writes=[BW[i % 3]])

        def stage3(i):
            h, qt, kb, first, last, g = pairs[i]
            b = h % 2
            O, BO = ps[4 + g % 2], psb[4 + g % 2]
            W = Wt[i % 3]
            P.op("pe", lambda e: e.matmul(O[:], VV[b][:, kb, :], W[:], start=first, stop=last), reads=[BW[i % 3], BV_[b]], writes=[BO])
            if last:
                o, Bo = Ot[g % 2], BOt[g % 2]
                P.op("act", lambda e: e.activation(o[:], O[:], AF.Copy), reads=[BO], writes=[Bo])
                q0 = qt * 512
                P.dma("sp", lambda e: e.dma_start(out=yT[h * 128:(h + 1) * 128, q0:q0 + 512], in_=o[:]), reads=[Bo], writes=[ByT])
                if qt == NQT - 1:
                    load_head(h + 2) if (h + 2) % 2 == h % 2 else None

        load_head(0)
        load_head(1)
        n = len(pairs)
        stage1z(0)
        for s_ in range(n + 2):
            if s_ < n:
                stage1a(s_)
            if 0 <= s_ - 1 < n:
                stage2a(s_ - 1)
            if s_ + 1 < n:
                stage1z(s_ + 1)
            if 0 <= s_ - 1 < n:
                stage2b(s_ - 1)
            if 0 <= s_ - 2 < n:
                stage3(s_ - 2)


def phase_gdn(nc, P, l, T, pfT, abtok, yT, gdn_conv_w, gdn_a_log, gdn_dt_bias, gdn_norm_w, Bpf, Bab, ByT, ps, psb, sb, ident, ones_bf, Bc):
    NB = T // 128
    NTT = T // 512
    with Phase(P) as st:
        def op(eng, fn, reads=(), writes=()):
            P.op(eng, fn, reads=list(reads), writes=list(writes))

        def t(name, shape, dt):
            return sb("g_" + name, shape, dt, st), Buf(name)
        Bk = Buf("gconst")
        tri, _ = t("tri", [128, 128], F32)
        trib, _ = t("trib", [128, 128], BF16)
        negL, _ = t("negL", [128, 128], F32)
        blk1, _ = t("blk1", [128, 128], F32)
        half0, _ = t("half0", [128, 128], F32)
        half1, _ = t("half1", [128, 128], F32)
        onesf, _ = t("onesf", [128, 128], F32)
        identb, _ = t("identb", [128, 128], BF16)
        op("pool", lambda e: e.memset(tri[:], 1.0), [], [Bk])
        op("pool", lambda e: e.affine_select(out=tri[:], in_=tri[:], pattern=[[1, 128]], compare_op=ALU.is_ge, fill=0.0, base=0, channel_multiplier=-1), [Bk], [Bk])
        op("pool", lambda e: e.memset(tri[0:64, 64:128], 0.0), [Bk], [Bk])
        op("pool", lambda e: e.tensor_copy(trib[:], tri[:]), [Bk], [Bk])
        op("pool", lambda e: e.memset(negL[:], -1.0), [Bk], [Bk])
        op("pool", lambda e: e.affine_select(out=negL[:], in_=negL[:], pattern=[[-1, 128]], compare_op=ALU.is_gt, fill=0.0, base=0, channel_multiplier=1), [Bk], [Bk])
        op("pool", lambda e: e.memset(negL[64:128, 0:64], 0.0), [Bk], [Bk])
        op("pool", lambda e: e.memset(blk1[:], 0.0), [Bk], [Bk])
        op("pool", lambda e: e.memset(blk1[0:64, 0:64], 1.0), [Bk], [Bk])
        op("pool", lambda e: e.memset(blk1[64:128, 64:128], 1.0), [Bk], [Bk])
        op("pool", lambda e: e.memset(half0[:], 0.0), [Bk], [Bk])
        op("pool", lambda e: e.memset(half0[0:64, :], 1.0), [Bk], [Bk])
        op("pool", lambda e: e.memset(half1[:], 0.0), [Bk], [Bk])
        op("pool", lambda e: e.memset(half1[64:128, :], 1.0), [Bk], [Bk])
        op("pool", lambda e: e.memset(onesf[:], 1.0), [Bk], [Bk])
        op("pool", lambda e: e.tensor_copy(identb[:], ident[:]), [Bk, Bc], [Bk])
        cwg, Bcw = t("cwg", [128, 3, 6, 4], F32)
        for x in range(3):
            for h in range(6):
                for i in range(4):
                    c0 = x * 768 + h * 128
                    P.dma("sp", lambda e, x=x, h=h, i=i, c0=c0: e.dma_start(out=cwg[:, x, h, i:i + 1], in_=gdn_conv_w[l, i, c0:c0 + 128].rearrange("(p o) -> p o", o=1)), writes=[Bcw])
        gnw, Bgnw = t("gnw", [128, 1], F32)
        P.dma("sp", lambda e: e.dma_start(out=gnw[:], in_=gdn_norm_w[l].rearrange("(p o) -> p o", o=1)), writes=[Bgnw])
        alb, Balb = t("alb", [128, 6], F32)
        dtb, Bdtb = t("dtb", [128, 6], F32)
        P.dma("sp", lambda e: e.dma_start(out=alb[:], in_=gdn_a_log[l:l + 1, :].broadcast_to([128, 6])), writes=[Balb])
        P.dma("sp", lambda e: e.dma_start(out=dtb[:], in_=gdn_dt_bias[l:l + 1, :].broadcast_to([128, 6])), writes=[Bdtb])
        nea, Bnea = t("nea", [128, 6], F32)
        op("act", lambda e: e.activation(nea[:], alb[:], AF.Exp), [Balb], [Bnea])
        op("dve", lambda e: e.tensor_scalar(nea[:], nea[:], -1.0, None, ALU.mult), [Bnea], [Bnea])
        ab, Bab_s = t("ab", [128, NB, 12], F32)
        P.dma("sp", lambda e: e.dma_start(out=ab[:], in_=abtok.rearrange("(blk p) c -> p blk c", p=128)), reads=[Bab], writes=[Bab_s])
        g_, Bg = t("g", [128, NB, 6], F32)
        beta, Bbeta = t("beta", [128, NB, 6], F32)
        gcs, Bgcs = t("gcs", [128, NB, 6], F32)
        kbs, Bkbs = t("kbs", [128, NB, 6], F32)
        kds, Bkds = t("kds", [128, NB, 6], F32)
        eb0, Beb0 = t("eb0", [128, NB, 6], F32)
        eb1, Beb1 = t("eb1", [128, NB, 6], F32)
        tA, BtA = t("tA", [128, 6], F32)
        tB, BtB = t("tB", [128, 6], F32)
        p7, B7 = ps[7], psb[7]
        for b in range(NB):
            op("dve", lambda e, b=b: e.tensor_tensor(tA[:], ab[:, b, 0:6], dtb[:], ALU.add), [Bab_s, Bdtb], [BtA])
            op("act", lambda e: e.activation(tA[:], tA[:], AF.Exp), [BtA], [BtA])
            op("act", lambda e: e.activation(tA[:], tA[:], AF.Ln, bias=1.0), [BtA], [BtA])
            op("dve", lambda e, b=b: e.tensor_tensor(g_[:, b, :], tA[:], nea[:], ALU.mult), [BtA, Bnea], [Bg])
            op("act", lambda e, b=b: e.activation(tB[:], ab[:, b, 6:12], AF.Exp, scale=-1.0), [Bab_s], [BtB])
            op("dve", lambda e: e.tensor_scalar(tB[:], tB[:], 1.0, None, ALU.add), [BtB], [BtB])
            op("dve", lambda e, b=b: e.reciprocal(beta[:, b, :], tB[:]), [BtB], [Bbeta])
            P.op("pe", lambda e, b=b: e.matmul(p7[:, 0:6], tri[:], g_[:, b, :], start=True, stop=True), reads=[Bg, Bk], writes=[B7])
            P.op("pe", lambda e, b=b: e.matmul(p7[:, 8:14], blk1[:], g_[:, b, :], start=True, stop=True), reads=[Bg, Bk], writes=[B7])
            P.op("pe", lambda e, b=b: e.matmul(p7[:, 16:22], half0[:], g_[:, b, :], start=True, stop=True), reads=[Bg, Bk], writes=[B7])
            P.op("pe", lambda e, b=b: e.matmul(p7[:, 24:30], half1[:], g_[:, b, :], start=True, stop=True), reads=[Bg, Bk], writes=[B7])
            op("dve", lambda e, b=b: e.tensor_copy(gcs[:, b, :], p7[:, 0:6]), [B7], [Bgcs])
            op("act", lambda e, b=b: e.activation(tA[:], p7[:, 0:6], AF.Exp), [B7], [BtA])
            op("dve", lambda e, b=b: e.tensor_tensor(kbs[:, b, :], tA[:], beta[:, b, :], ALU.mult), [BtA, Bbeta], [Bkbs])
            op("dve", lambda e, b=b: e.tensor_tensor(tB[:], p7[:, 8:14], gcs[:, b, :], ALU.subtract), [B7, Bgcs], [BtB])
            op("act", lambda e, b=b: e.activation(kds[:, b, :], tB[:], AF.Exp), [BtB], [Bkds])
            op("act", lambda e, b=b: e.activation(eb0[:, b, :], p7[:, 16:22], AF.Exp), [B7], [Beb0])
            op("act", lambda e, b=b: e.activation(eb1[:, b, :], p7[:, 24:30], AF.Exp), [B7], [Beb1])
        xin, Bxin = t("xin", [128, T + 3], F32)
        acc, Bacc = t("acc", [128, T], F32)
        sqbs = [t(f"sqb{i}", [128, 512], BF16) for i in range(3)]
        rrs = [t(f"rr{i}", [128, 512], F32) for i in range(3)]
        zts = [t(f"zt{i}", [128, 512], F32) for i in range(2)]
        ygs = [t(f"yg{i}", [128, 512], F32) for i in range(2)]
        yos = [t(f"yo{i}", [128, 512], BF16) for i in range(2)]
        nrm_banks = [(ps[4 + i], psb[4 + i]) for i in range(4)]
        nrm_ctr = [0]
        op("pool", lambda e: e.memset(xin[:, 0:3], 0.0), [], [Bxin])
        qscale = 128 ** -0.5

        class Ctx:
            pass
        ctxs = []
        for ci in range(2):
            C = Ctx()
            for nm, shp, dt in (("vc", [128, T], F32), ("qn", [128, T], BF16), ("kn", [128, T], BF16), ("oT", [128, T], F32),
                                ("S32", [128, 128], F32), ("Sb", [128, 128], BF16), ("gb", [128, 128], F32), ("d1", [128, 128], F32),
                                ("dl", [128, 128], F32), ("du", [128, 128], F32), ("er", [128, 128], F32), ("qd", [128, 128], BF16),
                                ("t1", [128, 128], F32), ("AT", [128, 128], BF16), ("kbg", [128, 128], BF16), ("kdec", [128, 128], BF16),
                                ("vb", [128, 128], BF16), ("ktok", [128, 128], F32), ("vtk", [128, 128], F32), ("PTm", [128, 2, 128], BF16),
                                ("wTm", [128, 2, 128], BF16), ("usb", [128, 128], F32), ("vnew", [128, 128], BF16)):
                tt_, bb_ = t(f"{nm}c{ci}", shp, dt)
                setattr(C, nm, tt_)
                setattr(C, "B" + nm, bb_)
            C.Nn = [t(f"N{i}c{ci}", [128, 128], BF16) for i in range(2)]
            C.NT = [t(f"NT{i}c{ci}", [128, 128], BF16) for i in range(2)]
            C.PT = [t(f"PT{i}c{ci}", [128, 128], BF16) for i in range(2)]
            C.pA, C.pB, C.pC, C.pD = ps[4 * ci:4 * ci + 4]
            C.BA, C.BB, C.BC, C.BD = psb[4 * ci:4 * ci + 4]
            C.pDb = C.pD[:].bitcast(BF16)
            op("pool", lambda e, C=C: e.memset(C.PTm[:], 0.0), [], [C.BPTm])
            op("pool", lambda e, C=C: e.memset(C.wTm[:], 0.0), [], [C.BwTm])
            ctxs.append(C)

        def conv_norm(h, C):
            for x in range(3):
                r0 = 1536 + x * 768 + h * 128
                P.dma("sp", lambda e, r0=r0: e.dma_start(out=xin[:, 3:T + 3], in_=pfT[r0:r0 + 128, :]), reads=[Bpf], writes=[Bxin])
                op("dve", lambda e, x=x: e.tensor_scalar(acc[:], xin[:, 3:T + 3], cwg[:, x, h, 3:4], None, ALU.mult), [Bxin, Bcw], [Bacc])
                for i in range(3):
                    op("dve", lambda e, x=x, i=i: e.scalar_tensor_tensor(acc[:], xin[:, i:i + T], cwg[:, x, h, i:i + 1], acc[:], ALU.mult, ALU.add), [Bxin, Bcw, Bacc], [Bacc])
                if x == 2:
                    op("act", lambda e: e.activation(C.vc[:], acc[:], AF.Silu), [Bacc], [C.Bvc])
                else:
                    op("act", lambda e: e.activation(acc[:], acc[:], AF.Silu), [Bacc], [Bacc])
                    dst, Bdst = (C.qn, C.Bqn) if x == 0 else (C.kn, C.Bkn)
                    sc_ = qscale if x == 0 else 1.0
                    for tt in range(NTT):
                        sl = slice(tt * 512, (tt + 1) * 512)
                        k_ = nrm_ctr[0]
                        nrm_ctr[0] += 1
                        sqb, Bsqb = sqbs[k_ % 3]
                        rr, Brr = rrs[k_ % 3]
                        pn, Bpn = nrm_banks[k_ % 4]
                        op("act", lambda e, sl=sl, sqb=sqb: e.activation(sqb[:], acc[:, sl], AF.Square), [Bacc], [Bsqb])
                        P.op("pe", lambda e, pn=pn, sqb=sqb: e.matmul(pn[:], ones_bf[:], sqb[:], start=True, stop=True), reads=[Bsqb, Bc], writes=[Bpn])
                        op("act", lambda e, pn=pn, rr=rr: e.activation(rr[:], pn[:], AF.Sqrt, bias=L2_EPS), [Bpn], [Brr])
                        op("dve", lambda e, rr=rr: e.reciprocal(rr[:], rr[:]), [Brr], [Brr])
                        op("dve", lambda e, sl=sl, dst=dst, sc_=sc_, rr=rr: e.scalar_tensor_tensor(dst[:, sl], acc[:, sl], sc_, rr[:], ALU.mult, ALU.mult), [Bacc, Brr], [Bdst])

        def block_ops(h, b, C, L):
            def rop(eng, fn, reads, writes):
                L.append(lambda: P.op(eng, fn, reads=list(reads), writes=list(writes)))
            pA, pB, pC, pD, pDb = C.pA, C.pB, C.pC, C.pD, C.pDb
            BA, BB, BC, BD = C.BA, C.BB, C.BC, C.BD
            c0 = b * 128
            ksl = C.kn[:, c0:c0 + 128]
            qsl = C.qn[:, c0:c0 + 128]
            rop("pe", lambda e: e.transpose(pDb[:, 0:128], ksl, identb[:]), [C.Bkn, Bk], [BD])
            rop("act", lambda e: e.activation(C.ktok[:], pDb[:, 0:128], AF.Copy), [BD], [C.Bktok])
            rop("dve", lambda e: e.tensor_scalar(C.kbg[:], C.ktok[:], kbs[:, b, h:h + 1], None, ALU.mult), [C.Bktok, Bkbs], [C.Bkbg])
            rop("dve", lambda e: e.tensor_scalar(C.kdec[:], C.ktok[:], kds[:, b, h:h + 1], None, ALU.mult), [C.Bktok, Bkds], [C.Bkdec])
            rop("pe", lambda e: e.transpose(pC[:, 0:128], C.vc[:, c0:c0 + 128], ident[:]), [C.Bvc, Bc], [BC])
            rop("act", lambda e: e.activation(C.vtk[:], pC[:, 0:128], AF.Copy), [BC], [C.Bvtk])
            rop("dve", lambda e: e.tensor_scalar(C.vb[:], C.vtk[:], beta[:, b, h:h + 1], None, ALU.mult), [C.Bvtk, Bbeta], [C.Bvb])
            rop("pe", lambda e: e.matmul(pA[:, 0:128], ksl, ksl, start=True, stop=True), [C.Bkn], [BA])
            rop("pe", lambda e: e.matmul(pB[:, 0:128], ksl, qsl, start=True, stop=True), [C.Bkn, C.Bqn], [BB])
            rop("dve", lambda e: e.tensor_scalar(C.gb[:], onesf[:], g_[:, b, h:h + 1], None, ALU.mult), [Bg, Bk], [C.Bgb])
            rop("pe", lambda e: e.matmul(pC[:, 0:128], C.gb[:], tri[:], start=True, stop=True), [C.Bgb, Bk], [BC])
            rop("dve", lambda e: e.tensor_scalar(C.d1[:], pC[:, 0:128], -1.0, gcs[:, b, h:h + 1], ALU.mult, ALU.add), [BC, Bgcs], [C.Bd1])
            rop("act", lambda e: e.activation(C.er[:], pC[:, 0:128], AF.Exp), [BC], [C.Ber])
            rop("dve", lambda e: e.tensor_scalar(C.dl[:], C.d1[:], 0.0, None, ALU.min), [C.Bd1], [C.Bdl])
            rop("dve", lambda e: e.tensor_scalar(C.du[:], C.d1[:], -1.0, 0.0, ALU.mult, ALU.min), [C.Bd1], [C.Bdu])
            rop("act", lambda e: e.activation(C.dl[:], C.dl[:], AF.Exp), [C.Bdl], [C.Bdl])
            rop("act", lambda e: e.activation(C.du[:], C.du[:], AF.Exp), [C.Bdu], [C.Bdu])
            rop("dve", lambda e: e.tensor_tensor(C.qd[:], qsl, C.er[:], ALU.mult), [C.Bqn, C.Ber], [C.Bqd])
            N, BN = C.Nn[0]
            NTt, BNT = C.NT[0]
            PTt, BPT = C.PT[0]
            rop("dve", lambda e: e.scalar_tensor_tensor(C.t1[:], pA[:, 0:128], beta[:, b, h:h + 1], C.dl[:], ALU.mult, ALU.mult), [BA, Bbeta, C.Bdl], [C.Bt1])
            rop("dve", lambda e, N=N: e.tensor_tensor(N[:], C.t1[:], negL[:], ALU.mult), [C.Bt1, Bk], [BN])
            rop("pe", lambda e, N=N: e.transpose(pDb[:, 0:128], N[:], identb[:]), [BN, Bk], [BD])
            rop("act", lambda e, NTt=NTt: e.activation(NTt[:], pDb[:, 0:128], AF.Copy), [BD], [BNT])
            rop("dve", lambda e, PTt=PTt: e.tensor_tensor(PTt[:], pDb[:, 0:128], identb[:], ALU.add), [BD, Bk], [BPT])
            rop("dve", lambda e: e.tensor_tensor(C.t1[:], pB[:, 0:128], C.du[:], ALU.mult), [BB, C.Bdu], [C.Bt1])
            rop("dve", lambda e: e.tensor_tensor(C.AT[:], C.t1[:], tri[:], ALU.mult), [C.Bt1, Bk], [C.BAT])
            cur = 0
            for step in range(5):
                N, BN = C.Nn[cur]
                NTt, BNT = C.NT[cur]
                PTt, BPT = C.PT[cur]
                N2, BN2 = C.Nn[1 - cur]
                NT2, BNT2 = C.NT[1 - cur]
                PT2, BPT2 = C.PT[1 - cur]
                rop("pe", lambda e, NTt=NTt, N=N: e.matmul(pA[:, 0:128], NTt[:], N[:], start=True, stop=True), [BNT, BN], [BA])
                rop("act", lambda e, N2=N2: e.activation(N2[:], pA[:, 0:128], AF.Copy), [BA], [BN2])
                if step < 4:
                    rop("pe", lambda e, NTt=NTt, N=N: e.matmul(pB[:, 0:128], N[:], NTt[:], start=True, stop=True), [BNT, BN], [BB])
                    rop("act", lambda e, NT2=NT2: e.activation(NT2[:], pB[:, 0:128], AF.Copy), [BB], [BNT2])
                rop("pe", lambda e, N2=N2, PTt=PTt: e.matmul(pC[:, 0:128], N2[:], PTt[:], start=True, stop=True), [BN2, BPT], [BC])
                rop("dve", lambda e, PT2=PT2, PTt=PTt: e.tensor_tensor(PT2[:], pC[:, 0:128], PTt[:], ALU.add), [BC, BPT], [BPT2])
                cur = 1 - cur
            PTt, BPT = C.PT[cur]
            rop("act", lambda e, PTt=PTt: e.activation(C.PTm[:, 0, 0:64], PTt[:, 0:64], AF.Copy), [BPT], [C.BPTm])
            rop("act", lambda e, PTt=PTt: e.activation(C.PTm[:, 1, 64:128], PTt[:, 64:128], AF.Copy), [BPT], [C.BPTm])
            rop("pe", lambda e, PTt=PTt: e.matmul(pA[:, 0:128], C.kbg[:], PTt[:], start=True, stop=True), [C.Bkbg, BPT], [BA])
            rop("dve", lambda e: e.tensor_copy(C.wTm[:, 0, 0:64], pA[:, 0:64]), [BA], [C.BwTm])
            rop("dve", lambda e: e.tensor_copy(C.wTm[:, 1, 64:128], pA[:, 64:128]), [BA], [C.BwTm])
            for c in range(2):
                cs = slice(c * 64, (c + 1) * 64)
                rop("pe", lambda e, c=c: e.matmul(pC[:, 0:128], C.PTm[:, c, :], C.vb[:], start=True, stop=True), [C.BPTm, C.Bvb], [BC])
                rop("act", lambda e: e.activation(C.usb[:], pC[:, 0:128], AF.Copy), [BC], [C.Busb])
                rop("pe", lambda e, c=c: e.matmul(pA[:, 0:128], C.wTm[:, c, :], C.Sb[:], start=True, stop=True), [C.BwTm, C.BSb], [BA])
                rop("dve", lambda e: e.tensor_tensor(C.vnew[:], C.usb[:], pA[:, 0:128], ALU.subtract), [C.Busb, BA], [C.Bvnew])
                rop("pe", lambda e, cs=cs: e.matmul(pD[:, cs], C.Sb[:], C.qd[:, cs], start=True, stop=False), [C.BSb, C.Bqd], [BD])
                rop("pe", lambda e, cs=cs: e.matmul(pD[:, cs], C.vnew[:], C.AT[:, cs], start=False, stop=True), [C.Bvnew, C.BAT], [BD])
                rop("pe", lambda e: e.matmul(pB[:, 0:128], C.kdec[:], C.vnew[:], start=True, stop=True), [C.Bkdec, C.Bvnew], [BB])
                ebc = eb0 if c == 0 else eb1
                Bebc = Beb0 if c == 0 else Beb1
                rop("dve", lambda e, ebc=ebc: e.scalar_tensor_tensor(C.S32[:], C.S32[:], ebc[:, b, h:h + 1], pB[:, 0:128], ALU.mult, ALU.add), [C.BS32, Bebc, BB], [C.BS32])
                rop("act", lambda e: e.activation(C.Sb[:], C.S32[:], AF.Copy), [C.BS32], [C.BSb])
            rop("act", lambda e: e.activation(C.oT[:, c0:c0 + 128], pD[:, 0:128], AF.Copy), [BD], [C.BoT])

        def onorm(h, C):
            for tt in range(NTT):
                sl = slice(tt * 512, (tt + 1) * 512)
                k_ = nrm_ctr[0]
                nrm_ctr[0] += 1
                sqb, Bsqb = sqbs[k_ % 3]
                rr, Brr = rrs[k_ % 3]
                pn, Bpn = nrm_banks[k_ % 4]
                zt, Bzt = zts[k_ % 2]
                yg, Byg = ygs[k_ % 2]
                yo, Byo = yos[k_ % 2]
                r0 = 1536 + 2304 + h * 128
                P.dma("sp", lambda e, r0=r0, sl=sl, zt=zt: e.dma_start(out=zt[:], in_=pfT[r0:r0 + 128, sl]), reads=[Bpf], writes=[Bzt])
                op("act", lambda e, sl=sl, sqb=sqb: e.activation(sqb[:], C.oT[:, sl], AF.Square), [C.BoT], [Bsqb])
                P.op("pe", lambda e, pn=pn, sqb=sqb: e.matmul(pn[:], ones_bf[:], sqb[:], start=True, stop=True), reads=[Bsqb, Bc], writes=[Bpn])
                op("act", lambda e, pn=pn, rr=rr: e.activation(rr[:], pn[:], AF.Sqrt, bias=RMS_EPS, scale=1.0 / 128), [Bpn], [Brr])
                op("dve", lambda e, rr=rr: e.reciprocal(rr[:], rr[:]), [Brr], [Brr])
                op("dve", lambda e, sl=sl, rr=rr, yg=yg: e.scalar_tensor_tensor(yg[:], C.oT[:, sl], gnw[:, 0:1], rr[:], ALU.mult, ALU.mult), [C.BoT, Brr, Bgnw], [Byg])
                op("act", lambda e, zt=zt: e.activation(zt[:], zt[:], AF.Silu), [Bzt], [Bzt])
                op("dve", lambda e, yo=yo, yg=yg, zt=zt: e.tensor_tensor(yo[:], yg[:], zt[:], ALU.mult), [Byg, Bzt], [Byo])
                P.dma("sp", lambda e, sl=sl, yo=yo: e.dma_start(out=yT[1280 + h * 128:1280 + (h + 1) * 128, sl], in_=yo[:]), reads=[Byo], writes=[ByT])

        for hp in range(3):
            heads = (2 * hp, 2 * hp + 1)
            lists = []
            for ci, h in enumerate(heads):
                C = ctxs[ci]
                conv_norm(h, C)
                L = []
                L.append(lambda C=C: P.op("pool", lambda e: e.memset(C.S32[:], 0.0), reads=[C.BS32], writes=[C.BS32]))
                L.append(lambda C=C: P.op("pool", lambda e: e.memset(C.Sb[:], 0.0), reads=[C.BSb], writes=[C.BSb]))
                for b in range(NB):
                    block_ops(h, b, C, L)
                lists.append(L)
            for i in range(max(len(L) for L in lists)):
                for L in lists:
                    if i < len(L):
                        L[i]()
            for ci, h in enumerate(heads):
                onorm(h, ctxs[ci])


_CACHE = {}


def kernel(**inputs):
    B, T, _ = inputs["x"].shape
    key = (T,)
    if key not in _CACHE:
        _CACHE[key] = build(T=T, depth=2)
    nc = _CACHE[key]
    in_maps = []
    for c in range(B):
        m = {k: np.ascontiguousarray(np.asarray(v, dtype=np.float32)) for k, v in inputs.items() if k != "x"}
        m["x"] = np.ascontiguousarray(np.asarray(inputs["x"][c], dtype=np.float32))
        in_maps.append(m)
    res = run_bass_kernel_spmd(nc, in_maps, core_ids=list(range(B)))
    return np.stack([np.asarray(r["out"], dtype=np.float32) for r in res.results], axis=0)
```

```python
import contextlib
import numpy as np
import concourse.bass as bass
import concourse.mybir as mybir
from concourse.bass_utils import run_bass_kernel_spmd

F32 = mybir.dt.float32
BF16 = mybir.dt.bfloat16
AF = mybir.ActivationFunctionType
ALU = mybir.AluOpType

D = 2048
KT = 16
DFF = 8192
IN_DIM = 6924
RMS_EPS = 1e-6
L2_EPS = 1e-6
import os
GV = os.environ.get("GV", "v3")
GSTOP = int(os.environ.get("GSTOP", "9"))


class Buf:
    __slots__ = ("name", "w", "r", "psum")

    def __init__(self, name="", psum=False):
        self.name = name
        self.w = None
        self.r = []
        self.psum = psum


class Phase(contextlib.ExitStack):
    def __init__(self, P):
        super().__init__()
        self.P = P

    def __exit__(self, *a):
        if a[0] is None:
            self.P.barrier()
        return super().__exit__(*a)


class Prog:
    ENG = ("pe", "act", "dve", "pool", "sp")

    def __init__(self, nc, stack):
        self.nc = nc
        self.lists = {e: [] for e in self.ENG}
        self.sem = {e: stack.enter_context(nc.semaphore("s_" + e)) for e in self.ENG if e != "sp"}
        self.cnt = {e: 0 for e in self.ENG}
        self.known = {e: {} for e in self.ENG}
        self.dsem = {}
        self.dcnt = {}
        self.drr = {}
        for q, n in (("sp", 16), ("pool", 8), ("act", 4)):
            self.dsem[q] = [stack.enter_context(nc.semaphore(f"d_{q}{i}")) for i in range(n)]
            self.dcnt[q] = [0] * n
            self.drr[q] = 0
        self.out_events = []

    def _wait(self, eng, ev):
        sem, val = ev[0], ev[1]
        k = self.known[eng]
        if k.get(id(sem), 0) >= val:
            return
        k[id(sem)] = val
        self.lists[eng].append(("wait", sem, val))

    def _deps(self, eng, reads, writes):
        for b in reads:
            if b.w is not None and not (eng == "pe" and b.w[2] == "pe"):
                self._wait(eng, b.w)
            if b.psum:
                for ev in b.r:
                    if ev[2] != eng:
                        self._wait(eng, ev)
        for b in writes:
            if b.w is not None and not (eng == "pe" and b.w[2] == "pe"):
                self._wait(eng, b.w)
            for ev in b.r:
                if not (eng == "pe" and ev[2] == "pe"):
                    self._wait(eng, ev)

    def _mark(self, ev, reads, writes):
        for b in reads:
            b.r = [e for e in b.r if e[0] is not ev[0]] + [ev]
        for b in writes:
            b.w = ev
            b.r = []

    def op(self, eng, fn, reads=(), writes=(), inc=True):
        self._deps(eng, reads, writes)
        if inc:
            self.cnt[eng] += 1
            ev = (self.sem[eng], self.cnt[eng], eng)
            self.lists[eng].append(("op", fn, self.sem[eng]))
        else:
            ev = (self.sem[eng], self.cnt[eng] + 1, eng)
            self.lists[eng].append(("op", fn, None))
        self._mark(ev, reads, writes)
        return ev

    def dma(self, q, fn, reads=(), writes=(), is_output=False):
        i = self.drr[q]
        self.drr[q] = (i + 1) % len(self.dsem[q])
        sem = self.dsem[q][i]
        if self.dcnt[q][i] > 0:
            self._wait(q, (sem, self.dcnt[q][i]))
        self._deps(q, reads, writes)
        self.dcnt[q][i] += 16
        ev = (sem, self.dcnt[q][i], "dma")
        self.lists[q].append(("dma", fn, sem))
        self._mark(ev, reads, writes)
        if is_output:
            self.out_events.append(ev)
        return ev

    def barrier(self):
        evs = [(self.sem[e], self.cnt[e], e) for e in self.sem if self.cnt[e] > 0]
        for q in self.dsem:
            for sem, c in zip(self.dsem[q], self.dcnt[q]):
                if c > 0:
                    evs.append((sem, c, "dma"))
        for eng in self.ENG:
            for ev in evs:
                self._wait(eng, ev)

    def finish(self):
        for ev in self.out_events:
            self._wait("sp", ev)

    def emit(self, block):
        def run(e, lst):
            for it in lst:
                if it[0] == "wait":
                    e.wait_ge(it[1], it[2])
                elif it[0] == "op":
                    ins = it[1](e)
                    if it[2] is not None:
                        ins.then_inc(it[2], 1)
                else:
                    it[1](e).then_inc(it[2], 16)
        L = self.lists

        @block.sync
        def _(e):
            run(e, L["sp"])

        @block.tensor
        def _(e):
            run(e, L["pe"])

        @block.scalar
        def _(e):
            run(e, L["act"])

        @block.vector
        def _(e):
            run(e, L["dve"])

        @block.gpsimd
        def _(e):
            run(e, L["pool"])


def build(T=4096, depth=2, flags=("sb", "sc", "gdn")):
    nc = bass.Bass("TRN2", target_bir_lowering=False)
    depth_ = depth
    depth = max(depth, 1)
    NTT = T // 512
    NTB = T // 128
    G = min(T, 2048)
    NG = T // G
    GT = G // 512

    def din(name, shape):
        return nc.dram_tensor(name, shape, F32, kind="ExternalInput").ap()

    x_in = din("x", [T, D])
    norm1_w = din("norm1_w", [depth, D])
    w_in = din("w_in", [depth, D, IN_DIM])
    sc_conv_w = din("sc_conv_w", [depth, 3, 512])
    gdn_conv_w = din("gdn_conv_w", [depth, 4, 2304])
    gdn_a_log = din("gdn_a_log", [depth, 6])
    gdn_dt_bias = din("gdn_dt_bias", [depth, 6])
    gdn_norm_w = din("gdn_norm_w", [depth, 128])
    w_out = din("w_out", [depth, D, D])
    norm2_w = din("norm2_w", [depth, D])
    w_up = din("w_up", [depth, D, DFF])
    w_down = din("w_down", [depth, DFF, D])
    final_norm_w = din("final_norm_w", [D])
    out = nc.dram_tensor("out", [T, D], F32, kind="ExternalOutput").ap()

    def scratch(name, shape, dt):
        return nc.dram_tensor(name, shape, dt, kind="Internal").ap()

    xT = scratch("xT", [D, T], F32)
    qkT = scratch("qkT", [1536, T], BF16)
    vtok = scratch("vtok", [T, 768], BF16)
    pfT = scratch("pfT", [4608, T], F32)
    abtok = scratch("abtok", [T, 12], F32)
    yT = scratch("yT", [D, T], BF16)
    winF = scratch("winF", [depth, 12, 128, KT, 512], BF16)
    winV = scratch("winV", [depth, 128, KT, 768], BF16)
    winAB = scratch("winAB", [depth, 128, KT, 12], BF16)
    woutF = scratch("woutF", [depth, 4, 128, KT, 512], BF16)
    wupF = scratch("wupF", [depth, 16, 128, KT, 512], BF16)
    wdnF = scratch("wdnF", [depth, 4, 4, 128, KT, 512], BF16)

    with contextlib.ExitStack() as gst:
        P = Prog(nc, gst)

        uniq = [0]

        def sb(name, shape, dt, st=gst):
            uniq[0] += 1
            return st.enter_context(nc.sbuf_tensor(f"{name}_{uniq[0]}", shape, dt))

        ps = [gst.enter_context(nc.psum_tensor(f"ps{i}", [128, 512], F32)) for i in range(8)]
        psb = [Buf(f"ps{i}", psum=True) for i in range(8)]

        ident = sb("ident", [128, 128], F32)
        ones_bf = sb("ones_bf", [128, 128], BF16)
        Bc = Buf("consts")

        P.op("pool", lambda e: e.memset(ident[:], 0.0), writes=[Bc])
        P.op("pool", lambda e: e.memset(ones_bf[:], 1.0), writes=[Bc])
        P.op("pool", lambda e: e.affine_select(out=ident[:], in_=ident[:], pattern=[[-1, 128]], compare_op=ALU.not_equal,
                                                fill=1.0, base=0, channel_multiplier=1), reads=[Bc], writes=[Bc])

        nw_all = sb("nw_all", [128, 2 * depth + 1, KT], F32)
        Bnw = Buf("nw")
        for i in range(2 * depth + 1):
            if i == 2 * depth:
                src = final_norm_w
            elif i % 2 == 0:
                src = norm1_w[i // 2]
            else:
                src = norm2_w[i // 2]
            P.dma("sp", lambda e, i=i, src=src: e.dma_start(out=nw_all[:, i, :], in_=src.rearrange("(kt p) -> p kt", p=128),
                                                             allow_slow_non_contiguous=True), writes=[Bnw])

        WB = {}

        def conv_dma(key, out_ap, in_ap):
            b = WB.setdefault(key, Buf(str(key)))
            P.dma("pool", lambda e: e.dma_start(out=out_ap, in_=in_ap), writes=[b])

        FCOLS = [0, 512, 1024] + [2304 + 512 * i for i in range(9)]
        for l in range(depth_):
            wl = w_in[l].rearrange("(kt p) c -> p kt c", p=128)
            for j, c0 in enumerate(FCOLS):
                for hh in range(2):
                    conv_dma(("win", l, j), winF[l, j, :, hh * 8:(hh + 1) * 8, :], wl[:, hh * 8:(hh + 1) * 8, c0:c0 + 512])
            for hh in range(2):
                conv_dma(("winV", l), winV[l, :, hh * 8:(hh + 1) * 8, :], wl[:, hh * 8:(hh + 1) * 8, 1536:2304])
            conv_dma(("winAB", l), winAB[l], wl[:, :, 6912:6924])
            wl = w_out[l].rearrange("(kt p) c -> p kt c", p=128)
            for j in range(4):
                for hh in range(2):
                    conv_dma(("wout", l, j), woutF[l, j, :, hh * 8:(hh + 1) * 8, :], wl[:, hh * 8:(hh + 1) * 8, j * 512:(j + 1) * 512])
            wl = w_up[l].rearrange("(kt p) c -> p kt c", p=128)
            for j in range(16):
                for hh in range(2):
                    conv_dma(("wup", l, j), wupF[l, j, :, hh * 8:(hh + 1) * 8, :], wl[:, hh * 8:(hh + 1) * 8, j * 512:(j + 1) * 512])
            wl = w_down[l].rearrange("(ft p) c -> p ft c", p=128)
            for j in range(4):
                for fq in range(4):
                    conv_dma(("wdn", l, j, fq), wdnF[l, j, fq], wl[:, fq * 16:(fq + 1) * 16, j * 512:(j + 1) * 512])

        BxT = [Buf(f"xT{tt}") for tt in range(NTT)]
        Bqk = Buf("qkT")
        Bv = Buf("vtok")
        Bpf = Buf("pfT")
        Bab = Buf("abtok")
        ByT = Buf("yT")

        xT_v = xT.rearrange("(kt p) t -> p kt t", p=128)
        yT_v = yT.rearrange("(kt p) t -> p kt t", p=128)

        class Rot:
            def __init__(self, tiles):
                self.t = tiles
                self.b = [Buf() for _ in tiles]
                self.i = 0

            def next(self):
                i = self.i
                self.i = (i + 1) % len(self.t)
                return self.t[i], self.b[i]

        class Pref:
            def __init__(self, rot, loaders, depth=2):
                self.rot, self.loaders, self.q, self.i = rot, loaders, [], 0
                assert len(rot.t) > depth
                for _ in range(depth):
                    self._issue()

            def _issue(self):
                if self.i < len(self.loaders):
                    wt, Bwt = self.rot.next()
                    self.loaders[self.i](wt, Bwt)
                    self.q.append((wt, Bwt))
                    self.i += 1

            def get(self):
                cur = self.q.pop(0)
                self._issue()
                return cur

        psrot = Rot(ps[0:4])
        psrot.b = psb[0:4]
        evac_flip = [0]

        def evac(out_ap, in_ap, reads, writes):
            evac_flip[0] ^= 1
            if evac_flip[0]:
                P.op("act", lambda e: e.activation(out_ap, in_ap, AF.Copy), reads=reads, writes=writes)
            else:
                P.op("dve", lambda e: e.tensor_copy(out_ap, in_ap), reads=reads, writes=writes)

        def mm(out_ap, lhsT, rhs, start, stop, reads, writes, inc):
            P.op("pe", lambda e: e.matmul(out_ap, lhsT, rhs, start=start, stop=stop), reads=reads, writes=writes, inc=inc)

        def rmsnorm_tile(st_, xs, Bxs, sq_rot, rr, Brr, widx, tt, hT, BhT, hcol0, ps_ss, Bps):
            P.dma("sp", lambda e: e.dma_start(out=xs[:], in_=xT_v[:, :, tt * 512:(tt + 1) * 512]), reads=[BxT[tt]], writes=[Bxs])
            for kt in range(KT):
                sq, Bsq = sq_rot.next()
                P.op("act", lambda e, sq=sq, kt=kt: e.activation(sq[:], xs[:, kt, :], AF.Square), reads=[Bxs], writes=[Bsq])
                mm(ps_ss[:], ones_bf[:], sq[:], kt == 0, kt == KT - 1, [Bsq, Bc], [Bps], True)
            P.op("act", lambda e: e.activation(rr[:], ps_ss[:], AF.Sqrt, bias=RMS_EPS, scale=1.0 / D), reads=[Bps], writes=[Brr])
            P.op("dve", lambda e: e.reciprocal(rr[:], rr[:]), reads=[Brr], writes=[Brr])
            for kt in range(KT):
                P.op("dve", lambda e, kt=kt: e.scalar_tensor_tensor(hT[:, kt, hcol0:hcol0 + 512], xs[:, kt, :], nw_all[:, widx, kt:kt + 1],
                                                                     rr[:], ALU.mult, ALU.mult),
                     reads=[Bxs, Brr, Bnw], writes=[BhT[kt]])

        with Phase(P) as st:
            xin = Rot([sb(f"xin{i}", [128, D], F32, st) for i in range(2)])
            xst = Rot([sb(f"xst{i}", [128, KT, 128], F32, st) for i in range(2)])
            for tb in range(NTB):
                xi, Bxi = xin.next()
                P.dma("sp", lambda e, xi=xi, tb=tb: e.dma_start(out=xi[:], in_=x_in[tb * 128:(tb + 1) * 128, :]), writes=[Bxi])
                xs_, Bxs_ = xst.next()
                for g in range(4):
                    pt, Bpt = psrot.next()
                    for j in range(4):
                        P.op("pe", lambda e, pt=pt, xi=xi, g=g, j=j: e.transpose(pt[:, j * 128:(j + 1) * 128], xi[:, (4 * g + j) * 128:(4 * g + j + 1) * 128], ident[:]),
                             reads=[Bxi, Bc], writes=[Bpt], inc=(j == 3))
                    evac(xs_[:, 4 * g:4 * g + 4, :], pt[:].rearrange("p (j t) -> p j t", j=4), [Bpt], [Bxs_])
                P.dma("sp", lambda e, xs_=xs_, tb=tb: e.dma_start(out=xT_v[:, :, tb * 128:(tb + 1) * 128], in_=xs_[:]),
                      reads=[Bxs_], writes=[BxT[tb // 4]])

        def _layer(l):
            with Phase(P) as st:
                hT = sb("hT_a", [128, KT, G], BF16, st)
                BhT = [Buf(f"hT{k}") for k in range(KT)]
                xs = sb("xs_a", [128, KT, 512], F32, st)
                Bxs = Buf("xs")
                sq_rot = Rot([sb(f"sq_a{i}", [128, 512], BF16, st) for i in range(3)])
                rr = sb("rr_a", [128, 512], F32, st)
                Brr = Buf("rr")
                wts = Rot([sb(f"wt_a{i}", [128, KT, 512], BF16, st) for i in range(3)])
                wab = sb("wab_a", [128, KT, 12], BF16, st)
                Bwab = Buf("wab")
                stg32 = Rot([sb(f"stg32_a{i}", [128, 512], F32, st) for i in range(3)])
                stg16 = Rot([sb(f"stg16_a{i}", [128, 512], BF16, st) for i in range(3)])
                ldA = []
                for g in range(NG):
                    for j in range(12):
                        ldA.append(lambda wt, Bwt, l=l, j=j: P.dma("sp", lambda e: e.dma_start(out=wt[:], in_=winF[l, j]), reads=[WB[("win", l, j)]], writes=[Bwt]))
                    for (c0, n) in ((0, 512), (512, 256)):
                        ldA.append(lambda wt, Bwt, l=l, c0=c0, n=n: P.dma("sp", lambda e: e.dma_start(out=wt[:, :, 0:n], in_=winV[l, :, :, c0:c0 + n]),
                                                                         reads=[WB[("winV", l)]], writes=[Bwt]))
                prefA = Pref(wts, ldA)
                for g in range(NG):
                    for t4 in range(GT):
                        rmsnorm_tile(st, xs, Bxs, sq_rot, rr, Brr, 2 * l, g * GT + t4, hT, BhT, t4 * 512, ps[7], psb[7])
                    for j in range(12):
                        wt, Bwt = prefA.get()
                        for ci in range(4):
                            for t4 in range(GT):
                                pt, Bpt = psrot.next()
                                for kt in range(KT):
                                    mm(pt[:], wt[:, kt, ci * 128:(ci + 1) * 128], hT[:, kt, t4 * 512:(t4 + 1) * 512], kt == 0, kt == KT - 1,
                                       [Bwt, BhT[kt]], [Bpt], kt == KT - 1)
                                tok0 = g * G + t4 * 512
                                if j < 3:
                                    sg, Bsg = stg16.next()
                                    evac(sg[:], pt[:], [Bpt], [Bsg])
                                    r0 = j * 512 + ci * 128
                                    P.dma("sp", lambda e, sg=sg, r0=r0, tok0=tok0: e.dma_start(out=qkT[r0:r0 + 128, tok0:tok0 + 512], in_=sg[:]),
                                          reads=[Bsg], writes=[Bqk])
                                else:
                                    sg, Bsg = stg32.next()
                                    evac(sg[:], pt[:], [Bpt], [Bsg])
                                    r0 = (j - 3) * 512 + ci * 128
                                    P.dma("sp", lambda e, sg=sg, r0=r0, tok0=tok0: e.dma_start(out=pfT[r0:r0 + 128, tok0:tok0 + 512], in_=sg[:]),
                                          reads=[Bsg], writes=[Bpf])
                    for (c0, n) in ((0, 512), (512, 256)):
                        wt, Bwt = prefA.get()
                        for tb in range(G // 128):
                            pt, Bpt = psrot.next()
                            for kt in range(KT):
                                mm(pt[:, 0:n], hT[:, kt, tb * 128:(tb + 1) * 128], wt[:, kt, 0:n], kt == 0, kt == KT - 1, [Bwt, BhT[kt]], [Bpt], kt == KT - 1)
                            sg, Bsg = stg16.next()
                            evac(sg[:, 0:n], pt[:, 0:n], [Bpt], [Bsg])
                            tok0 = g * G + tb * 128
                            P.dma("sp", lambda e, sg=sg, c0=c0, n=n, tok0=tok0: e.dma_start(out=vtok[tok0:tok0 + 128, c0:c0 + n], in_=sg[:, 0:n]),
                                  reads=[Bsg], writes=[Bv])
                    P.dma("sp", lambda e, l=l: e.dma_start(out=wab[:], in_=winAB[l]), reads=[WB[("winAB", l)]], writes=[Bwab])
                    for tb in range(G // 128):
                        pt, Bpt = psrot.next()
                        for kt in range(KT):
                            mm(pt[:, 0:12], hT[:, kt, tb * 128:(tb + 1) * 128], wab[:, kt, :], kt == 0, kt == KT - 1, [Bwab, BhT[kt]], [Bpt], kt == KT - 1)
                        sg, Bsg = stg32.next()
                        evac(sg[:, 0:12], pt[:, 0:12], [Bpt], [Bsg])
                        tok0 = g * G + tb * 128
                        P.dma("sp", lambda e, sg=sg, tok0=tok0: e.dma_start(out=abtok[tok0:tok0 + 128, :], in_=sg[:, 0:12]),
                              reads=[Bsg], writes=[Bab])

            if "sb" in flags:
                phase_sb(nc, P, l, T, qkT, vtok, yT, Bqk, Bv, ByT, ps, psb, sb)
            else:
                zero_rows(nc, P, yT, ByT, 0, 768, T, sb)
            if "sc" in flags:
                phase_sc(nc, P, l, T, pfT, yT, sc_conv_w, Bpf, ByT, sb)
            else:
                zero_rows(nc, P, yT, ByT, 768, 512, T, sb)
            if "gdn" in flags:
                phase_gdn(nc, P, l, T, pfT, abtok, yT, gdn_conv_w, gdn_a_log, gdn_dt_bias, gdn_norm_w, Bpf, Bab, ByT, ps, psb, sb, ident, ones_bf, Bc)
            else:
                zero_rows(nc, P, yT, ByT, 1280, 768, T, sb)

            with Phase(P) as st:
                yS = sb("yS_e", [128, KT, G], BF16, st)
                ByS = Buf("yS")
                wts = Rot([sb(f"wt_e{i}", [128, KT, 512], BF16, st) for i in range(3)])
                xold = Rot([sb(f"xold_e{i}", [128, 512], F32, st) for i in range(3)])
                stg = Rot([sb(f"stg_e{i}", [128, 512], F32, st) for i in range(3)])
                ldE = []
                for g in range(NG):
                    for j in range(4):
                        ldE.append(lambda wt, Bwt, l=l, j=j: P.dma("sp", lambda e: e.dma_start(out=wt[:], in_=woutF[l, j]), reads=[WB[("wout", l, j)]], writes=[Bwt]))
                prefE = Pref(wts, ldE)
                for g in range(NG):
                    P.dma("sp", lambda e, g=g: e.dma_start(out=yS[:], in_=yT_v[:, :, g * G:(g + 1) * G]), reads=[ByT], writes=[ByS])
                    for j in range(4):
                        wt, Bwt = prefE.get()
                        for ci in range(4):
                            dm = 4 * j + ci
                            for t4 in range(GT):
                                tt = g * GT + t4
                                xo, Bxo = xold.next()
                                P.dma("sp", lambda e, xo=xo, dm=dm, tt=tt: e.dma_start(out=xo[:], in_=xT[dm * 128:(dm + 1) * 128, tt * 512:(tt + 1) * 512]),
                                      reads=[BxT[tt]], writes=[Bxo])
                                pt, Bpt = psrot.next()
                                for kt in range(KT):
                                    mm(pt[:], wt[:, kt, ci * 128:(ci + 1) * 128], yS[:, kt, t4 * 512:(t4 + 1) * 512], kt == 0, kt == KT - 1,
                                       [Bwt, ByS], [Bpt], kt == KT - 1)
                                sg, Bsg = stg.next()
                                P.op("dve", lambda e, sg=sg, pt=pt, xo=xo: e.tensor_tensor(sg[:], pt[:], xo[:], ALU.add), reads=[Bpt, Bxo], writes=[Bsg])
                                P.dma("sp", lambda e, sg=sg, dm=dm, tt=tt: e.dma_start(out=xT[dm * 128:(dm + 1) * 128, tt * 512:(tt + 1) * 512], in_=sg[:]),
                                      reads=[Bsg], writes=[BxT[tt]])

            with Phase(P) as st:
                hT = sb("hT_f", [128, KT, 512], BF16, st)
                BhT = [Buf(f"hT{k}") for k in range(KT)]
                aT = sb("aT_f", [128, 64, 512], BF16, st)
                BaT = Buf("aT")
                xs = sb("xs_f", [128, KT, 512], F32, st)
                Bxs = Buf("xs")
                sq_rot = Rot([sb(f"sq_f{i}", [128, 512], BF16, st) for i in range(3)])
                rr = sb("rr_f", [128, 512], F32, st)
                Brr = Buf("rr")
                wts = Rot([sb(f"wt_f{i}", [128, KT, 512], BF16, st) for i in range(3)])
                rl = Rot([sb(f"rl_f{i}", [128, 512], F32, st) for i in range(3)])
                xold = Rot([sb(f"xold_f{i}", [128, 512], F32, st) for i in range(4)])
                stg = Rot([sb(f"stg_f{i}", [128, 512], F32, st) for i in range(4)])
                ldF = []
                for tt in range(NTT):
                    for j in range(16):
                        ldF.append(lambda wt, Bwt, l=l, j=j: P.dma("sp", lambda e: e.dma_start(out=wt[:], in_=wupF[l, j]), reads=[WB[("wup", l, j)]], writes=[Bwt]))
                    for j in range(4):
                        for fq in range(4):
                            ldF.append(lambda wt, Bwt, l=l, j=j, fq=fq: P.dma("sp", lambda e: e.dma_start(out=wt[:], in_=wdnF[l, j, fq]), reads=[WB[("wdn", l, j, fq)]], writes=[Bwt]))
                prefF = Pref(wts, ldF)
                rmsnorm_tile(st, xs, Bxs, sq_rot, rr, Brr, 2 * l + 1, 0, hT, BhT, 0, ps[7], psb[7])
                for tt in range(NTT):
                    for j in range(16):
                        wt, Bwt = prefF.get()
                        for ci in range(4):
                            pt, Bpt = psrot.next()
                            for kt in range(KT):
                                mm(pt[:], wt[:, kt, ci * 128:(ci + 1) * 128], hT[:, kt, :], kt == 0, kt == KT - 1, [Bwt, BhT[kt]], [Bpt], kt == KT - 1)
                            r_, Br_ = rl.next()
                            P.op("act", lambda e, r_=r_, pt=pt: e.activation(r_[:], pt[:], AF.Relu), reads=[Bpt], writes=[Br_])
                            P.op("dve", lambda e, r_=r_, f=4 * j + ci: e.tensor_tensor(aT[:, f, :], r_[:], r_[:], ALU.mult), reads=[Br_], writes=[BaT])
                    if tt + 1 < NTT:
                        rmsnorm_tile(st, xs, Bxs, sq_rot, rr, Brr, 2 * l + 1, tt + 1, hT, BhT, 0, ps[7], psb[7])
                    for j in range(4):
                        for fq in range(4):
                            wt, Bwt = prefF.get()
                            for ft in range(16):
                                f = fq * 16 + ft
                                for ci in range(4):
                                    mm(ps[ci][:], wt[:, ft, ci * 128:(ci + 1) * 128], aT[:, f, :], f == 0, f == 63, [Bwt, BaT], [psb[ci]], f == 63 or ft == 15)
                        for ci in range(4):
                            dm = 4 * j + ci
                            xo, Bxo = xold.next()
                            P.dma("sp", lambda e, xo=xo, dm=dm, tt=tt: e.dma_start(out=xo[:], in_=xT[dm * 128:(dm + 1) * 128, tt * 512:(tt + 1) * 512]),
                                  reads=[BxT[tt]], writes=[Bxo])
                            sg, Bsg = stg.next()
                            P.op("dve", lambda e, sg=sg, ci=ci, xo=xo: e.tensor_tensor(sg[:], ps[ci][:], xo[:], ALU.add), reads=[psb[ci], Bxo], writes=[Bsg])
                            P.dma("sp", lambda e, sg=sg, dm=dm, tt=tt: e.dma_start(out=xT[dm * 128:(dm + 1) * 128, tt * 512:(tt + 1) * 512], in_=sg[:]),
                                  reads=[Bsg], writes=[BxT[tt]])

        for l in range(depth_):
            _layer(l)

        with Phase(P) as st:
            hF = sb("hF_o", [128, KT, 512], F32, st)
            BhF = [Buf(f"hF{k}") for k in range(KT)]
            xs = sb("xs_o", [128, KT, 512], F32, st)
            Bxs = Buf("xs")
            sq_rot = Rot([sb(f"sq_o{i}", [128, 512], BF16, st) for i in range(3)])
            rr = sb("rr_o", [128, 512], F32, st)
            Brr = Buf("rr")
            ost = Rot([sb(f"ost{i}", [128, D], F32, st) for i in range(2)])
            for tt in range(NTT):
                rmsnorm_tile(st, xs, Bxs, sq_rot, rr, Brr, 2 * depth, tt, hF, BhF, 0, ps[7], psb[7])
                for s4 in range(4):
                    os_, Bos = ost.next()
                    for g in range(4):
                        pt, Bpt = psrot.next()
                        for j in range(4):
                            P.op("pe", lambda e, pt=pt, g=g, j=j, s4=s4: e.transpose(pt[:, j * 128:(j + 1) * 128], hF[:, 4 * g + j, s4 * 128:(s4 + 1) * 128], ident[:]),
                                 reads=[BhF[4 * g + j], Bc], writes=[Bpt], inc=(j == 3))
                        evac(os_[:, g * 512:(g + 1) * 512], pt[:], [Bpt], [Bos])
                    tok0 = tt * 512 + s4 * 128
                    P.dma("sp", lambda e, os_=os_, tok0=tok0: e.dma_start(out=out[tok0:tok0 + 128, :], in_=os_[:]), reads=[Bos], is_output=True)

        P.finish()
        with nc.Block() as block:
            P.emit(block)
    return nc


def zero_rows(nc, P, yT, ByT, r0, nrows, T, sb):
    with Phase(P) as st:
        z = sb(f"zr{r0}", [128, T], BF16, st)
        Bz = Buf("z")
        P.op("pool", lambda e: e.memset(z[:], 0.0), writes=[Bz])
        for r in range(r0, r0 + nrows, 128):
            P.dma("sp", lambda e, r=r: e.dma_start(out=yT[r:r + 128, :], in_=z[:]), reads=[Bz], writes=[ByT])
        for ev in list(ByT.r) + ([ByT.w] if ByT.w else []):
            pass


def phase_sc(nc, P, l, T, pfT, yT, sc_conv_w, Bpf, ByT, sb):
    with Phase(P) as st:
        cw = sb("cw_sc", [128, 4, 3], F32, st)
        Bcw = Buf("cw")
        for ci in range(4):
            for k in range(3):
                P.dma("sp", lambda e, ci=ci, k=k: e.dma_start(out=cw[:, ci, k:k + 1], in_=sc_conv_w[l, k, ci * 128:(ci + 1) * 128].rearrange("(p o) -> p o", o=1)), writes=[Bcw])
        bt = sb("b_sc", [128, T], F32, st)
        ct = sb("c_sc", [128, T], F32, st)
        ht = sb("h_sc", [128, T], F32, st)
        acc = sb("acc_sc", [128, T], F32, st)
        yo = sb("yo_sc", [128, T], BF16, st)
        Bb, Bc_, Bh, Ba, By = Buf(), Buf(), Buf(), Buf(), Buf()
        for ci in range(4):
            P.dma("sp", lambda e, ci=ci: e.dma_start(out=bt[:], in_=pfT[ci * 128:(ci + 1) * 128, :]), reads=[Bpf], writes=[Bb])
            P.dma("sp", lambda e, ci=ci: e.dma_start(out=ct[:], in_=pfT[512 + ci * 128:512 + (ci + 1) * 128, :]), reads=[Bpf], writes=[Bc_])
            P.dma("sp", lambda e, ci=ci: e.dma_start(out=ht[:], in_=pfT[1024 + ci * 128:1024 + (ci + 1) * 128, :]), reads=[Bpf], writes=[Bh])
            P.op("dve", lambda e: e.tensor_tensor(ct[:], ct[:], ht[:], ALU.mult), reads=[Bc_, Bh], writes=[Bc_])
            P.op("dve", lambda e, ci=ci: e.tensor_scalar(acc[:], ct[:], cw[:, ci, 2:3], None, ALU.mult), reads=[Bc_, Bcw], writes=[Ba])
            P.op("dve", lambda e, ci=ci: e.scalar_tensor_tensor(acc[:, 1:T], ct[:, 0:T - 1], cw[:, ci, 1:2], acc[:, 1:T], ALU.mult, ALU.add), reads=[Bc_, Bcw, Ba], writes=[Ba])
            P.op("dve", lambda e, ci=ci: e.scalar_tensor_tensor(acc[:, 2:T], ct[:, 0:T - 2], cw[:, ci, 0:1], acc[:, 2:T], ALU.mult, ALU.add), reads=[Bc_, Bcw, Ba], writes=[Ba])
            P.op("dve", lambda e: e.tensor_tensor(yo[:], acc[:], bt[:], ALU.mult), reads=[Ba, Bb], writes=[By])
            P.dma("sp", lambda e, ci=ci: e.dma_start(out=yT[768 + ci * 128:768 + (ci + 1) * 128, :], in_=yo[:]), reads=[By], writes=[ByT])


def phase_sb(nc, P, l, T, qkT, vtok, yT, Bqk, Bv, ByT, ps, psb, sb):
    NQT = T // 512
    NTB = T // 128
    scale = 128 ** -0.5
    with Phase(P) as st:
        Bk = Buf("sbconst")
        tmp = sb("sbtmp", [128, 4, 512], F32, st)
        uinc = sb("uinc", [128, 128], BF16, st)
        remm = sb("remm", [128, 128], BF16, st)
        masks = sb("masks", [128, 4, 512], BF16, st)
        P.op("pool", lambda e: e.memset(tmp[:, 0, 0:128], -1.0), writes=[Bk])
        P.op("pool", lambda e: e.affine_select(out=tmp[:, 0, 0:128], in_=tmp[:, 0, 0:128], pattern=[[-1, 128]], compare_op=ALU.is_ge, fill=0.0,
                                                base=0, channel_multiplier=1), reads=[Bk], writes=[Bk])
        P.op("pool", lambda e: e.tensor_copy(uinc[:], tmp[:, 0, 0:128]), reads=[Bk], writes=[Bk])
        P.op("pool", lambda e: e.memset(tmp[:, 0, 0:128], -1.0), reads=[Bk], writes=[Bk])
        P.op("pool", lambda e: e.affine_select(out=tmp[:, 0, 0:128], in_=tmp[:, 0, 0:128], pattern=[[1, 128]], compare_op=ALU.is_gt, fill=0.0,
                                                base=0, channel_multiplier=-1), reads=[Bk], writes=[Bk])
        P.op("pool", lambda e: e.tensor_copy(remm[:], tmp[:, 0, 0:128]), reads=[Bk], writes=[Bk])
        P.op("pool", lambda e: e.memset(tmp[:], 1.0), reads=[Bk], writes=[Bk])
        P.op("pool", lambda e: e.affine_select(out=tmp[:], in_=tmp[:], pattern=[[-128, 4], [1, 512]], compare_op=ALU.is_gt, fill=0.0,
                                                base=0, channel_multiplier=-1), reads=[Bk], writes=[Bk])
        P.op("pool", lambda e: e.tensor_copy(masks[:], tmp[:]), reads=[Bk], writes=[Bk])

        QT = [sb(f"QT{i}", [128, T], BF16, st) for i in range(2)]
        KTt = [sb(f"KTt{i}", [128, T], BF16, st) for i in range(2)]
        VV = [sb(f"VV{i}", [128, NTB, 128], BF16, st) for i in range(2)]
        BQ = [Buf(), Buf()]
        BK_ = [Buf(), Buf()]
        BV_ = [Buf(), Buf()]
        Et = [sb(f"E{i}", [128, 512], F32, st) for i in range(3)]
        SPt = [sb(f"SP{i}", [128, 512], BF16, st) for i in range(3)]
        Xt = [sb(f"X{i}", [128, 512], F32, st) for i in range(2)]
        Wt = [sb(f"W{i}", [128, 512], BF16, st) for i in range(3)]
        Ot = [sb(f"Osb{i}", [128, 512], BF16, st) for i in range(2)]
        BE = [Buf() for _ in range(3)]
        BSP = [Buf() for _ in range(3)]
        BX = [Buf() for _ in range(2)]
        BW = [Buf() for _ in range(3)]
        BOt = [Buf() for _ in range(2)]

        pairs = []
        g = 0
        for h in range(6):
            for qt in range(NQT):
                kbs = list(range(4 * qt + 3, -1, -1))
                for n, kb in enumerate(kbs):
                    pairs.append((h, qt, kb, n == 0, n == len(kbs) - 1, g))
                g += 1
        loaded = set()

        def load_head(h):
            if h in loaded or h >= 6:
                return
            loaded.add(h)
            b = h % 2
            P.dma("sp", lambda e: e.dma_start(out=QT[b][:], in_=qkT[h * 128:(h + 1) * 128, :]), reads=[Bqk], writes=[BQ[b]])
            P.dma("sp", lambda e: e.dma_start(out=KTt[b][:], in_=qkT[768 + h * 128:768 + (h + 1) * 128, :]), reads=[Bqk], writes=[BK_[b]])
            P.dma("sp", lambda e: e.dma_start(out=VV[b][:], in_=vtok[:, h * 128:(h + 1) * 128].rearrange("(blk p) d -> p blk d", p=128)),
                  reads=[Bv], writes=[BV_[b]])

        def stage1z(i):
            h, qt, kb, first, last, g = pairs[i]
            load_head(h)
            b = h % 2
            z, Bz = ps[i % 2], psb[i % 2]
            q0 = qt * 512
            P.op("pe", lambda e: e.matmul(z[:], KTt[b][:, kb * 128:(kb + 1) * 128], QT[b][:, q0:q0 + 512], start=True, stop=True),
                 reads=[BK_[b], BQ[b]], writes=[Bz])

        def stage1a(i):
            h, qt, kb, first, last, g = pairs[i]
            z, Bz = ps[i % 2], psb[i % 2]
            E, SPb = Et[i % 3], SPt[i % 3]
            P.op("act", lambda e: e.activation(E[:], z[:], AF.Exp, scale=scale), reads=[Bz], writes=[BE[i % 3]])
            P.op("act", lambda e: e.activation(SPb[:], E[:], AF.Ln, bias=1.0), reads=[BE[i % 3]], writes=[BSP[i % 3]])
            r = kb - 4 * qt
            if r >= 0:
                P.op("dve", lambda e: e.tensor_tensor(SPb[:], SPb[:], masks[:, r, :], ALU.mult), reads=[BSP[i % 3], Bk], writes=[BSP[i % 3]])
                P.op("pool", lambda e: e.tensor_tensor(E[:], E[:], masks[:, r, :], ALU.mult), reads=[BE[i % 3], Bk], writes=[BE[i % 3]])

        def stage2a(i):
            h, qt, kb, first, last, g = pairs[i]
            C, BC = ps[2 + g % 2], psb[2 + g % 2]
            SPb = SPt[i % 3]
            P.op("pe", lambda e: e.matmul(C[:], uinc[:], SPb[:], start=first, stop=True, skip_group_check=True), reads=[BSP[i % 3], Bk], writes=[BC])

        def stage2b(i):
            h, qt, kb, first, last, g = pairs[i]
            C, BC = ps[2 + g % 2], psb[2 + g % 2]
            E, SPb = Et[i % 3], SPt[i % 3]
            X, W = Xt[i % 2], Wt[i % 3]
            P.op("act", lambda e: e.activation(X[:], C[:], AF.Exp), reads=[BC], writes=[BX[i % 2]])
            P.op("pe", lambda e: e.matmul(C[:], remm[:], SPb[:], start=False, stop=True, skip_group_check=True), reads=[BSP[i % 3], Bk], writes=[BC])
            P.op("dve", lambda e: e.tensor_tensor(W[:], X[:], E[:], ALU.mult), reads=[BX[i % 2], BE[i % 3]], writes=[BW[i % 3]])

        def stage3(i):
            h, qt, kb, first, last, g = pairs[i]
            b = h % 2
            O, BO = ps[4 + g % 2], psb[4 + g % 2]
            W = Wt[i % 3]
            P.op("pe", lambda e: e.matmul(O[:], VV[b][:, kb, :], W[:], start=first, stop=last), reads=[BW[i % 3], BV_[b]], writes=[BO])
            if last:
                o, Bo = Ot[g % 2], BOt[g % 2]
                P.op("act", lambda e: e.activation(o[:], O[:], AF.Copy), reads=[BO], writes=[Bo])
                q0 = qt * 512
                P.dma("sp", lambda e: e.dma_start(out=yT[h * 128:(h + 1) * 128, q0:q0 + 512], in_=o[:]), reads=[Bo], writes=[ByT])
                if qt == NQT - 1:
                    load_head(h + 2) if (h + 2) % 2 == h % 2 else None

        load_head(0)
        load_head(1)
        n = len(pairs)
        stage1z(0)
        for s_ in range(n + 2):
            if s_ < n:
                stage1a(s_)
            if 0 <= s_ - 1 < n:
                stage2a(s_ - 1)
            if s_ + 1 < n:
                stage1z(s_ + 1)
            if 0 <= s_ - 1 < n:
                stage2b(s_ - 1)
            if 0 <= s_ - 2 < n:
                stage3(s_ - 2)


def phase_gdn(nc, P, l, T, pfT, abtok, yT, gdn_conv_w, gdn_a_log, gdn_dt_bias, gdn_norm_w, Bpf, Bab, ByT, ps, psb, sb, ident, ones_bf, Bc):
    NB = T // 128
    NTT = T // 512
    with Phase(P) as st:
        def op(eng, fn, reads=(), writes=()):
            P.op(eng, fn, reads=list(reads), writes=list(writes))

        def t(name, shape, dt):
            return sb("g_" + name, shape, dt, st), Buf(name)
        Bk = Buf("gconst")
        tri, _ = t("tri", [128, 128], F32)
        trib, _ = t("trib", [128, 128], BF16)
        negL, _ = t("negL", [128, 128], F32)
        blk1, _ = t("blk1", [128, 128], F32)
        half0, _ = t("half0", [128, 128], F32)
        half1, _ = t("half1", [128, 128], F32)
        onesf, _ = t("onesf", [128, 128], F32)
        identb, _ = t("identb", [128, 128], BF16)
        op("pool", lambda e: e.memset(tri[:], 1.0), [], [Bk])
        op("pool", lambda e: e.affine_select(out=tri[:], in_=tri[:], pattern=[[1, 128]], compare_op=ALU.is_ge, fill=0.0, base=0, channel_multiplier=-1), [Bk], [Bk])
        op("pool", lambda e: e.memset(tri[0:64, 64:128], 0.0), [Bk], [Bk])
        op("pool", lambda e: e.tensor_copy(trib[:], tri[:]), [Bk], [Bk])
        op("pool", lambda e: e.memset(negL[:], -1.0), [Bk], [Bk])
        op("pool", lambda e: e.affine_select(out=negL[:], in_=negL[:], pattern=[[-1, 128]], compare_op=ALU.is_gt, fill=0.0, base=0, channel_multiplier=1), [Bk], [Bk])
        op("pool", lambda e: e.memset(negL[64:128, 0:64], 0.0), [Bk], [Bk])
        op("pool", lambda e: e.memset(blk1[:], 0.0), [Bk], [Bk])
        op("pool", lambda e: e.memset(blk1[0:64, 0:64], 1.0), [Bk], [Bk])
        op("pool", lambda e: e.memset(blk1[64:128, 64:128], 1.0), [Bk], [Bk])
        op("pool", lambda e: e.memset(half0[:], 0.0), [Bk], [Bk])
        op("pool", lambda e: e.memset(half0[0:64, :], 1.0), [Bk], [Bk])
        op("pool", lambda e: e.memset(half1[:], 0.0), [Bk], [Bk])
        op("pool", lambda e: e.memset(half1[64:128, :], 1.0), [Bk], [Bk])
        op("pool", lambda e: e.memset(onesf[:], 1.0), [Bk], [Bk])
        op("pool", lambda e: e.tensor_copy(identb[:], ident[:]), [Bk, Bc], [Bk])
        cwg, Bcw = t("cwg", [128, 3, 6, 4], F32)
        for x in range(3):
            for h in range(6):
                for i in range(4):
                    c0 = x * 768 + h * 128
                    P.dma("sp", lambda e, x=x, h=h, i=i, c0=c0: e.dma_start(out=cwg[:, x, h, i:i + 1], in_=gdn_conv_w[l, i, c0:c0 + 128].rearrange("(p o) -> p o", o=1)), writes=[Bcw])
        gnw, Bgnw = t("gnw", [128, 1], F32)
        P.dma("sp", lambda e: e.dma_start(out=gnw[:], in_=gdn_norm_w[l].rearrange("(p o) -> p o", o=1)), writes=[Bgnw])
        alb, Balb = t("alb", [128, 6], F32)
        dtb, Bdtb = t("dtb", [128, 6], F32)
        P.dma("sp", lambda e: e.dma_start(out=alb[:], in_=gdn_a_log[l:l + 1, :].broadcast_to([128, 6])), writes=[Balb])
        P.dma("sp", lambda e: e.dma_start(out=dtb[:], in_=gdn_dt_bias[l:l + 1, :].broadcast_to([128, 6])), writes=[Bdtb])
        nea, Bnea = t("nea", [128, 6], F32)
        op("act", lambda e: e.activation(nea[:], alb[:], AF.Exp), [Balb], [Bnea])
        op("dve", lambda e: e.tensor_scalar(nea[:], nea[:], -1.0, None, ALU.mult), [Bnea], [Bnea])
        ab, Bab_s = t("ab", [128, NB, 12], F32)
        P.dma("sp", lambda e: e.dma_start(out=ab[:], in_=abtok.rearrange("(blk p) c -> p blk c", p=128)), reads=[Bab], writes=[Bab_s])
        g_, Bg = t("g", [128, NB, 6], F32)
        beta, Bbeta = t("beta", [128, NB, 6], F32)
        gcs, Bgcs = t("gcs", [128, NB, 6], F32)
        kbs, Bkbs = t("kbs", [128, NB, 6], F32)
        kds, Bkds = t("kds", [128, NB, 6], F32)
        eb0, Beb0 = t("eb0", [128, NB, 6], F32)
        eb1, Beb1 = t("eb1", [128, NB, 6], F32)
        tA, BtA = t("tA", [128, 6], F32)
        tB, BtB = t("tB", [128, 6], F32)
        p7, B7 = ps[7], psb[7]
        for b in range(NB):
            op("dve", lambda e, b=b: e.tensor_tensor(tA[:], ab[:, b, 0:6], dtb[:], ALU.add), [Bab_s, Bdtb], [BtA])
            op("act", lambda e: e.activation(tA[:], tA[:], AF.Exp), [BtA], [BtA])
            op("act", lambda e: e.activation(tA[:], tA[:], AF.Ln, bias=1.0), [BtA], [BtA])
            op("dve", lambda e, b=b: e.tensor_tensor(g_[:, b, :], tA[:], nea[:], ALU.mult), [BtA, Bnea], [Bg])
            op("act", lambda e, b=b: e.activation(tB[:], ab[:, b, 6:12], AF.Exp, scale=-1.0), [Bab_s], [BtB])
            op("dve", lambda e: e.tensor_scalar(tB[:], tB[:], 1.0, None, ALU.add), [BtB], [BtB])
            op("dve", lambda e, b=b: e.reciprocal(beta[:, b, :], tB[:]), [BtB], [Bbeta])
            P.op("pe", lambda e, b=b: e.matmul(p7[:, 0:6], tri[:], g_[:, b, :], start=True, stop=True), reads=[Bg, Bk], writes=[B7])
            P.op("pe", lambda e, b=b: e.matmul(p7[:, 8:14], blk1[:], g_[:, b, :], start=True, stop=True), reads=[Bg, Bk], writes=[B7])
            P.op("pe", lambda e, b=b: e.matmul(p7[:, 16:22], half0[:], g_[:, b, :], start=True, stop=True), reads=[Bg, Bk], writes=[B7])
            P.op("pe", lambda e, b=b: e.matmul(p7[:, 24:30], half1[:], g_[:, b, :], start=True, stop=True), reads=[Bg, Bk], writes=[B7])
            op("dve", lambda e, b=b: e.tensor_copy(gcs[:, b, :], p7[:, 0:6]), [B7], [Bgcs])
            op("act", lambda e, b=b: e.activation(tA[:], p7[:, 0:6], AF.Exp), [B7], [BtA])
            op("dve", lambda e, b=b: e.tensor_tensor(kbs[:, b, :], tA[:], beta[:, b, :], ALU.mult), [BtA, Bbeta], [Bkbs])
            op("dve", lambda e, b=b: e.tensor_tensor(tB[:], p7[:, 8:14], gcs[:, b, :], ALU.subtract), [B7, Bgcs], [BtB])
            op("act", lambda e, b=b: e.activation(kds[:, b, :], tB[:], AF.Exp), [BtB], [Bkds])
            op("act", lambda e, b=b: e.activation(eb0[:, b, :], p7[:, 16:22], AF.Exp), [B7], [Beb0])
            op("act", lambda e, b=b: e.activation(eb1[:, b, :], p7[:, 24:30], AF.Exp), [B7], [Beb1])
        xin, Bxin = t("xin", [128, T + 3], F32)
        acc, Bacc = t("acc", [128, T], F32)
        sqb, Bsqb = t("sqb", [128, 512], BF16)
        rr, Brr = t("rr", [128, 512], F32)
        zt, Bzt = t("zt", [128, 512], F32)
        yg, Byg = t("yg", [128, 512], F32)
        yo, Byo = t("yo", [128, 512], BF16)
        op("pool", lambda e: e.memset(xin[:, 0:3], 0.0), [], [Bxin])
        qscale = 128 ** -0.5

        class Ctx:
            pass
        ctxs = []
        for ci in range(2):
            C = Ctx()
            for nm, shp, dt in (("vc", [128, T], F32), ("qn", [128, T], BF16), ("kn", [128, T], BF16), ("oT", [128, T], F32),
                                ("S32", [128, 128], F32), ("Sb", [128, 128], BF16), ("gb", [128, 128], F32), ("d1", [128, 128], F32),
                                ("dl", [128, 128], F32), ("du", [128, 128], F32), ("er", [128, 128], F32), ("qd", [128, 128], BF16),
                                ("t1", [128, 128], F32), ("AT", [128, 128], BF16), ("kbg", [128, 128], BF16), ("kdec", [128, 128], BF16),
                                ("vb", [128, 128], BF16), ("ktok", [128, 128], F32), ("vtk", [128, 128], F32), ("PTm", [128, 2, 128], BF16),
                                ("wTm", [128, 2, 128], BF16), ("usb", [128, 128], F32), ("vnew", [128, 128], BF16)):
                tt_, bb_ = t(f"{nm}c{ci}", shp, dt)
                setattr(C, nm, tt_)
                setattr(C, "B" + nm, bb_)
            C.Nn = [t(f"N{i}c{ci}", [128, 128], BF16) for i in range(2)]
            C.NT = [t(f"NT{i}c{ci}", [128, 128], BF16) for i in range(2)]
            C.PT = [t(f"PT{i}c{ci}", [128, 128], BF16) for i in range(2)]
            C.pA, C.pB, C.pC, C.pD = ps[4 * ci:4 * ci + 4]
            C.BA, C.BB, C.BC, C.BD = psb[4 * ci:4 * ci + 4]
            C.pDb = C.pD[:].bitcast(BF16)
            op("pool", lambda e, C=C: e.memset(C.PTm[:], 0.0), [], [C.BPTm])
            op("pool", lambda e, C=C: e.memset(C.wTm[:], 0.0), [], [C.BwTm])
            ctxs.append(C)

        def conv_norm(h, C):
            for x in range(3):
                r0 = 1536 + x * 768 + h * 128
                P.dma("sp", lambda e, r0=r0: e.dma_start(out=xin[:, 3:T + 3], in_=pfT[r0:r0 + 128, :]), reads=[Bpf], writes=[Bxin])
                op("dve", lambda e, x=x: e.tensor_scalar(acc[:], xin[:, 3:T + 3], cwg[:, x, h, 3:4], None, ALU.mult), [Bxin, Bcw], [Bacc])
                for i in range(3):
                    op("dve", lambda e, x=x, i=i: e.scalar_tensor_tensor(acc[:], xin[:, i:i + T], cwg[:, x, h, i:i + 1], acc[:], ALU.mult, ALU.add), [Bxin, Bcw, Bacc], [Bacc])
                if x == 2:
                    op("act", lambda e: e.activation(C.vc[:], acc[:], AF.Silu), [Bacc], [C.Bvc])
                else:
                    op("act", lambda e: e.activation(acc[:], acc[:], AF.Silu), [Bacc], [Bacc])
                    dst, Bdst = (C.qn, C.Bqn) if x == 0 else (C.kn, C.Bkn)
                    sc_ = qscale if x == 0 else 1.0
                    for tt in range(NTT):
                        sl = slice(tt * 512, (tt + 1) * 512)
                        op("act", lambda e, sl=sl: e.activation(sqb[:], acc[:, sl], AF.Square), [Bacc], [Bsqb])
                        P.op("pe", lambda e: e.matmul(p7[:], ones_bf[:], sqb[:], start=True, stop=True), reads=[Bsqb, Bc], writes=[B7])
                        op("act", lambda e: e.activation(rr[:], p7[:], AF.Sqrt, bias=L2_EPS), [B7], [Brr])
                        op("dve", lambda e: e.reciprocal(rr[:], rr[:]), [Brr], [Brr])
                        op("dve", lambda e, sl=sl, dst=dst, sc_=sc_: e.scalar_tensor_tensor(dst[:, sl], acc[:, sl], sc_, rr[:], ALU.mult, ALU.mult), [Bacc, Brr], [Bdst])

        def block_ops(h, b, C, L):
            def rop(eng, fn, reads, writes):
                L.append(lambda: P.op(eng, fn, reads=list(reads), writes=list(writes)))
            pA, pB, pC, pD, pDb = C.pA, C.pB, C.pC, C.pD, C.pDb
            BA, BB, BC, BD = C.BA, C.BB, C.BC, C.BD
            c0 = b * 128
            ksl = C.kn[:, c0:c0 + 128]
            qsl = C.qn[:, c0:c0 + 128]
            rop("pe", lambda e: e.transpose(pDb[:, 0:128], ksl, identb[:]), [C.Bkn, Bk], [BD])
            rop("act", lambda e: e.activation(C.ktok[:], pDb[:, 0:128], AF.Copy), [BD], [C.Bktok])
            rop("dve", lambda e: e.tensor_scalar(C.kbg[:], C.ktok[:], kbs[:, b, h:h + 1], None, ALU.mult), [C.Bktok, Bkbs], [C.Bkbg])
            rop("dve", lambda e: e.tensor_scalar(C.kdec[:], C.ktok[:], kds[:, b, h:h + 1], None, ALU.mult), [C.Bktok, Bkds], [C.Bkdec])
            rop("pe", lambda e: e.transpose(pC[:, 0:128], C.vc[:, c0:c0 + 128], ident[:]), [C.Bvc, Bc], [BC])
            rop("act", lambda e: e.activation(C.vtk[:], pC[:, 0:128], AF.Copy), [BC], [C.Bvtk])
            rop("dve", lambda e: e.tensor_scalar(C.vb[:], C.vtk[:], beta[:, b, h:h + 1], None, ALU.mult), [C.Bvtk, Bbeta], [C.Bvb])
            rop("pe", lambda e: e.matmul(pA[:, 0:128], ksl, ksl, start=True, stop=True), [C.Bkn], [BA])
            rop("pe", lambda e: e.matmul(pB[:, 0:128], ksl, qsl, start=True, stop=True), [C.Bkn, C.Bqn], [BB])
            rop("dve", lambda e: e.tensor_scalar(C.gb[:], onesf[:], g_[:, b, h:h + 1], None, ALU.mult), [Bg, Bk], [C.Bgb])
            rop("pe", lambda e: e.matmul(pC[:, 0:128], C.gb[:], tri[:], start=True, stop=True), [C.Bgb, Bk], [BC])
            rop("dve", lambda e: e.tensor_scalar(C.d1[:], pC[:, 0:128], -1.0, gcs[:, b, h:h + 1], ALU.mult, ALU.add), [BC, Bgcs], [C.Bd1])
            rop("act", lambda e: e.activation(C.er[:], pC[:, 0:128], AF.Exp), [BC], [C.Ber])
            rop("dve", lambda e: e.tensor_scalar(C.dl[:], C.d1[:], 0.0, None, ALU.min), [C.Bd1], [C.Bdl])
            rop("dve", lambda e: e.tensor_scalar(C.du[:], C.d1[:], -1.0, 0.0, ALU.mult, ALU.min), [C.Bd1], [C.Bdu])
            rop("act", lambda e: e.activation(C.dl[:], C.dl[:], AF.Exp), [C.Bdl], [C.Bdl])
            rop("act", lambda e: e.activation(C.du[:], C.du[:], AF.Exp), [C.Bdu], [C.Bdu])
            rop("dve", lambda e: e.tensor_tensor(C.qd[:], qsl, C.er[:], ALU.mult), [C.Bqn, C.Ber], [C.Bqd])
            N, BN = C.Nn[0]
            NTt, BNT = C.NT[0]
            PTt, BPT = C.PT[0]
            rop("dve", lambda e: e.scalar_tensor_tensor(C.t1[:], pA[:, 0:128], beta[:, b, h:h + 1], C.dl[:], ALU.mult, ALU.mult), [BA, Bbeta, C.Bdl], [C.Bt1])
            rop("dve", lambda e, N=N: e.tensor_tensor(N[:], C.t1[:], negL[:], ALU.mult), [C.Bt1, Bk], [BN])
            rop("pe", lambda e, N=N: e.transpose(pDb[:, 0:128], N[:], identb[:]), [BN, Bk], [BD])
            rop("act", lambda e, NTt=NTt: e.activation(NTt[:], pDb[:, 0:128], AF.Copy), [BD], [BNT])
            rop("dve", lambda e, PTt=PTt: e.tensor_tensor(PTt[:], pDb[:, 0:128], identb[:], ALU.add), [BD, Bk], [BPT])
            rop("dve", lambda e: e.tensor_tensor(C.t1[:], pB[:, 0:128], C.du[:], ALU.mult), [BB, C.Bdu], [C.Bt1])
            rop("dve", lambda e: e.tensor_tensor(C.AT[:], C.t1[:], tri[:], ALU.mult), [C.Bt1, Bk], [C.BAT])
            cur = 0
            for step in range(5):
                N, BN = C.Nn[cur]
                NTt, BNT = C.NT[cur]
                PTt, BPT = C.PT[cur]
                N2, BN2 = C.Nn[1 - cur]
                NT2, BNT2 = C.NT[1 - cur]
                PT2, BPT2 = C.PT[1 - cur]
                rop("pe", lambda e, NTt=NTt, N=N: e.matmul(pA[:, 0:128], NTt[:], N[:], start=True, stop=True), [BNT, BN], [BA])
                rop("act", lambda e, N2=N2: e.activation(N2[:], pA[:, 0:128], AF.Copy), [BA], [BN2])
                if step < 4:
                    rop("pe", lambda e, NTt=NTt, N=N: e.matmul(pB[:, 0:128], N[:], NTt[:], start=True, stop=True), [BNT, BN], [BB])
                    rop("act", lambda e, NT2=NT2: e.activation(NT2[:], pB[:, 0:128], AF.Copy), [BB], [BNT2])
                rop("pe", lambda e, N2=N2, PTt=PTt: e.matmul(pC[:, 0:128], N2[:], PTt[:], start=True, stop=True), [BN2, BPT], [BC])
                rop("dve", lambda e, PT2=PT2, PTt=PTt: e.tensor_tensor(PT2[:], pC[:, 0:128], PTt[:], ALU.add), [BC, BPT], [BPT2])
                cur = 1 - cur
            PTt, BPT = C.PT[cur]
            rop("act", lambda e, PTt=PTt: e.activation(C.PTm[:, 0, 0:64], PTt[:, 0:64], AF.Copy), [BPT], [C.BPTm])
            rop("act", lambda e, PTt=PTt: e.activation(C.PTm[:, 1, 64:128], PTt[:, 64:128], AF.Copy), [BPT], [C.BPTm])
            rop("pe", lambda e, PTt=PTt: e.matmul(pA[:, 0:128], C.kbg[:], PTt[:], start=True, stop=True), [C.Bkbg, BPT], [BA])
            rop("dve", lambda e: e.tensor_copy(C.wTm[:, 0, 0:64], pA[:, 0:64]), [BA], [C.BwTm])
            rop("dve", lambda e: e.tensor_copy(C.wTm[:, 1, 64:128], pA[:, 64:128]), [BA], [C.BwTm])
            for c in range(2):
                cs = slice(c * 64, (c + 1) * 64)
                rop("pe", lambda e, c=c: e.matmul(pC[:, 0:128], C.PTm[:, c, :], C.vb[:], start=True, stop=True), [C.BPTm, C.Bvb], [BC])
                rop("act", lambda e: e.activation(C.usb[:], pC[:, 0:128], AF.Copy), [BC], [C.Busb])
                rop("pe", lambda e, c=c: e.matmul(pA[:, 0:128], C.wTm[:, c, :], C.Sb[:], start=True, stop=True), [C.BwTm, C.BSb], [BA])
                rop("dve", lambda e: e.tensor_tensor(C.vnew[:], C.usb[:], pA[:, 0:128], ALU.subtract), [C.Busb, BA], [C.Bvnew])
                rop("pe", lambda e, cs=cs: e.matmul(pD[:, cs], C.Sb[:], C.qd[:, cs], start=True, stop=False), [C.BSb, C.Bqd], [BD])
                rop("pe", lambda e, cs=cs: e.matmul(pD[:, cs], C.vnew[:], C.AT[:, cs], start=False, stop=True), [C.Bvnew, C.BAT], [BD])
                rop("pe", lambda e: e.matmul(pB[:, 0:128], C.kdec[:], C.vnew[:], start=True, stop=True), [C.Bkdec, C.Bvnew], [BB])
                ebc = eb0 if c == 0 else eb1
                Bebc = Beb0 if c == 0 else Beb1
                rop("dve", lambda e, ebc=ebc: e.scalar_tensor_tensor(C.S32[:], C.S32[:], ebc[:, b, h:h + 1], pB[:, 0:128], ALU.mult, ALU.add), [C.BS32, Bebc, BB], [C.BS32])
                rop("act", lambda e: e.activation(C.Sb[:], C.S32[:], AF.Copy), [C.BS32], [C.BSb])
            rop("act", lambda e: e.activation(C.oT[:, c0:c0 + 128], pD[:, 0:128], AF.Copy), [BD], [C.BoT])

        def onorm(h, C):
            for tt in range(NTT):
                sl = slice(tt * 512, (tt + 1) * 512)
                op("act", lambda e, sl=sl: e.activation(sqb[:], C.oT[:, sl], AF.Square), [C.BoT], [Bsqb])
                P.op("pe", lambda e: e.matmul(p7[:], ones_bf[:], sqb[:], start=True, stop=True), reads=[Bsqb, Bc], writes=[B7])
                op("act", lambda e: e.activation(rr[:], p7[:], AF.Sqrt, bias=RMS_EPS, scale=1.0 / 128), [B7], [Brr])
                op("dve", lambda e: e.reciprocal(rr[:], rr[:]), [Brr], [Brr])
                op("dve", lambda e, sl=sl: e.scalar_tensor_tensor(yg[:], C.oT[:, sl], gnw[:, 0:1], rr[:], ALU.mult, ALU.mult), [C.BoT, Brr, Bgnw], [Byg])
                r0 = 1536 + 2304 + h * 128
                P.dma("sp", lambda e, r0=r0, sl=sl: e.dma_start(out=zt[:], in_=pfT[r0:r0 + 128, sl]), reads=[Bpf], writes=[Bzt])
                op("act", lambda e: e.activation(zt[:], zt[:], AF.Silu), [Bzt], [Bzt])
                op("dve", lambda e: e.tensor_tensor(yo[:], yg[:], zt[:], ALU.mult), [Byg, Bzt], [Byo])
                P.dma("sp", lambda e, sl=sl: e.dma_start(out=yT[1280 + h * 128:1280 + (h + 1) * 128, sl], in_=yo[:]), reads=[Byo], writes=[ByT])

        for hp in range(3):
            heads = (2 * hp, 2 * hp + 1)
            lists = []
            for ci, h in enumerate(heads):
                C = ctxs[ci]
                conv_norm(h, C)
                L = []
                L.append(lambda C=C: P.op("pool", lambda e: e.memset(C.S32[:], 0.0), reads=[C.BS32], writes=[C.BS32]))
                L.append(lambda C=C: P.op("pool", lambda e: e.memset(C.Sb[:], 0.0), reads=[C.BSb], writes=[C.BSb]))
                for b in range(NB):
                    block_ops(h, b, C, L)
                lists.append(L)
            for i in range(max(len(L) for L in lists)):
                for L in lists:
                    if i < len(L):
                        L[i]()
            for ci, h in enumerate(heads):
                onorm(h, ctxs[ci])


_CACHE = {}


def kernel(**inputs):
    B, T, _ = inputs["x"].shape
    key = (T,)
    if key not in _CACHE:
        _CACHE[key] = build(T=T, depth=2)
    nc = _CACHE[key]
    work = [0, 1, 4, 5][:B]
    params = {k: np.ascontiguousarray(np.asarray(v, dtype=np.float32)) for k, v in inputs.items() if k != "x"}
    zeros = {k: np.zeros_like(v) for k, v in params.items()}
    zeros["x"] = np.zeros((T, inputs["x"].shape[2]), np.float32)
    in_maps = []
    for c in range(8):
        if c in work:
            m = dict(params)
            m["x"] = np.ascontiguousarray(np.asarray(inputs["x"][work.index(c)], dtype=np.float32))
        else:
            m = zeros
        in_maps.append(m)
    res = run_bass_kernel_spmd(nc, in_maps, core_ids=list(range(8)))
    return np.stack([np.asarray(res.results[c]["out"], dtype=np.float32) for c in work], axis=0)
```
